# Optimizing a Trainium2 kernel written in Bass

```python
import math
import jax, jax.numpy as jnp
from jax import lax
import numpy as np

D_MODEL = 1024
BATCH = 4
SEQ = 4096
DEPTH = 1
DEC_BATCH = 128
DEC_SEQ = 1
PAST_LEN = 2048
PAGE_SIZE = 128

D_MIX = D_MODEL
H_A = 4
DH_A = 64
D_A = H_A * 2 * DH_A
ROT_DIM = DH_A // 4
ROPE_THETA = 500000.0
H_B = 8
DH_B = 64
D_B = H_B * DH_B
LORA_W = 64
LORA_A = 64
LORA_G = 128
SHIFT_DIM = 3 * D_B + LORA_W + LORA_A + LORA_G
D_IN = 3 * D_A + SHIFT_DIM
D_FF = 2816
Q_BLOCK = 128
NORM_EPS = 1e-6
GN_EPS = 64e-5
B_SPLITS = [D_B, 2 * D_B, 3 * D_B, 3 * D_B + LORA_W, 3 * D_B + LORA_W + LORA_A]

kernel_name = 'hymba_diffattn_rwkv7_macaron_step'


def rmsnorm(x, g):
    xf = x.astype(jnp.float32)
    y = xf * lax.rsqrt(jnp.mean(xf * xf, axis=-1, keepdims=True) + NORM_EPS)
    return (y * g.astype(jnp.float32)).astype(x.dtype)


def swiglu(x, w_gate, w_up, w_down):
    return (jax.nn.silu(x @ w_gate) * (x @ w_up)) @ w_down


def partial_rope(x, pos):
    half = ROT_DIM // 2
    inv_freq = ROPE_THETA ** (-jnp.arange(half, dtype=jnp.float32) / half)
    ang = pos.astype(jnp.float32)[:, None] * inv_freq[None, :]
    cos = jnp.cos(ang)[:, None, None, :].astype(x.dtype)
    sin = jnp.sin(ang)[:, None, None, :].astype(x.dtype)
    x1, x2, rest = x[..., :half], x[..., half:ROT_DIM], x[..., ROT_DIM:]
    return jnp.concatenate([x1 * cos - x2 * sin, x2 * cos + x1 * sin, rest], axis=-1)


def diff_softmax_attend(q, k, v, q_pos, k_pos, lam):
    s = jnp.einsum('nqhjd,nkhjd->nhjqk', q, k, preferred_element_type=jnp.float32) * (DH_A ** -0.5)
    mask = k_pos[None, :] <= q_pos[:, None]
    p = jax.nn.softmax(jnp.where(mask, s, -jnp.inf), axis=-1)
    p = p[:, :, 0] - lam * p[:, :, 1]
    return jnp.einsum('nhqk,nkhe->nqhe', p.astype(v.dtype), v)


def attend(q, k, v, q_pos, k_pos, lam):
    n, t = q.shape[0], q.shape[1]
    if t > Q_BLOCK and t % Q_BLOCK == 0:
        nb = t // Q_BLOCK
        qb = jnp.moveaxis(q.reshape(n, nb, Q_BLOCK, H_A, 2, DH_A), 1, 0)
        pb = q_pos.reshape(nb, Q_BLOCK)
        out = lax.map(lambda blk: diff_softmax_attend(blk[0], k, v, blk[1], k_pos, lam), (qb, pb))
        return jnp.moveaxis(out, 0, 1).reshape(n, t, H_A, 2 * DH_A)
    return diff_softmax_attend(q, k, v, q_pos, k_pos, lam)


def wkv_scan(s0, r, decay, k, v, kk, a):
    def step(s, inp):
        r_t, w_t, k_t, v_t, kk_t, a_t = inp
        sa = jnp.einsum('nhvk,nhk->nhv', s, -kk_t)
        s = (s * w_t[:, :, None, :] + sa[..., None] * (kk_t * a_t)[:, :, None, :]
             + v_t[..., None] * k_t[:, :, None, :])
        return s, jnp.einsum('nhvk,nhk->nhv', s, r_t)
    xs = tuple(jnp.moveaxis(z, 1, 0) for z in (r, decay, k, v, kk, a))
    s, ys = lax.scan(step, s0, xs)
    return s, jnp.moveaxis(ys, 0, 1)


def rwkv7_time_mix(pb, prev, s0, lp):
    n, t, _ = pb.shape
    f32 = jnp.float32
    shifted = jnp.concatenate([prev[:, None, :].astype(pb.dtype), pb[:, :-1]], axis=1)
    xs = pb + (shifted - pb) * lp['mu_shift']
    r, k, v, xw, xa, xg = jnp.split(xs, B_SPLITS, axis=-1)
    w = -jax.nn.softplus(-(lp['w0'] + jnp.tanh(xw) @ lp['w2']).astype(f32)) - 0.5
    decay = jnp.exp(-jnp.exp(w))
    a = jax.nn.sigmoid((lp['a0'] + xa @ lp['a2']).astype(f32))
    g = jax.nn.sigmoid(xg) @ lp['g2']
    heads = lambda z: z.reshape(n, t, H_B, DH_B).astype(f32)
    r, k, v, decay, a = heads(r), heads(k), heads(v), heads(decay), heads(a)
    kk = k * lp['k_k'].astype(f32).reshape(H_B, DH_B)
    kk = kk / jnp.maximum(jnp.sqrt(jnp.sum(kk * kk, axis=-1, keepdims=True)), 1e-12)
    k = k * (1.0 + (a - 1.0) * lp['k_a'].astype(f32).reshape(H_B, DH_B))
    s, y = wkv_scan(s0.astype(f32), r, decay, k, v, kk, a)
    mean = jnp.mean(y, axis=-1, keepdims=True)
    var = jnp.mean(jnp.square(y - mean), axis=-1, keepdims=True)
    y = ((y - mean) * lax.rsqrt(var + GN_EPS) * lp['ln_x_w'].astype(f32).reshape(H_B, DH_B)
         + lp['ln_x_b'].astype(f32).reshape(H_B, DH_B))
    y = y + jnp.sum(r * k * lp['r_k'].astype(f32), axis=-1, keepdims=True) * v
    y = y.reshape(n, t, D_B).astype(pb.dtype) * g
    return y, s.astype(s0.dtype), pb[:, -1]


def decoder_layer(x, pos, past_k, past_v, past_pos, prev_shift, s0, lp, lam_init):
    n, t, _ = x.shape
    f32 = jnp.float32
    h = x + 0.5 * rmsnorm(swiglu(rmsnorm(x, lp['n_ffn1_pre']), lp['ffn1_gate'], lp['ffn1_up'],
                                 lp['ffn1_down']), lp['n_ffn1_post'])
    u = rmsnorm(h, lp['n_mix_pre'])
    proj = u @ lp['w_in']
    q, k, v = jnp.split(proj[..., :3 * D_A], 3, axis=-1)
    pb = proj[..., 3 * D_A:]
    q = partial_rope(q.reshape(n, t, H_A, 2, DH_A), pos)
    k = partial_rope(k.reshape(n, t, H_A, 2, DH_A), pos)
    v = v.reshape(n, t, H_A, 2 * DH_A)
    if past_k is None:
        k_all, v_all, k_pos = k, v, pos
    else:
        k_all = jnp.concatenate([past_k.astype(k.dtype), k], axis=1)
        v_all = jnp.concatenate([past_v.astype(v.dtype), v], axis=1)
        k_pos = jnp.concatenate([past_pos, pos])
    lam = (jnp.exp(jnp.sum(lp['lambda_q1'].astype(f32) * lp['lambda_k1'].astype(f32)))
           - jnp.exp(jnp.sum(lp['lambda_q2'].astype(f32) * lp['lambda_k2'].astype(f32))) + lam_init)
    ya = attend(q, k_all, v_all, pos, k_pos, lam)
    ya = (rmsnorm(ya, lp['subln']) * (1.0 - lam_init)).reshape(n, t, D_A)
    yb, s_new, shift_new = rwkv7_time_mix(pb, prev_shift, s0, lp)
    mix = jnp.concatenate([ya, yb], axis=-1) @ lp['w_out']
    h = h + rmsnorm(mix, lp['n_mix_post'])
    h = h + 0.5 * rmsnorm(swiglu(rmsnorm(h, lp['n_ffn2_pre']), lp['ffn2_gate'], lp['ffn2_up'],
                                 lp['ffn2_down']), lp['n_ffn2_post'])
    return h, k.reshape(n, t, H_A, 2 * DH_A), v, s_new, shift_new


def setup_inputs(seed: int = 0) -> dict:
    key = jax.random.key(seed)
    ks = iter(jax.random.split(key, 64))
    f32 = jnp.float32
    nrm = lambda shape, scale: scale * jax.random.normal(next(ks), shape, f32)
    gain = lambda shape: 1.0 + nrm(shape, 0.05)
    n_pages = PAST_LEN // PAGE_SIZE
    n_used = DEC_BATCH * n_pages
    n_pool = n_used + (n_used + 3) // 4
    x_prompt = nrm((BATCH, SEQ, D_MODEL), 1.0)
    x_sample = nrm((DEC_BATCH, DEC_SEQ, D_MODEL), 1.0)
    cache_k = nrm((DEPTH, n_pool, PAGE_SIZE, H_A, 2 * DH_A), 1.0)
    cache_v = nrm((DEPTH, n_pool, PAGE_SIZE, H_A, 2 * DH_A), 1.0)
    state_wkv = nrm((DEPTH, DEC_BATCH, H_B, DH_B, DH_B), 0.3)
    state_shift = nrm((DEPTH, DEC_BATCH, SHIFT_DIM), 1.0)
    page_table = jax.random.permutation(next(ks), n_pool)[:n_used].reshape(DEC_BATCH, n_pages).astype(jnp.int32)
    return {
        'x_prompt': x_prompt, 'x_sample': x_sample,
        'cache_k': cache_k, 'cache_v': cache_v,
        'state_wkv': state_wkv, 'state_shift': state_shift,
        'page_table': page_table,
        'n_ffn1_pre': gain((DEPTH, D_MODEL)), 'n_ffn1_post': gain((DEPTH, D_MODEL)),
        'ffn1_gate': nrm((DEPTH, D_MODEL, D_FF), D_MODEL ** -0.5),
        'ffn1_up': nrm((DEPTH, D_MODEL, D_FF), D_MODEL ** -0.5),
        'ffn1_down': nrm((DEPTH, D_FF, D_MODEL), D_FF ** -0.5),
        'n_mix_pre': gain((DEPTH, D_MODEL)), 'n_mix_post': gain((DEPTH, D_MODEL)),
        'w_in': nrm((DEPTH, D_MODEL, D_IN), D_MODEL ** -0.5),
        'w_out': nrm((DEPTH, D_MIX, D_MODEL), D_MIX ** -0.5),
        'lambda_q1': nrm((DEPTH, DH_A), 0.1), 'lambda_k1': nrm((DEPTH, DH_A), 0.1),
        'lambda_q2': nrm((DEPTH, DH_A), 0.1), 'lambda_k2': nrm((DEPTH, DH_A), 0.1),
        'subln': gain((DEPTH, 2 * DH_A)),
        'mu_shift': jax.random.uniform(next(ks), (DEPTH, SHIFT_DIM), f32),
        'w0': nrm((DEPTH, D_B), 0.5),
        'w2': nrm((DEPTH, LORA_W, D_B), 0.1),
        'a0': nrm((DEPTH, D_B), 0.5),
        'a2': nrm((DEPTH, LORA_A, D_B), 0.5 * LORA_A ** -0.5),
        'g2': nrm((DEPTH, LORA_G, D_B), LORA_G ** -0.5),
        'k_k': 0.85 + nrm((DEPTH, D_B), 0.05),
        'k_a': 1.0 + nrm((DEPTH, D_B), 0.05),
        'r_k': nrm((DEPTH, H_B, DH_B), 0.1),
        'ln_x_w': gain((DEPTH, D_B)), 'ln_x_b': nrm((DEPTH, D_B), 0.02),
        'n_ffn2_pre': gain((DEPTH, D_MODEL)), 'n_ffn2_post': gain((DEPTH, D_MODEL)),
        'ffn2_gate': nrm((DEPTH, D_MODEL, D_FF), D_MODEL ** -0.5),
        'ffn2_up': nrm((DEPTH, D_MODEL, D_FF), D_MODEL ** -0.5),
        'ffn2_down': nrm((DEPTH, D_FF, D_MODEL), D_FF ** -0.5),
    }


def reference(x_prompt, x_sample, cache_k, cache_v, state_wkv, state_shift, page_table,
              n_ffn1_pre, n_ffn1_post, ffn1_gate, ffn1_up, ffn1_down,
              n_mix_pre, n_mix_post, w_in, w_out,
              lambda_q1, lambda_k1, lambda_q2, lambda_k2, subln,
              mu_shift, w0, w2, a0, a2, g2, k_k, k_a, r_k, ln_x_w, ln_x_b,
              n_ffn2_pre, n_ffn2_post, ffn2_gate, ffn2_up, ffn2_down):
    n_p, t_p = x_prompt.shape[0], x_prompt.shape[1]
    n_s, t_s = x_sample.shape[0], x_sample.shape[1]
    n_pages = page_table.shape[1]
    past_len = n_pages * PAGE_SIZE
    pos_p = jnp.arange(t_p, dtype=jnp.int32)
    pos_s = past_len + jnp.arange(t_s, dtype=jnp.int32)
    pos_past = jnp.arange(past_len, dtype=jnp.int32)
    yp, ys = x_prompt, x_sample
    kp_l, vp_l, sp_l, hp_l, ks_l, vs_l, ss_l, hs_l = [], [], [], [], [], [], [], []
    for l in range(DEPTH):
        lp = {
            'n_ffn1_pre': n_ffn1_pre[l], 'n_ffn1_post': n_ffn1_post[l],
            'ffn1_gate': ffn1_gate[l], 'ffn1_up': ffn1_up[l], 'ffn1_down': ffn1_down[l],
            'n_mix_pre': n_mix_pre[l], 'n_mix_post': n_mix_post[l],
            'w_in': w_in[l], 'w_out': w_out[l],
            'lambda_q1': lambda_q1[l], 'lambda_k1': lambda_k1[l],
            'lambda_q2': lambda_q2[l], 'lambda_k2': lambda_k2[l], 'subln': subln[l],
            'mu_shift': mu_shift[l], 'w0': w0[l], 'w2': w2[l], 'a0': a0[l], 'a2': a2[l],
            'g2': g2[l], 'k_k': k_k[l], 'k_a': k_a[l], 'r_k': r_k[l],
            'ln_x_w': ln_x_w[l], 'ln_x_b': ln_x_b[l],
            'n_ffn2_pre': n_ffn2_pre[l], 'n_ffn2_post': n_ffn2_post[l],
            'ffn2_gate': ffn2_gate[l], 'ffn2_up': ffn2_up[l], 'ffn2_down': ffn2_down[l],
        }
        lam_init = 0.8 - 0.6 * math.exp(-0.3 * l)
        prev0 = jnp.zeros((n_p, SHIFT_DIM), x_prompt.dtype)
        s0 = jnp.zeros((n_p, H_B, DH_B, DH_B), state_wkv.dtype)
        yp, kp, vp, sp, hp = decoder_layer(yp, pos_p, None, None, None, prev0, s0, lp, lam_init)
        past_k = cache_k[l][page_table].reshape(n_s, past_len, H_A, 2, DH_A)
        past_v = cache_v[l][page_table].reshape(n_s, past_len, H_A, 2 * DH_A)
        ys, kq, vq, sq, hq = decoder_layer(ys, pos_s, past_k, past_v, pos_past,
                                           state_shift[l], state_wkv[l], lp, lam_init)
        kp_l.append(kp); vp_l.append(vp); sp_l.append(sp); hp_l.append(hp)
        ks_l.append(kq); vs_l.append(vq); ss_l.append(sq); hs_l.append(hq)
    return (yp, ys,
            jnp.stack(kp_l), jnp.stack(vp_l), jnp.stack(sp_l), jnp.stack(hp_l),
            jnp.stack(ks_l), jnp.stack(vs_l), jnp.stack(ss_l), jnp.stack(hs_l))
```

```python
import numpy as np
import concourse.bass as bass
import concourse.mybir as mybir
from concourse.bass_utils import run_bass_kernel_spmd

F32 = mybir.dt.float32
BF16 = mybir.dt.bfloat16
I32 = mybir.dt.int32
AF = mybir.ActivationFunctionType
ALU = mybir.AluOpType
AX = mybir.AxisListType

D = 1024
DFF = 2816
NFF = DFF // 128
H_A, DH_A = 4, 64
D_A = 512
H_B, DH_B = 8, 64
D_B = 512
SHIFT = 1792
D_IN = 3 * D_A + SHIFT
PAGE = 128
NORM_EPS = 1e-6
GN_EPS = 64e-5
LAM_INIT = 0.8 - 0.6 * 1.0
NS = 16
CH = 64


class Buf:
    __slots__ = ("name", "w", "r")

    def __init__(self, name):
        self.name = name
        self.w = None
        self.r = {}


class Eng:
    def __init__(self, name, eng, sem, is_pe=False, is_dma=False):
        self.name, self.eng, self.sem = name, eng, sem
        self.cnt = 0
        self.known = {}
        self.is_pe = is_pe
        self.is_dma = is_dma


class Ctx:
    def __init__(self, nc, sems, dma_sems):
        self.nc = nc
        self.semobj = {}
        self.E = {}
        for nm, eng, pe in (("pe", None, True), ("act", None, False), ("dve", None, False), ("pool", None, False), ("sp", None, False)):
            self.semobj[nm] = sems[nm]
            self.E[nm] = Eng(nm, None, sems[nm], is_pe=pe)
        self.dma_pool = dma_sems
        self.dma_i = {q: 0 for q in dma_sems}
        for q, lst in dma_sems.items():
            for i, s in enumerate(lst):
                self.semobj[(q, i)] = s

    def bind(self, name, eng):
        self.E[name].eng = eng

    def _waits(self, E, R, W):
        need = {}
        for b in R:
            if b.w is not None:
                k, v = b.w
                need[k] = max(need.get(k, 0), v)
        for b in W:
            if b.w is not None:
                k, v = b.w
                need[k] = max(need.get(k, 0), v)
            for k, v in b.r.items():
                need[k] = max(need.get(k, 0), v)
        for k, v in need.items():
            if E.known.get(k, 0) >= v:
                continue
            if E.is_pe and k == "pe":
                continue
            E.eng.wait_ge(self.semobj[k], v)
            E.known[k] = v

    def op(self, en, fn, R=(), W=(), inc=True):
        E = self.E[en]
        self._waits(E, R, W)
        ins = fn(E.eng)
        if inc:
            ins.then_inc(E.sem, 1)
            E.cnt += 1
            ev = (en, E.cnt)
        else:
            ev = (en, E.cnt + 1)
        for b in R:
            b.r[ev[0]] = max(b.r.get(ev[0], 0), ev[1])
        for b in W:
            b.w = ev
            b.r = {}
        return ins

    def dma(self, q, fn, R=(), W=()):
        E = self.E[q]
        pool = self.dma_pool[q]
        i = self.dma_i[q]
        self.dma_i[q] += 1
        slot = i % len(pool)
        val = 16 * (i // len(pool) + 1)
        key = (q, slot)
        if val > 16 and E.known.get(key, 0) < val - 16:
            E.eng.wait_ge(pool[slot], val - 16)
            E.known[key] = val - 16
        self._waits(E, R, W)
        ins = fn(E.eng)
        ins.then_inc(pool[slot], 16)
        ev = (key, val)
        for b in R:
            b.r[key] = max(b.r.get(key, 0), val)
        for b in W:
            b.w = ev
            b.r = {}
        return ev

    def wait_all_dma(self, q_wait="sp"):
        E = self.E[q_wait]
        for q, pool in self.dma_pool.items():
            n = self.dma_i[q]
            for slot in range(len(pool)):
                cnt = (n - slot + len(pool) - 1) // len(pool) if n > slot else 0
                if cnt > 0 and E.known.get((q, slot), 0) < 16 * cnt:
                    E.eng.wait_ge(pool[slot], 16 * cnt)

    def barrier(self):
        names = ["pe", "act", "dve", "pool"]
        for a in names:
            Ea = self.E[a]
            for b in names:
                if a == b:
                    continue
                v = self.E[b].cnt
                if v > 0 and Ea.known.get(b, 0) < v:
                    Ea.eng.wait_ge(self.semobj[b], v)
                    Ea.known[b] = v


def build_program(SEQ, NPOOL, NPAGES, debug=False, stage=9):
    TT = 128
    TF = min(512, SEQ)
    NSUB = TF // TT
    NTF = SEQ // TF
    NT = SEQ // TT
    NBLK = TT // 128
    nc = bass.Bass("TRN2", target_bir_lowering=False)
    qkv_s = nc.dram_tensor("qkv_s", [TF, 1536], F32).ap()
    pb_s = nc.dram_tensor("pb_s", [128, 14, TF], F32).ap()
    KT_s = nc.dram_tensor("KT_s", [4, 128, SEQ], BF16).ap()
    V_s = nc.dram_tensor("V_s", [4, 128, SEQ // 128, 128], BF16).ap()
    dt_in = lambda name, shape, dt=F32: nc.dram_tensor(name, list(shape), dt, kind="ExternalInput").ap()
    dt_out = lambda name, shape, dt=F32: nc.dram_tensor(name, list(shape), dt, kind="ExternalOutput").ap()

    I = {}
    I["xp"] = dt_in("xp", [SEQ, D])
    for nm in ("ffn1_gate", "ffn1_up", "ffn2_gate", "ffn2_up"):
        I[nm] = dt_in(nm, [D, DFF])
    for nm in ("ffn1_down", "ffn2_down"):
        I[nm] = dt_in(nm, [DFF, D])
    I["w_in"] = dt_in("w_in", [D, D_IN])
    I["w_out"] = dt_in("w_out", [D, D])
    for nm in ("n_ffn1_pre", "n_ffn1_post", "n_mix_pre", "n_mix_post", "n_ffn2_pre", "n_ffn2_post"):
        I[nm] = dt_in(nm, [128, 8])
    I["ident"] = dt_in("ident", [128, 128])
    I["ones"] = dt_in("ones", [128, 128])

    I["cosT"] = dt_in("cosT", [SEQ, 128])
    I["sinT"] = dt_in("sinT", [SEQ, 128])
    I["mu"] = dt_in("mu", [128, 14])
    I["cmask"] = dt_in("cmask", [128, TT // 128, TT])
    I["lamv"] = dt_in("lamv", [1, 4 * 64])
    I["subln"] = dt_in("subln", [128, 1])
    for nm in ("w0", "a0", "k_k", "k_a", "r_k", "ln_x_w", "ln_x_b"):
        I[nm] = dt_in(nm, [128, 4])
    I["wa2"] = dt_in("wa2", [128, 512])
    I["g2"] = dt_in("g2", [128, 512])
    for nm in ("resetm", "maskA4", "maskC8", "I8", "blockones"):
        I[nm] = dt_in(nm, {"resetm": [128, TT], "maskA4": [128, 512], "maskC8": [128, 512], "I8": [128, 512], "blockones": [128, 128]}[nm])

    NPG = NPAGES
    I["xs_pad"] = dt_in("xs_pad", [128, D])
    I["sshT"] = dt_in("sshT", [128, 14, 128])
    I["pt_own"] = dt_in("pt_own", [1, NS * NPG], I32)
    I["cache_k"] = dt_in("cache_k", [NPOOL * 128, 512])
    I["cache_v"] = dt_in("cache_v", [NPOOL * 128, 512])
    I["swkv"] = dt_in("swkv", [NS, 8, 64, 64])
    I["cosS"] = dt_in("cosS", [128, 128])
    I["sinS"] = dt_in("sinS", [128, 128])
    I["diag64"] = dt_in("diag64", [128, 64])

    O = {}
    O["ys_pad"] = dt_out("ys_pad", [128, D])
    O["ks_pad"] = dt_out("ks_pad", [128, 512])
    O["vs_pad"] = dt_out("vs_pad", [128, 512])
    O["shs"] = dt_out("shs", [128, 14, 128])
    O["wkvs"] = dt_out("wkvs", [NS, 8, 64, 64])
    O["yp"] = dt_out("yp", [SEQ, D])
    O["kp"] = dt_out("kp", [SEQ, 512])
    O["vp"] = dt_out("vp", [SEQ, 512])
    O["shp"] = dt_out("shp", [128, 14])
    O["wkvp"] = dt_out("wkvp", [8, 64, 64])

    from contextlib import ExitStack
    es = ExitStack()
    with es:
        sem = {nm: es.enter_context(nc.semaphore("s_" + nm)) for nm in ("pe", "act", "dve", "pool", "sp")}
        dsem = {q: [es.enter_context(nc.semaphore(f"d_{q}{i}")) for i in range(n)] for q, n in (("sp", 12), ("pool", 12))}
        sb = lambda name, shape, dt=F32: es.enter_context(nc.sbuf_tensor(name, list(shape), dt))
        ps = lambda name, shape, dt=F32: es.enter_context(nc.psum_tensor(name, list(shape), dt))

        ident = sb("ident_t", [128, 128]); ident_b = Buf("ident")
        ones_bf = sb("ones_bf", [128, 128], BF16); ones_b = Buf("ones")
        gains = {nm: (sb("g_" + nm, [128, 8]), Buf(nm)) for nm in ("n_ffn1_pre", "n_ffn1_post", "n_mix_pre", "n_mix_post", "n_ffn2_pre", "n_ffn2_post")}
        eps_t = sb("eps_t", [128, 1]); eps_b = Buf("eps")
        xT = sb("xT", [128, 8, TF]); xT_b = Buf("xT")
        xtok = sb("xtok", [128, NBLK, D]); xtok_b = Buf("xtok")
        z = sb("z", [128, 8, TF]); z_b = Buf("z")
        u = sb("u", [128, 8, TF], BF16); u_b = Buf("u")
        hid = sb("hid", [128, NFF, TF], BF16); hid_b = [Buf(f"hid{f}") for f in range(NFF)]
        sq = hid[:, 0:8, :]; sq_b = Buf("sq")
        rstd = sb("rstd", [128, TF]); rstd_b = Buf("rstd")
        sg = [sb(f"sg{i}", [128, TF], BF16) for i in range(2)]; sg_b = [Buf(f"sg{i}") for i in range(2)]
        NW = 4
        wring = [sb(f"wr{i}", [128, 8, 256], BF16) for i in range(NW)]; wring_b = [Buf(f"wr{i}") for i in range(NW)]
        ND = 2
        dring = [sb(f"dr{i}", [128, NFF, 128], BF16) for i in range(ND)]; dring_b = [Buf(f"dr{i}") for i in range(ND)]
        pbank = [ps(f"pb{i}", [128, 512]) for i in range(8)]; pbank_b = [Buf(f"pb{i}") for i in range(8)]

        NKB = SEQ // 128
        qk_tm = xtok; qk_b = xtok_b
        v_tm = sb("v_tm", [128, NBLK, 512]); v_b = Buf("v_tm")
        cos_t = sb("cos_t", [128, NBLK, 128]); sin_t = sb("sin_t", [128, NBLK, 128]); cs_b = Buf("cs")
        rt = [sb(f"rt{i}", [128, 16, 8]) for i in range(4)]; rt_b = [Buf(f"rt{i}") for i in range(4)]
        KC = 1024
        kring = [sb(f"kring{i}", [128, KC], BF16) for i in range(2)]; kring_b = [Buf(f"kring{i}") for i in range(2)]
        vring = [sb(f"vring{i}", [128, KC // 128, 128], BF16) for i in range(2)]; vring_b = [Buf(f"vring{i}") for i in range(2)]
        ktn = sb("ktn", [128, 4, 128], BF16); ktn_b = Buf("ktn")
        vbn = sb("vbn", [128, 4, 128], BF16); vbn_b = Buf("vbn")
        stg = [sb(f"stg{i}", [128, 256]) for i in range(3)]; stg_b = [Buf(f"stg{i}") for i in range(3)]
        stg2 = [sb("stg20", [128, TF])] * 2; stg2_b = [Buf("stg20")] * 2
        qkv_sb = Buf("qkv_s"); pb_sb = Buf("pb_s"); KTs_b = Buf("KT_s"); Vs_b = Buf("V_s")
        QT = sb("QT", [128, 4, TT], BF16); QT_b = Buf("QT")
        pbT = sb("pbT", [128, 14, TT + 1]); pbT_b = Buf("pbT")
        xs = sb("xs", [128, 14, TT]); xs_b = Buf("xs")
        mu_t = sb("mu_t", [128, 14]); mu_b = Buf("mu")
        cmask = sb("cmask_t", [128, NBLK, TT], BF16); cmask_b = Buf("cmask")
        PT = [sb(f"PT{i}", [128, 4 * TT], BF16) for i in range(2)]; PT_b = [Buf(f"PT{i}") for i in range(2)]
        o0 = sb("o0", [128, TT]); o0_b = Buf("o0")
        rl = sb("rl", [128, TT]); rl_b = Buf("rl")
        ya = sb("ya", [128, TT]); ya_b = Buf("ya")
        mixT = hid[:, 8:16, :]; mixT_b = Buf("mixT")
        lamrow = sb("lamrow", [1, 4 * 64]); lamrow_b = Buf("lamrow")
        lam1 = sb("lam1", [1, 4]); lam1_b = Buf("lam1")
        ones_f = sb("ones_f", [1, 128]); onesf_b = Buf("ones_f")
        lam_t = sb("lam_t", [128, 1]); lam_b = Buf("lam_t")
        subln_t = sb("subln_t", [128, 1]); subln_b = Buf("subln")
        eps128 = sb("eps128", [128, 1])
        shc = sb("shc", [128, 14]); shc_b = Buf("shc")
        NCH = TT // 64
        cv = {nm: (sb("c_" + nm, [128, 4]), Buf("c_" + nm)) for nm in ("w0", "a0", "k_k", "k_a", "r_k", "ln_x_w", "ln_x_b", "nw0", "omka")}
        wa2 = sb("wa2_t", [128, 512], BF16); wa2_b = Buf("wa2")
        g2t = sb("g2_t", [128, 512], BF16); g2_b = Buf("g2")
        resetm = sb("resetm_t", [128, TT]); maskA4 = sb("maskA4_t", [128, 512]); maskC8 = sb("maskC8_t", [128, 512])
        I8 = sb("I8_t", [128, 512]); bones = sb("bones_t", [128, 128], BF16); cst_b = Buf("scan_consts")
        lora_in = sb("lora_in", [128, TT], BF16); lora_b = Buf("lora_in")
        sgx = sb("sgx", [128, TT], BF16); sgx_b = Buf("sgx")
        ld = sb("ld", [128, 4, TT]); ld_b = Buf("ld")
        a_t = sb("a_t", [128, 4, TT]); a_b = Buf("a_t")
        gT = sb("gT", [128, 4, TT]); gT_b = Buf("gT")
        bs_t = sb("bs_t", [128, 4, TT]); bs_b = Buf("bs_t")
        f1 = sb("f1", [128, TT]); f2 = sb("f2", [128, TT]); f3 = sb("f3", [128, TT]); f4 = sb("f4", [128, TT]); f5 = sb("f5", [128, TT]); f6 = sb("f6", [128, TT])
        f_b = [Buf(f"f{i}") for i in range(6)]
        fbf = sb("fbf", [128, TT], BF16); fbf_b = Buf("fbf")
        gC = sb("gC", [128, 4, NCH]); gC_b = Buf("gC")
        kr = sb("kr", [128, 4, NCH, 128], BF16); kr_b = Buf("kr")
        kb_ = sb("kb_", [128, 4, NCH, 128], BF16); kb_b = Buf("kb")
        khat = z[:, 4:8, :]; khat_b = z_b
        bhat = v_tm[:, 0, :].rearrange("p (a b) -> p a b", a=4); bhat_b = v_b
        V_tm = sb("V_tm", [128, NBLK, 512], BF16); Vtm_b = Buf("V_tm")
        Kh_tm = sb("Kh_tm", [128, NBLK, 512], BF16); Kh_b = Buf("Kh_tm")
        Bh_tm = sb("Bh_tm", [128, NBLK, 512], BF16); Bh_b = Buf("Bh_tm")
        Am = sb("Am", [128, 8, 128], BF16); Am_b = Buf("Am")
        Bm = sb("Bm", [128, 8, 128], BF16); Bm_b = Buf("Bm")
        Zb = [sb(f"Zb{i}", [128, 8, 64], BF16) for i in range(2)]; Zb_b = [Buf(f"Zb{i}") for i in range(2)]
        ZTb = [sb(f"ZTb{i}", [128, 8, 64], BF16) for i in range(2)]; ZTb_b = [Buf(f"ZTb{i}") for i in range(2)]
        Wf = sb("Wf", [128, 8, 64]); Wf_b = Buf("Wf")
        Wb = sb("Wb", [128, 8, 64], BF16); Wb_b = Buf("Wb")
        Rb = sb("Rb", [128, 8, 64], BF16); Rb_b = Buf("Rb")
        Ub = sb("Ub", [128, 8, 64], BF16); Ub_b = Buf("Ub")
        Af = sb("Af", [128, 4, 64]); Af_b = Buf("Af")
        Ab = sb("Ab", [128, 4, 64], BF16); Ab_b = Buf("Ab")
        y_tm = sb("y_tm", [128, NBLK, 512]); ytm_b = Buf("y_tm")
        ysq = z[:, 0:4, :].rearrange("p a (b c) -> p b (a c)", b=NBLK) if False else None; ysq_b = z_b
        gst = sb("gst", [128, 4, NBLK * 8]); gst_b = Buf("gst")
        gneps = sb("gneps", [128, 1])
        wkvT = y_tm[0:64, 0, :].rearrange("p (h k) -> p h k", k=64); wkvT_b = ytm_b

        NPG1 = NPG + 1
        sel_t = sb("sel_t", [128, 128], BF16); sel_b = Buf("sel")
        diag64 = sb("diag64_t", [128, 64]); bones_f = sb("bones_f", [128, 128]); sconst_b = Buf("sconst")
        ptb = sb("ptb", [128, NS * NPG], I32); idx = sb("idx", [128, NS * NPG], I32); iota_c = sb("iota_c", [128, 1], I32); idx_b = Buf("idx")
        NKR = 4
        kpg = [sb(f"kpg{i}", [128, 512], BF16) for i in range(NKR)]; kpg_b = [Buf(f"kpg{i}") for i in range(NKR)]
        vpg = [sb(f"vpg{i}", [128, 512], BF16) for i in range(NKR)]; vpg_b = [Buf(f"vpg{i}") for i in range(NKR)]
        qs_bf = sb("qs_bf", [128, 512], BF16); qs_b = Buf("qs_bf")
        knew_bf = sb("knew_bf", [128, 512], BF16); vnew_bf = sb("vnew_bf", [128, 512], BF16); new_b = Buf("newkv")
        prod = [ld[:].rearrange("p a b -> p (a b)"), a_t[:].rearrange("p a b -> p (a b)")]; prod_b = [ld_b, a_b]
        sT = sb("sT", [128, NPG1, 8]); sT_b = Buf("sT")
        pT = sb("pT", [128, NPG1, 8], BF16); pT_b = Buf("pT")
        psum8 = sb("psum8", [128, 8]); psum8_b = Buf("psum8")
        ones_f32 = sb("ones_f32", [128, 128])
        oS = sb("oS", [128, NS, 8]); oS_b = Buf("oS")
        lS = sb("lS", [128, NS, 8]); lS_b = Buf("lS")
        yaS = sb("yaS", [128, 4, NS]); yaS_b = Buf("yaS")
        raw = {nm: (sb("raw_" + nm, [128, 4, NS]), Buf("raw_" + nm)) for nm in ("kkn", "kf", "b", "w", "r")}
        HN = NS // 8
        S_t = sb("S_t", [128, HN, 4, 64]); S_b = Buf("S_t")
        Xb = sb("Xb", [128, HN, 4, 64]); Xb_b = Buf("Xb")
        Xe = xtok[:, 0, 0:4 * HN * 64].rearrange("p (m n k) -> p m n k", m=4, k=64); Xe_b = xtok_b
        T1 = z[:].rearrange("p a b -> p (a b)")[:, 0:HN * 256].rearrange("p (n m k) -> p n m k", m=4, k=64); T1_b = z_b
        red = sb("red", [128, HN, 4]); red_b = Buf("red")
        yS = sb("yS", [128, 4, NS]); yS_b = Buf("yS")
        gS = [sb(f"gS{i}", [128, 4 * NS]) for i in range(3)]; gS_b = [Buf(f"gS{i}") for i in range(3)]

        blk = es.enter_context(nc.Block())
        cx = Ctx(nc, sem, dsem)
        cx.bind("pe", nc.tensor); cx.bind("act", nc.scalar); cx.bind("dve", nc.vector)
        cx.bind("pool", nc.gpsimd); cx.bind("sp", nc.sync)
        st = {"w": 0, "d": 0, "pb": 0}

        cx.dma("sp", lambda e: e.dma_start(out=ident[:], in_=I["ident"]), W=[ident_b])
        cx.dma("pool", lambda e: e.dma_start(out=ones_bf[:], in_=I["ones"]), W=[ones_b])
        for nm, (t, b) in gains.items():
            cx.dma("sp", lambda e, t=t, nm=nm: e.dma_start(out=t[:], in_=I[nm]), W=[b])
        cx.op("dve", lambda e: e.memset(eps_t[:], NORM_EPS), W=[eps_b])

        def load_wpiece(W, col0, ncols=256, row0=0):
            i = st["w"] % NW
            st["w"] += 1
            src = W[row0:row0 + 1024, col0:col0 + ncols].rearrange("(c p) n -> p c n", p=128)
            cx.dma("pool", lambda e: e.dma_start(out=wring[i][:, :, 0:ncols], in_=src), W=[wring_b[i]])
            return wring[i], wring_b[i]

        def load_dpiece(W, col0):
            i = st["d"] % ND
            st["d"] += 1
            src = W[:, col0:col0 + 128].rearrange("(c p) n -> p c n", p=128)
            cx.dma("pool", lambda e: e.dma_start(out=dring[i][:], in_=src), W=[dring_b[i]])
            return dring[i], dring_b[i]

        def next_bank():
            i = st["pb"] % 6
            st["pb"] += 1
            return pbank[i], pbank_b[i]

        def rms_stats(src, src_b, N):
            cx.op("act", lambda e: e.activation(out=sq[:, :, 0:N], in_=src[:, :, 0:N], func=AF.Square), R=[src_b], W=[sq_b])
            pb, pbb = next_bank()
            for c in range(8):
                cx.op("pe", lambda e, c=c: e.matmul(pb[:, 0:N], lhsT=ones_bf[:], rhs=sq[:, c, 0:N], start=(c == 0), stop=(c == 7)),
                      R=[ones_b, sq_b], W=[pbb], inc=(c == 7))
            cx.op("act", lambda e: e.activation(out=rstd[:, 0:N], in_=pb[:, 0:N], func=AF.Sqrt, scale=1.0 / D, bias=eps_t[:]),
                  R=[pbb, eps_b], W=[rstd_b])
            cx.op("dve", lambda e: e.reciprocal(out=rstd[:, 0:N], in_=rstd[:, 0:N]), R=[rstd_b], W=[rstd_b])

        def pre_norm(src, src_b, gname, N):
            rms_stats(src, src_b, N)
            g, gb = gains[gname]
            for c in range(8):
                cx.op("dve", lambda e, c=c: e.scalar_tensor_tensor(out=u[:, c, 0:N], in0=src[:, c, 0:N], scalar=g[:, c:c + 1],
                                                                  in1=rstd[:, 0:N], op0=ALU.mult, op1=ALU.mult),
                      R=[src_b, gb, rstd_b], W=[u_b])

        def post_norm_add(gname, N, half):
            rms_stats(z, z_b, N)
            g, gb = gains[gname]
            for c in range(8):
                cx.op("dve", lambda e, c=c: e.scalar_tensor_tensor(out=z[:, c, 0:N], in0=z[:, c, 0:N], scalar=g[:, c:c + 1],
                                                                  in1=rstd[:, 0:N], op0=ALU.mult, op1=ALU.mult),
                      R=[z_b, gb, rstd_b], W=[z_b])
            cx.op("dve", lambda e: e.scalar_tensor_tensor(out=xT[:, :, 0:N], in0=z[:, :, 0:N], scalar=(0.5 if half else 1.0),
                                                          in1=xT[:, :, 0:N], op0=ALU.mult, op1=ALU.add),
                  R=[z_b, xT_b], W=[xT_b])

        def ffn(Wg, Wu, Wd, gpre, gpost, N):
            pre_norm(xT, xT_b, gpre, N)
            for f2 in range(NFF // 2):
                wg, wgb = load_wpiece(Wg, f2 * 256)
                wu, wub = load_wpiece(Wu, f2 * 256)
                for ff in range(2):
                    f = f2 * 2 + ff
                    pg, pgb = next_bank()
                    pu, pub = next_bank()
                    for c in range(8):
                        cx.op("pe", lambda e, c=c: e.matmul(pg[:, 0:N], lhsT=wg[:, c, ff * 128:(ff + 1) * 128], rhs=u[:, c, 0:N],
                                                            start=(c == 0), stop=(c == 7)), R=[wgb, u_b], W=[pgb], inc=(c == 7))
                    for c in range(8):
                        cx.op("pe", lambda e, c=c: e.matmul(pu[:, 0:N], lhsT=wu[:, c, ff * 128:(ff + 1) * 128], rhs=u[:, c, 0:N],
                                                            start=(c == 0), stop=(c == 7)), R=[wub, u_b], W=[pub], inc=(c == 7))
                    s, sbb = sg[f % 2], sg_b[f % 2]
                    cx.op("act", lambda e: e.activation(out=s[:, 0:N], in_=pg[:, 0:N], func=AF.Silu), R=[pgb], W=[sbb])
                    cx.op("dve", lambda e, f=f: e.tensor_tensor(out=hid[:, f, 0:N], in0=s[:, 0:N], in1=pu[:, 0:N], op=ALU.mult),
                          R=[sbb, pub], W=[hid_b[f]] + ([sq_b] if f < 8 else []) + ([mixT_b] if 8 <= f < 16 else []))
            for m in range(8):
                wd, wdb = load_dpiece(Wd, m * 128)
                pz, pzb = next_bank()
                for f in range(NFF):
                    cx.op("pe", lambda e, f=f: e.matmul(pz[:, 0:N], lhsT=wd[:, f, :], rhs=hid[:, f, 0:N], start=(f == 0), stop=(f == NFF - 1)),
                          R=[wdb, hid_b[f]] + ([sq_b] if f < 8 else []) + ([mixT_b] if 8 <= f < 16 else []), W=[pzb], inc=(f == NFF - 1))
                cx.op("act", lambda e, m=m: e.copy(out=z[:, m, 0:N], in_=pz[:, 0:N]), R=[pzb], W=[z_b])
            post_norm_add(gpost, N, True)

        def load_x_tile(src_rows, nblk):
            for b in range(nblk):
                cx.dma("sp", lambda e: e.dma_start(out=xtok[:, 0, :], in_=src_rows[b * 128:(b + 1) * 128, :]), W=[xtok_b])
                for c2 in range(2):
                    pb, pbb = next_bank()
                    for cc in range(4):
                        c = c2 * 4 + cc
                        cx.op("pe", lambda e, c=c, cc=cc: e.transpose(pb[:, cc * 128:(cc + 1) * 128], xtok[:, 0, c * 128:(c + 1) * 128], ident[:]),
                              R=[xtok_b, ident_b], W=[pbb], inc=(cc == 3))
                    cx.op("act", lambda e: e.copy(out=xT[:, c2 * 4:(c2 + 1) * 4, b * 128:(b + 1) * 128], in_=pb[:, :].rearrange("p (c t) -> p c t", t=128)), R=[pbb], W=[xT_b])

        def store_x_tile(dst_rows, nblk):
            for b in range(nblk):
                for c2 in range(2):
                    pb, pbb = next_bank()
                    for cc in range(4):
                        c = c2 * 4 + cc
                        cx.op("pe", lambda e, c=c, cc=cc: e.transpose(pb[:, cc * 128:(cc + 1) * 128], xT[:, c, b * 128:(b + 1) * 128], ident[:]),
                              R=[xT_b, ident_b], W=[pbb], inc=(cc == 3))
                    cx.op("act", lambda e: e.copy(out=xtok[:, 0, c2 * 512:(c2 + 1) * 512], in_=pb[:, :]), R=[pbb], W=[xtok_b])
                cx.dma("sp", lambda e: e.dma_start(out=dst_rows[b * 128:(b + 1) * 128, :], in_=xtok[:, 0, :]), R=[xtok_b])

        cx.dma("sp", lambda e: e.dma_start(out=mu_t[:], in_=I["mu"]), W=[mu_b])
        cx.dma("pool", lambda e: e.dma_start(out=cmask[:], in_=I["cmask"]), W=[cmask_b])
        cx.dma("sp", lambda e: e.dma_start(out=lamrow[:], in_=I["lamv"]), W=[lamrow_b])
        cx.dma("sp", lambda e: e.dma_start(out=subln_t[:], in_=I["subln"]), W=[subln_b])
        cx.op("dve", lambda e: e.memset(ones_f[:], 1.0), W=[onesf_b])
        cx.op("dve", lambda e: e.memset(eps128[:], NORM_EPS), W=[eps_b])
        cx.op("dve", lambda e: e.memset(pbT[:, :, 0:1], 0.0), W=[pbT_b])
        lr = lamrow[:].rearrange("p (a d) -> p a d", d=64)
        cx.op("dve", lambda e: e.tensor_tensor(out=lr[:, 0:1, :], in0=lr[:, 0:1, :], in1=lr[:, 1:2, :], op=ALU.mult), R=[lamrow_b], W=[lamrow_b])
        cx.op("dve", lambda e: e.tensor_tensor(out=lr[:, 2:3, :], in0=lr[:, 2:3, :], in1=lr[:, 3:4, :], op=ALU.mult), R=[lamrow_b], W=[lamrow_b])
        cx.op("dve", lambda e: e.reduce_sum(out=lam1[:, 0:1], in_=lr[:, 0, :], axis=AX.X), R=[lamrow_b], W=[lam1_b])
        cx.op("dve", lambda e: e.reduce_sum(out=lam1[:, 1:2], in_=lr[:, 2, :], axis=AX.X), R=[lamrow_b], W=[lam1_b])
        cx.op("act", lambda e: e.activation(out=lam1[:, 2:4], in_=lam1[:, 0:2], func=AF.Exp), R=[lam1_b], W=[lam1_b])
        cx.op("dve", lambda e: e.tensor_tensor(out=lam1[:, 0:1], in0=lam1[:, 2:3], in1=lam1[:, 3:4], op=ALU.subtract), R=[lam1_b], W=[lam1_b])
        cx.op("dve", lambda e: e.tensor_scalar(out=lam1[:, 0:1], in0=lam1[:, 0:1], scalar1=LAM_INIT, scalar2=-1.0, op0=ALU.add, op1=ALU.mult), R=[lam1_b], W=[lam1_b])
        _pb, _pbb = next_bank()
        cx.op("pe", lambda e: e.matmul(_pb[:, 0:1], lhsT=ones_f[:], rhs=lam1[:, 0:1], start=True, stop=True), R=[onesf_b, lam1_b], W=[_pbb])
        cx.op("act", lambda e: e.copy(out=lam_t[:], in_=_pb[:, 0:1]), R=[_pbb], W=[lam_b])

        def win_stage(N):
            nb = N // 128
            pre_norm(xT, xT_b, "n_mix_pre", N)
            for pi in range(6):
                wp, wpb = load_wpiece(I["w_in"], pi * 256)
                for b in range(nb):
                    pb, pbb = next_bank()
                    for c in range(8):
                        cx.op("pe", lambda e, c=c: e.matmul(pb[:, 0:256], lhsT=u[:, c, b * 128:(b + 1) * 128], rhs=wp[:, c, :], start=(c == 0), stop=(c == 7)),
                              R=[wpb, u_b], W=[pbb], inc=(c == 7))
                    k = st.get("stg", 0) % 3
                    st["stg"] = st.get("stg", 0) + 1
                    cx.op("act", lambda e: e.copy(out=stg[k][:], in_=pb[:, 0:256]), R=[pbb], W=[stg_b[k]])
                    cx.dma("sp", lambda e: e.dma_start(out=qkv_s[b * 128:(b + 1) * 128, pi * 256:(pi + 1) * 256], in_=stg[k][:]), R=[stg_b[k]], W=[qkv_sb])
            for pi in range(7):
                wp, wpb = load_wpiece(I["w_in"], 1536 + pi * 256)
                for hf in range(2):
                    j = pi * 2 + hf
                    pb, pbb = next_bank()
                    for c in range(8):
                        cx.op("pe", lambda e, c=c: e.matmul(pb[:, 0:N], lhsT=wp[:, c, hf * 128:(hf + 1) * 128], rhs=u[:, c, 0:N], start=(c == 0), stop=(c == 7)),
                              R=[wpb, u_b], W=[pbb], inc=(c == 7))
                    k = j % 2
                    cx.op("act", lambda e: e.copy(out=stg2[k][:, 0:N], in_=pb[:, 0:N]), R=[pbb], W=[stg2_b[k]])
                    cx.dma("sp", lambda e: e.dma_start(out=pb_s[:, j, 0:N], in_=stg2[k][:, 0:N]), R=[stg2_b[k]], W=[pb_sb])

        def sub_front(s, cos_src, sin_src):
            N = TT
            nblk = 1
            cx.dma("sp", lambda e: e.dma_start(out=qk_tm[:, 0, :], in_=qkv_s[s * 128:(s + 1) * 128, 0:1024]), R=[qkv_sb], W=[qk_b])
            cx.dma("sp", lambda e: e.dma_start(out=v_tm[:, 0, :], in_=qkv_s[s * 128:(s + 1) * 128, 1024:1536]), R=[qkv_sb], W=[v_b])
            cx.dma("sp", lambda e: e.dma_start(out=pbT[:, :, 1:N + 1], in_=pb_s[:, :, s * 128:(s + 1) * 128]), R=[pb_sb], W=[pbT_b])
            cx.dma("sp", lambda e: e.dma_start(out=cos_t[:, 0:nblk, :], in_=cos_src.rearrange("(b p) d -> p b d", p=128)), W=[cs_b])
            cx.dma("sp", lambda e: e.dma_start(out=sin_t[:, 0:nblk, :], in_=sin_src.rearrange("(b p) d -> p b d", p=128)), W=[cs_b])
            for b in range(nblk):
                xv = qk_tm[:, b, :].rearrange("p (g d) -> p g d", d=64)
                cv_ = cos_t[:, b, :].rearrange("p (g d) -> p g d", d=8)
                sv = sin_t[:, b, :].rearrange("p (g d) -> p g d", d=8)
                x1, x2 = xv[:, :, 0:8], xv[:, :, 8:16]
                cx.op("dve", lambda e: e.tensor_tensor(out=rt[0][:], in0=x1, in1=cv_, op=ALU.mult), R=[qk_b, cs_b], W=[rt_b[0]])
                cx.op("dve", lambda e: e.tensor_tensor(out=rt[1][:], in0=x2, in1=sv, op=ALU.mult), R=[qk_b, cs_b], W=[rt_b[1]])
                cx.op("dve", lambda e: e.tensor_tensor(out=rt[2][:], in0=x2, in1=cv_, op=ALU.mult), R=[qk_b, cs_b], W=[rt_b[2]])
                cx.op("dve", lambda e: e.tensor_tensor(out=rt[3][:], in0=x1, in1=sv, op=ALU.mult), R=[qk_b, cs_b], W=[rt_b[3]])
                cx.op("dve", lambda e: e.tensor_tensor(out=x1, in0=rt[0][:], in1=rt[1][:], op=ALU.subtract), R=[rt_b[0], rt_b[1]], W=[qk_b])
                cx.op("dve", lambda e: e.tensor_tensor(out=x2, in0=rt[2][:], in1=rt[3][:], op=ALU.add), R=[rt_b[2], rt_b[3]], W=[qk_b])
            return nblk

        def prompt_kv_out(t, nblk):
            cx.dma("sp", lambda e: e.dma_start(out=O["kp"][t * TT:(t + 1) * TT, :].rearrange("(b p) d -> p b d", p=128), in_=qk_tm[:, 0:nblk, 512:1024]), R=[qk_b])
            cx.dma("sp", lambda e: e.dma_start(out=O["vp"][t * TT:(t + 1) * TT, :].rearrange("(b p) d -> p b d", p=128), in_=v_tm[:, 0:nblk, :]), R=[v_b])
            for h in range(4):
                pb, pbb = next_bank()
                for b in range(nblk):
                    cx.op("pe", lambda e, b=b: e.transpose(pb[:, b * 128:(b + 1) * 128], qk_tm[:, b, h * 128:(h + 1) * 128], ident[:]),
                          R=[qk_b, ident_b], W=[pbb], inc=(b == nblk - 1))
                cx.op("act", lambda e: e.activation(out=QT[:, h, 0:nblk * 128], in_=pb[:, 0:nblk * 128], func=AF.Copy, scale=0.125), R=[pbb], W=[QT_b])
                pb2, pbb2 = next_bank()
                for b in range(nblk):
                    cx.op("pe", lambda e, b=b: e.transpose(pb2[:, b * 128:(b + 1) * 128], qk_tm[:, b, 512 + h * 128:512 + (h + 1) * 128], ident[:]),
                          R=[qk_b, ident_b], W=[pbb2], inc=(b == nblk - 1))
                cx.op("act", lambda e: e.copy(out=ktn[:, h, :], in_=pb2[:, 0:128]), R=[pbb2], W=[ktn_b])
            cx.op("pool", lambda e: e.tensor_copy(out=vbn[:].rearrange("p h e -> p (h e)"), in_=v_tm[:, 0, :]), R=[v_b], W=[vbn_b])
            cx.dma("sp", lambda e: e.dma_start(out=KT_s[:, :, t * 128:(t + 1) * 128].rearrange("h p k -> p h k"), in_=ktn[:]), R=[ktn_b], W=[KTs_b])
            cx.dma("sp", lambda e: e.dma_start(out=V_s[:, :, t, :].rearrange("h p e -> p h e"), in_=vbn[:]), R=[vbn_b], W=[Vs_b])

        def pb_proj(N):
            for pi in range(7):
                wp, wpb = load_wpiece(I["w_in"], 1536 + pi * 256)
                for hf in range(2):
                    j = pi * 2 + hf
                    pb, pbb = next_bank()
                    for c in range(8):
                        cx.op("pe", lambda e, c=c: e.matmul(pb[:, 0:N], lhsT=wp[:, c, hf * 128:(hf + 1) * 128], rhs=u[:, c, 0:N], start=(c == 0), stop=(c == 7)),
                              R=[wpb, u_b], W=[pbb], inc=(c == 7))
                    cx.op("act", lambda e, j=j: e.copy(out=pbT[:, j, 1:N + 1], in_=pb[:, 0:N]), R=[pbb], W=[pbT_b])

        def shift_mix(N):
            cx.op("dve", lambda e: e.tensor_tensor(out=xs[:, :, 0:N], in0=pbT[:, :, 0:N], in1=pbT[:, :, 1:N + 1], op=ALU.subtract), R=[pbT_b], W=[xs_b])
            for j in range(14):
                cx.op("dve", lambda e, j=j: e.scalar_tensor_tensor(out=xs[:, j, 0:N], in0=xs[:, j, 0:N], scalar=mu_t[:, j:j + 1], in1=pbT[:, j, 1:N + 1],
                                                                  op0=ALU.mult, op1=ALU.add), R=[xs_b, mu_b, pbT_b], W=[xs_b])

        def prompt_attention(t):
            for h in range(4):
                attention_head(t, h)

        def attention_head(t, h):
            nkb = (t + 1) * NBLK
            nch = (nkb + 7) // 8
            if True:
                for j in range(2):
                    po, pob = pbank[6], pbank_b[6]
                    pl, plb = pbank[7], pbank_b[7]
                    for ch in range(nch):
                        nb = min(8, nkb - ch * 8)
                        ri = st.get("kv", 0) % 2
                        st["kv"] = st.get("kv", 0) + 1
                        kc, kcb, vc, vcb = kring[ri], kring_b[ri], vring[ri], vring_b[ri]
                        cx.dma("sp", lambda e: e.dma_start(out=kc[:, 0:nb * 128], in_=KT_s[h, :, ch * KC:ch * KC + nb * 128]), R=[KTs_b], W=[kcb])
                        cx.dma("sp", lambda e: e.dma_start(out=vc[:, 0:nb, :], in_=V_s[h, :, ch * 8:ch * 8 + nb, :]), R=[Vs_b], W=[vcb])
                        ngr = (nb + 3) // 4
                        prev = None
                        for g in range(ngr + 1):
                            if g < ngr:
                                k0 = g * 4
                                ng = min(4, nb - k0)
                                pS, pSb = next_bank()
                                for i in range(ng):
                                    kl = k0 + i
                                    cx.op("pe", lambda e: e.matmul(pS[:, i * TT:(i + 1) * TT], lhsT=kc[64 * j:64 * j + 64, kl * 128:(kl + 1) * 128], rhs=QT[64 * j:64 * j + 64, h, :],
                                                                   start=True, stop=True), R=[kcb, QT_b], W=[pSb], inc=(i == ng - 1))
                                pi_ = st.get("pt", 0) % 2
                                st["pt"] = st.get("pt", 0) + 1
                                p, pbf = PT[pi_], PT_b[pi_]
                                cx.op("act", lambda e: e.activation(out=p[:, 0:ng * TT], in_=pS[:, 0:ng * TT], func=AF.Exp), R=[pSb], W=[pbf])
                                for i in range(ng):
                                    r = ch * 8 + k0 + i - t * NBLK
                                    if r >= 0:
                                        cx.op("pool", lambda e: e.tensor_tensor(out=p[:, i * TT:(i + 1) * TT], in0=p[:, i * TT:(i + 1) * TT], in1=cmask[:, r, :], op=ALU.mult),
                                              R=[pbf, cmask_b], W=[pbf])
                                cur = (k0, ng, p, pbf)
                            else:
                                cur = None
                            if prev is not None:
                                k00, ng0, p0, pbf0 = prev
                                for i in range(ng0):
                                    kl0 = k00 + i
                                    kb0 = ch * 8 + kl0
                                    cx.op("pe", lambda e: e.matmul(po[:, 0:TT], lhsT=vc[:, kl0, :], rhs=p0[:, i * TT:(i + 1) * TT], start=(kb0 == 0), stop=(kb0 == nkb - 1)),
                                          R=[vcb, pbf0], W=[pob], inc=False)
                                    cx.op("pe", lambda e: e.matmul(pl[:, 0:TT], lhsT=ones_bf[:], rhs=p0[:, i * TT:(i + 1) * TT], start=(kb0 == 0), stop=(kb0 == nkb - 1)),
                                          R=[ones_b, pbf0], W=[plb], inc=True)
                            prev = cur
                    cx.op("dve", lambda e: e.reciprocal(out=rl[:], in_=pl[:, 0:TT]), R=[plb], W=[rl_b])
                    if j == 0:
                        cx.op("dve", lambda e: e.tensor_tensor(out=o0[:], in0=po[:, 0:TT], in1=rl[:], op=ALU.mult), R=[pob, rl_b], W=[o0_b])
                    else:
                        cx.op("dve", lambda e: e.tensor_tensor(out=ya[:], in0=po[:, 0:TT], in1=rl[:], op=ALU.mult), R=[pob, rl_b], W=[ya_b])
                        cx.op("dve", lambda e: e.scalar_tensor_tensor(out=ya[:], in0=ya[:], scalar=lam_t[:, 0:1], in1=o0[:], op0=ALU.mult, op1=ALU.add),
                              R=[ya_b, lam_b, o0_b], W=[ya_b])
                subln_norm(h, TT)

        def subln_norm(h, N):
            cx.op("act", lambda e: e.activation(out=sq[:, 0, 0:N], in_=ya[:, 0:N], func=AF.Square), R=[ya_b], W=[sq_b])
            pb, pbb = next_bank()
            cx.op("pe", lambda e: e.matmul(pb[:, 0:N], lhsT=ones_bf[:], rhs=sq[:, 0, 0:N], start=True, stop=True), R=[ones_b, sq_b], W=[pbb])
            cx.op("act", lambda e: e.activation(out=rstd[:, 0:N], in_=pb[:, 0:N], func=AF.Sqrt, scale=1.0 / 128, bias=eps128[:]), R=[pbb, eps_b], W=[rstd_b])
            cx.op("dve", lambda e: e.reciprocal(out=rstd[:, 0:N], in_=rstd[:, 0:N]), R=[rstd_b], W=[rstd_b])
            cx.op("dve", lambda e: e.scalar_tensor_tensor(out=ya[:, 0:N], in0=ya[:, 0:N], scalar=subln_t[:, 0:1], in1=rstd[:, 0:N], op0=ALU.mult, op1=ALU.mult),
                  R=[ya_b, subln_b, rstd_b], W=[ya_b])
            cx.op("dve", lambda e: e.tensor_scalar(out=mixT[:, h, st["moff"]:st["moff"] + N], in0=ya[:, 0:N], scalar1=1.0 - LAM_INIT, scalar2=None, op0=ALU.mult), R=[ya_b], W=[mixT_b])

        for nm in ("w0", "a0", "k_k", "k_a", "r_k", "ln_x_w", "ln_x_b"):
            cx.dma("sp", lambda e, nm=nm: e.dma_start(out=cv[nm][0][:], in_=I[nm]), W=[cv[nm][1]])
        cx.dma("pool", lambda e: e.dma_start(out=wa2[:], in_=I["wa2"]), W=[wa2_b])
        cx.dma("pool", lambda e: e.dma_start(out=g2t[:], in_=I["g2"]), W=[g2_b])
        cx.dma("pool", lambda e: e.dma_start(out=bones[:], in_=I["blockones"]), W=[cst_b])
        for tl, nm in ((resetm, "resetm"), (maskA4, "maskA4"), (maskC8, "maskC8"), (I8, "I8")):
            cx.dma("sp", lambda e, tl=tl, nm=nm: e.dma_start(out=tl[:], in_=I[nm]), W=[cst_b])
        cx.op("dve", lambda e: e.tensor_scalar(out=cv["nw0"][0][:], in0=cv["w0"][0][:], scalar1=-1.0, scalar2=None, op0=ALU.mult), R=[cv["w0"][1]], W=[cv["nw0"][1]])
        cx.op("dve", lambda e: e.tensor_scalar(out=cv["omka"][0][:], in0=cv["k_a"][0][:], scalar1=-1.0, scalar2=1.0, op0=ALU.mult, op1=ALU.add), R=[cv["k_a"][1]], W=[cv["omka"][1]])
        cx.op("dve", lambda e: e.memset(gneps[:], GN_EPS), W=[eps_b])
        cx.op("dve", lambda e: e.memset(Af[:], 0.0), W=[Af_b])
        cx.op("dve", lambda e: e.memset(Ab[:], 0.0), W=[Ab_b])

        def C(nm):
            return cv[nm][0]

        def Cb(nm):
            return cv[nm][1]

        def prep_head(N):
            nch = N // 64
            nblk = N // 128
            cx.op("act", lambda e: e.activation(out=lora_in[0:64, 0:N], in_=xs[0:64, 12, 0:N], func=AF.Tanh), R=[xs_b], W=[lora_b])
            cx.op("dve", lambda e: e.tensor_copy(out=lora_in[64:128, 0:N], in_=xs[64:128, 12, 0:N]), R=[xs_b], W=[lora_b])
            cx.op("act", lambda e: e.activation(out=sgx[:, 0:N], in_=xs[:, 13, 0:N], func=AF.Sigmoid), R=[xs_b], W=[sgx_b])

        def prep_m(N, m, sample=False):
            nch = N // 64
            ms = slice(m * 128, (m + 1) * 128)
            pa, pab = next_bank()
            cx.op("pe", lambda e: e.matmul(pa[:, 0:N], lhsT=wa2[64:128, ms], rhs=lora_in[64:128, 0:N], start=True, stop=True), R=[wa2_b, lora_b], W=[pab])
            cx.op("act", lambda e: e.activation(out=a_t[:, m, 0:N], in_=pa[:, 0:N], func=AF.Sigmoid, bias=C("a0")[:, m:m + 1]), R=[pab, Cb("a0")], W=[a_b])
            pg, pgb = next_bank()
            cx.op("pe", lambda e: e.matmul(pg[:, 0:N], lhsT=g2t[:, ms], rhs=sgx[:, 0:N], start=True, stop=True), R=[g2_b, sgx_b], W=[pgb])
            cx.op("act", lambda e: e.copy(out=gT[:, m, 0:N], in_=pg[:, 0:N]), R=[pgb], W=[gT_b])
            ms = slice(m * 128, (m + 1) * 128)
            pw, pwb = next_bank()
            cx.op("pe", lambda e: e.matmul(pw[:, 0:N], lhsT=wa2[0:64, ms], rhs=lora_in[0:64, 0:N], start=True, stop=True), R=[wa2_b, lora_b], W=[pwb])
            cx.op("act", lambda e: e.activation(out=f1[:, 0:N], in_=pw[:, 0:N], func=AF.Exp, scale=-1.0, bias=C("nw0")[:, m:m + 1]), R=[pwb, Cb("nw0")], W=[f_b[0]])
            cx.op("dve", lambda e: e.tensor_scalar(out=f1[:, 0:N], in0=f1[:, 0:N], scalar1=1.0, scalar2=None, op0=ALU.add), R=[f_b[0]], W=[f_b[0]])
            cx.op("act", lambda e: e.activation(out=f1[:, 0:N], in_=f1[:, 0:N], func=AF.Ln), R=[f_b[0]], W=[f_b[0]])
            cx.op("dve", lambda e: e.tensor_scalar(out=f1[:, 0:N], in0=f1[:, 0:N], scalar1=-1.0, scalar2=-0.5, op0=ALU.mult, op1=ALU.add), R=[f_b[0]], W=[f_b[0]])
            cx.op("act", lambda e: e.activation(out=f1[:, 0:N], in_=f1[:, 0:N], func=AF.Exp), R=[f_b[0]], W=[f_b[0]])
            cx.op("dve", lambda e: e.tensor_scalar(out=ld[:, m, 0:N], in0=f1[:, 0:N], scalar1=-1.0, scalar2=None, op0=ALU.mult), R=[f_b[0]], W=[ld_b])
            r_, k_ = xs[:, m, 0:N], xs[:, 4 + m, 0:N]
            A_ = a_t[:, m, 0:N]
            cx.op("dve", lambda e: e.tensor_scalar(out=f1[:, 0:N], in0=k_, scalar1=C("k_k")[:, m:m + 1], scalar2=None, op0=ALU.mult), R=[xs_b, Cb("k_k")], W=[f_b[0]])
            cx.op("act", lambda e: e.activation(out=fbf[:, 0:N], in_=f1[:, 0:N], func=AF.Square), R=[f_b[0]], W=[fbf_b])
            pn, pnb = next_bank()
            cx.op("pe", lambda e: e.matmul(pn[:, 0:N], lhsT=bones[:], rhs=fbf[:, 0:N], start=True, stop=True), R=[cst_b, fbf_b], W=[pnb])
            cx.op("act", lambda e: e.activation(out=f2[:, 0:N], in_=pn[:, 0:N], func=AF.Sqrt), R=[pnb], W=[f_b[1]])
            cx.op("dve", lambda e: e.tensor_scalar(out=f2[:, 0:N], in0=f2[:, 0:N], scalar1=1e-12, scalar2=None, op0=ALU.max), R=[f_b[1]], W=[f_b[1]])
            cx.op("dve", lambda e: e.reciprocal(out=f2[:, 0:N], in_=f2[:, 0:N]), R=[f_b[1]], W=[f_b[1]])
            cx.op("dve", lambda e: e.tensor_tensor(out=f1[:, 0:N], in0=f1[:, 0:N], in1=f2[:, 0:N], op=ALU.mult), R=[f_b[0], f_b[1]], W=[f_b[0]])
            cx.op("dve", lambda e: e.tensor_scalar(out=f2[:, 0:N], in0=A_, scalar1=C("k_a")[:, m:m + 1], scalar2=C("omka")[:, m:m + 1], op0=ALU.mult, op1=ALU.add),
                  R=[a_b, Cb("k_a"), Cb("omka")], W=[f_b[1]])
            cx.op("dve", lambda e: e.tensor_tensor(out=f2[:, 0:N], in0=f2[:, 0:N], in1=k_, op=ALU.mult), R=[f_b[1], xs_b], W=[f_b[1]])
            cx.op("dve", lambda e: e.tensor_tensor(out=f3[:, 0:N], in0=f1[:, 0:N], in1=A_, op=ALU.mult), R=[f_b[0], a_b], W=[f_b[2]])
            if sample:
                cx.op("pool", lambda e: e.tensor_copy(out=raw["kkn"][0][:, m, :], in_=f1[:, 0:NS]), R=[f_b[0]], W=[raw["kkn"][1]])
                cx.op("pool", lambda e: e.tensor_copy(out=raw["kf"][0][:, m, :], in_=f2[:, 0:NS]), R=[f_b[1]], W=[raw["kf"][1]])
                cx.op("pool", lambda e: e.tensor_copy(out=raw["b"][0][:, m, :], in_=f3[:, 0:NS]), R=[f_b[2]], W=[raw["b"][1]])
                cx.op("pool", lambda e: e.tensor_copy(out=raw["r"][0][:, m, :], in_=xs[:, m, 0:NS]), R=[xs_b], W=[raw["r"][1]])
                cx.op("act", lambda e: e.activation(out=raw["w"][0][:, m, :], in_=ld[:, m, 0:NS], func=AF.Exp), R=[ld_b], W=[raw["w"][1]])
            cx.op("dve", lambda e: e.scalar_tensor_tensor(out=fbf[:, 0:N], in0=r_, scalar=C("r_k")[:, m:m + 1], in1=f2[:, 0:N], op0=ALU.mult, op1=ALU.mult),
                  R=[xs_b, Cb("r_k"), f_b[1]], W=[fbf_b])
            pq, pqb = next_bank()
            cx.op("pe", lambda e: e.matmul(pq[:, 0:N], lhsT=bones[:], rhs=fbf[:, 0:N], start=True, stop=True), R=[cst_b, fbf_b], W=[pqb])
            cx.op("act", lambda e: e.copy(out=bs_t[:, m, 0:N], in_=pq[:, 0:N]), R=[pqb], W=[bs_b])
            if sample:
                return
            cx.op("dve", lambda e: e.tensor_tensor_scan(out=f4[:, 0:N], data0=resetm[:, 0:N], data1=ld[:, m, 0:N], initial=0.0, op0=ALU.mult, op1=ALU.add),
                  R=[cst_b, ld_b], W=[f_b[3]])
            c3 = f4[:, 0:N].rearrange("p (c t) -> p c t", t=64)
            cx.op("act", lambda e: e.activation(out=gC[:, m, 0:nch], in_=c3[:, :, 63], func=AF.Exp), R=[f_b[3]], W=[gC_b])
            cx.op("act", lambda e: e.activation(out=f5[:, 0:N], in_=f4[:, 0:N], func=AF.Exp), R=[f_b[3]], W=[f_b[4]])
            cx.op("dve", lambda e: e.tensor_tensor(out=f6[:, 0:N], in0=f4[:, 0:N], in1=ld[:, m, 0:N], op=ALU.subtract), R=[f_b[3], ld_b], W=[f_b[5]])
            cx.op("act", lambda e: e.activation(out=f6[:, 0:N], in_=f6[:, 0:N], func=AF.Exp), R=[f_b[5]], W=[f_b[5]])
            krv = kr[:, m, 0:nch, :]
            kbv = kb_[:, m, 0:nch, :]
            v3 = lambda tl: tl[:, 0:N].rearrange("p (c t) -> p c t", t=64)
            cx.op("dve", lambda e: e.tensor_tensor(out=krv[:, :, 0:64], in0=v3(f1), in1=v3(f6), op=ALU.mult), R=[f_b[0], f_b[5]], W=[kr_b])
            cx.op("dve", lambda e: e.tensor_tensor(out=krv[:, :, 64:128], in0=xs[:, m, 0:N].rearrange("p (c t) -> p c t", t=64), in1=v3(f5), op=ALU.mult),
                  R=[xs_b, f_b[4]], W=[kr_b])
            cx.op("dve", lambda e: e.tensor_tensor(out=v3(f6), in0=c3[:, :, 63:64].to_broadcast([128, nch, 64]), in1=c3, op=ALU.subtract), R=[f_b[3]], W=[f_b[5]])
            cx.op("act", lambda e: e.activation(out=f6[:, 0:N], in_=f6[:, 0:N], func=AF.Exp), R=[f_b[5]], W=[f_b[5]])
            cx.op("dve", lambda e: e.tensor_tensor(out=khat[:, m, 0:N], in0=f2[:, 0:N], in1=f6[:, 0:N], op=ALU.mult), R=[f_b[1], f_b[5]], W=[khat_b])
            cx.op("dve", lambda e: e.scalar_tensor_tensor(out=bhat[:, m, 0:N], in0=f3[:, 0:N], scalar=-1.0, in1=f6[:, 0:N], op0=ALU.mult, op1=ALU.mult),
                  R=[f_b[2], f_b[5]], W=[bhat_b])
            cx.op("act", lambda e: e.activation(out=f5[:, 0:N], in_=f4[:, 0:N], func=AF.Exp, scale=-1.0), R=[f_b[3]], W=[f_b[4]])
            cx.op("dve", lambda e: e.tensor_tensor(out=f2[:, 0:N], in0=f2[:, 0:N], in1=f5[:, 0:N], op=ALU.mult), R=[f_b[1], f_b[4]], W=[f_b[1]])
            cx.op("dve", lambda e: e.tensor_tensor(out=f3[:, 0:N], in0=f3[:, 0:N], in1=f5[:, 0:N], op=ALU.mult), R=[f_b[2], f_b[4]], W=[f_b[2]])
            cx.op("pool", lambda e: e.tensor_copy(out=kbv[:, :, 0:64], in_=v3(f2)), R=[f_b[1]], W=[kb_b])
            cx.op("pool", lambda e: e.tensor_copy(out=kbv[:, :, 64:128], in_=v3(f3)), R=[f_b[2]], W=[kb_b])

        def prep_tail(N, sample=False):
            nblk = N // 128
            if sample:
                return
            for (srcf, srcb, dst, dstb) in ((lambda m, b: xs[:, 8 + m, b * 128:(b + 1) * 128], xs_b, V_tm, Vtm_b),
                                            (lambda m, b: khat[:, m, b * 128:(b + 1) * 128], khat_b, Kh_tm, Kh_b),
                                            (lambda m, b: bhat[:, m, b * 128:(b + 1) * 128], bhat_b, Bh_tm, Bh_b)):
                for b in range(nblk):
                    pb, pbb = next_bank()
                    for m in range(4):
                        cx.op("pe", lambda e, m=m: e.transpose(pb[:, m * 128:(m + 1) * 128], srcf(m, b), ident[:]), R=[srcb, ident_b], W=[pbb], inc=(m == 3))
                    cx.op("act", lambda e: e.copy(out=dst[:, b, :], in_=pb[:, :]), R=[pbb], W=[dstb])


        def rwkv_prep(N, sample=False):
            prep_head(N)
            for m in range(4):
                prep_m(N, m, sample)
            prep_tail(N, sample)

        def mm64(out, lhsT, rhs, rowb, colb, R, W, start=True, stop=True, inc=True):
            Epe = cx.E["pe"]
            if st.get("rowb") is not None and st["rowb"] != rowb and Epe.cnt > 0:
                Epe.eng.wait_ge(Epe.sem, Epe.cnt)
            st["rowb"] = rowb
            cx.op("pe", lambda e: e.matmul(out, lhsT=lhsT, rhs=rhs, start=start, stop=stop, tile_position=(rowb, colb)), R=R, W=W, inc=True)

        def rwkv_block(b, stage=9):
            pA0, pA0b = next_bank(); pA1, pA1b = next_bank()
            pB0, pB0b = next_bank(); pB1, pB1b = next_bank()
            pC, pCb = next_bank()
            for cp in range(2):
                ci = 2 * b + cp
                cb = 64 * cp
                for h in range(8):
                    hb, m = 64 * (h % 2), h // 2
                    pa, pab = (pA0, pA0b) if h < 4 else (pA1, pA1b)
                    pbk, pbkb = (pB0, pB0b) if h < 4 else (pB1, pB1b)
                    hs = slice((h % 4) * 128, (h % 4 + 1) * 128)
                    mm64(pa[cb:cb + 64, hs], kb_[hb:hb + 64, m, ci, 0:64], kr[hb:hb + 64, m, ci, :], hb, cb, [kb_b, kr_b], [pab])
                    mm64(pbk[cb:cb + 64, hs], kb_[hb:hb + 64, m, ci, 64:128], kr[hb:hb + 64, m, ci, :], hb, cb, [kb_b, kr_b], [pbkb])
                    mm64(pC[cb:cb + 64, h * 64:(h + 1) * 64], kr[hb:hb + 64, m, ci, 0:64], kb_[hb:hb + 64, m, ci, 64:128], hb, cb, [kb_b, kr_b], [pCb])
            Amf = Am[:].rearrange("p h c -> p (h c)")
            Bmf = Bm[:].rearrange("p h c -> p (h c)")
            cx.op("dve", lambda e: e.tensor_tensor(out=Amf[:, 0:512], in0=pA0[:, :], in1=maskA4[:], op=ALU.mult), R=[pA0b, cst_b], W=[Am_b])
            cx.op("dve", lambda e: e.tensor_tensor(out=Amf[:, 512:1024], in0=pA1[:, :], in1=maskA4[:], op=ALU.mult), R=[pA1b, cst_b], W=[Am_b])
            cx.op("dve", lambda e: e.scalar_tensor_tensor(out=Bmf[:, 0:512], in0=pB0[:, :], scalar=-1.0, in1=maskA4[:], op0=ALU.mult, op1=ALU.mult), R=[pB0b, cst_b], W=[Bm_b])
            cx.op("dve", lambda e: e.scalar_tensor_tensor(out=Bmf[:, 512:1024], in0=pB1[:, :], scalar=-1.0, in1=maskA4[:], op0=ALU.mult, op1=ALU.mult), R=[pB1b, cst_b], W=[Bm_b])
            if stage < 3:
                return
            zc, zcb, ztc, ztcb = Zb[0], Zb_b[0], ZTb[0], ZTb_b[0]
            fl = lambda tl: tl[:].rearrange("p h c -> p (h c)")
            cx.op("dve", lambda e: e.tensor_copy(out=zc[:], in_=Bm[:, :, 0:64]), R=[Bm_b], W=[zcb])
            cx.op("dve", lambda e: e.scalar_tensor_tensor(out=fl(ztc), in0=pC[:, :], scalar=-1.0, in1=maskC8[:], op0=ALU.mult, op1=ALU.mult), R=[pCb, cst_b], W=[ztcb])
            cx.op("dve", lambda e: e.tensor_tensor(out=fl(Wf), in0=fl(zc), in1=I8[:], op=ALU.add), R=[zcb, cst_b], W=[Wf_b])
            cx.op("act", lambda e: e.copy(out=Wb[:], in_=Wf[:]), R=[Wf_b], W=[Wb_b])
            cur = 0
            for lvl in range(5):
                nz, nzb, nzt, nztb = Zb[1 - cur], Zb_b[1 - cur], ZTb[1 - cur], ZTb_b[1 - cur]
                pz, pzb = next_bank(); pzt, pztb = next_bank()
                for cp in range(2):
                    cb = 64 * cp
                    for h in range(8):
                        hs = slice(h * 64, (h + 1) * 64)
                        if lvl < 4:
                            mm64(pz[cb:cb + 64, hs], ztc[cb:cb + 64, h, :], zc[cb:cb + 64, h, :], cb, cb, [ztcb, zcb], [pzb])
                        mm64(pzt[cb:cb + 64, hs], zc[cb:cb + 64, h, :], ztc[cb:cb + 64, h, :], cb, cb, [ztcb, zcb], [pztb])
                if lvl < 4:
                    cx.op("act", lambda e: e.copy(out=fl(nz), in_=pz[:, :]), R=[pzb], W=[nzb])
                cx.op("dve", lambda e: e.tensor_copy(out=fl(nzt), in_=pzt[:, :]), R=[pztb], W=[nztb])
                pw, pwb = next_bank()
                for cp in range(2):
                    cb = 64 * cp
                    for h in range(8):
                        hs = slice(h * 64, (h + 1) * 64)
                        mm64(pw[cb:cb + 64, hs], nzt[cb:cb + 64, h, :], Wb[cb:cb + 64, h, :], cb, cb, [nztb, Wb_b], [pwb])
                cx.op("dve", lambda e: e.tensor_tensor(out=fl(Wf), in0=fl(Wf), in1=pw[:, :], op=ALU.add), R=[Wf_b, pwb], W=[Wf_b])
                cx.op("act", lambda e: e.copy(out=Wb[:], in_=Wf[:]), R=[Wf_b], W=[Wb_b])
                cur = 1 - cur
                zc, zcb, ztc, ztcb = Zb[cur], Zb_b[cur], ZTb[cur], ZTb_b[cur]
            if stage < 4:
                return
            for cp in range(2):
                ci = 2 * b + cp
                cb = 64 * cp
                pR, pRb = next_bank()
                for h in range(8):
                    hb, m = 64 * (h % 2), h // 2
                    hs = slice(h * 64, (h + 1) * 64)
                    mm64(pR[cb:cb + 64, hs], kr[hb:hb + 64, m, ci, 0:64], Ab[hb:hb + 64, m, :], hb, cb, [kr_b, Ab_b], [pRb], start=True, stop=False, inc=False)
                    mm64(pR[cb:cb + 64, hs], Am[cb:cb + 64, h, 0:64], V_tm[cb:cb + 64, b, hs], cb, cb, [Am_b, Vtm_b], [pRb], start=False, stop=True, inc=(h == 7))
                cx.op("act", lambda e: e.copy(out=Rb[cb:cb + 64].rearrange("p h c -> p (h c)"), in_=pR[cb:cb + 64, :]), R=[pRb], W=[Rb_b])
                pU, pUb = next_bank()
                for h in range(8):
                    hs = slice(h * 64, (h + 1) * 64)
                    mm64(pU[cb:cb + 64, hs], Wb[cb:cb + 64, h, :], Rb[cb:cb + 64, h, :], cb, cb, [Wb_b, Rb_b], [pUb], inc=(h == 7))
                cx.op("act", lambda e: e.copy(out=Ub[cb:cb + 64].rearrange("p h c -> p (h c)"), in_=pU[cb:cb + 64, :]), R=[pUb], W=[Ub_b])
                pY, pYb = next_bank()
                pAn, pAnb = next_bank()
                for h in range(8):
                    hb, m = 64 * (h % 2), h // 2
                    hs = slice(h * 64, (h + 1) * 64)
                    mm64(pY[cb:cb + 64, hs], kr[hb:hb + 64, m, ci, 64:128], Ab[hb:hb + 64, m, :], hb, cb, [kr_b, Ab_b], [pYb], start=True, stop=False, inc=False)
                    mm64(pY[cb:cb + 64, hs], Am[cb:cb + 64, h, 64:128], V_tm[cb:cb + 64, b, hs], cb, cb, [Am_b, Vtm_b], [pYb], start=False, stop=False, inc=False)
                    mm64(pY[cb:cb + 64, hs], Bm[cb:cb + 64, h, 64:128], Ub[cb:cb + 64, h, :], cb, cb, [Bm_b, Ub_b], [pYb], start=False, stop=True, inc=(h == 7))
                cx.op("act", lambda e: e.copy(out=y_tm[cb:cb + 64, b, :], in_=pY[cb:cb + 64, :]), R=[pYb], W=[ytm_b])
                for h in range(8):
                    hb, m = 64 * (h % 2), h // 2
                    hs = slice(h * 64, (h + 1) * 64)
                    ms_ = slice(m * 64, (m + 1) * 64)
                    mm64(pAn[hb:hb + 64, ms_], Kh_tm[cb:cb + 64, b, hs], V_tm[cb:cb + 64, b, hs], cb, hb, [Kh_b, Vtm_b], [pAnb], start=True, stop=False, inc=False)
                    mm64(pAn[hb:hb + 64, ms_], Bh_tm[cb:cb + 64, b, hs], Ub[cb:cb + 64, h, :], cb, hb, [Bh_b, Ub_b], [pAnb], start=False, stop=True, inc=(h == 7))
                for m in range(4):
                    cx.op("dve", lambda e, m=m: e.scalar_tensor_tensor(out=Af[:, m, :], in0=Af[:, m, :], scalar=gC[:, m, ci:ci + 1], in1=pAn[:, m * 64:(m + 1) * 64],
                                                                      op0=ALU.mult, op1=ALU.add), R=[Af_b, gC_b, pAnb], W=[Af_b])
                cx.op("act", lambda e: e.copy(out=Ab[:], in_=Af[:]), R=[Af_b], W=[Ab_b])

        def rwkv_finish(N):
            nblk = N // 128
            ng = nblk * 8
            yv = y_tm[:, 0:nblk, :].rearrange("p b (h v) -> p (b h) v", v=64)
            ysq_v = z[:, 0:4, :].rearrange("p a t -> p (a t)")[:, 0:nblk * 512]
            sv = ysq_v.rearrange("p (g v) -> p g v", v=64)
            cx.op("dve", lambda e: e.reduce_sum(out=gst[:, 0, 0:ng], in_=yv, axis=AX.X), R=[ytm_b], W=[gst_b])
            cx.op("pool", lambda e: e.tensor_tensor(out=ysq_v, in0=y_tm[:, 0:nblk, :].rearrange("p b c -> p (b c)"), in1=y_tm[:, 0:nblk, :].rearrange("p b c -> p (b c)"), op=ALU.mult), R=[ytm_b], W=[ysq_b])
            cx.op("dve", lambda e: e.reduce_sum(out=gst[:, 1, 0:ng], in_=sv, axis=AX.X), R=[ysq_b], W=[gst_b])
            cx.op("dve", lambda e: e.tensor_scalar(out=gst[:, 0, 0:ng], in0=gst[:, 0, 0:ng], scalar1=1.0 / 64, scalar2=None, op0=ALU.mult), R=[gst_b], W=[gst_b])
            cx.op("dve", lambda e: e.tensor_tensor(out=gst[:, 2, 0:ng], in0=gst[:, 0, 0:ng], in1=gst[:, 0, 0:ng], op=ALU.mult), R=[gst_b], W=[gst_b])
            cx.op("dve", lambda e: e.scalar_tensor_tensor(out=gst[:, 1, 0:ng], in0=gst[:, 1, 0:ng], scalar=1.0 / 64, in1=gst[:, 2, 0:ng], op0=ALU.mult, op1=ALU.subtract),
                  R=[gst_b], W=[gst_b])
            cx.op("act", lambda e: e.activation(out=gst[:, 1, 0:ng], in_=gst[:, 1, 0:ng], func=AF.Sqrt, bias=gneps[:]), R=[gst_b, eps_b], W=[gst_b])
            cx.op("dve", lambda e: e.reciprocal(out=gst[:, 1, 0:ng], in_=gst[:, 1, 0:ng]), R=[gst_b], W=[gst_b])
            cx.op("dve", lambda e: e.tensor_tensor(out=yv, in0=yv, in1=gst[:, 0, 0:ng].unsqueeze(2).to_broadcast([128, ng, 64]), op=ALU.subtract), R=[ytm_b, gst_b], W=[ytm_b])
            cx.op("dve", lambda e: e.tensor_tensor(out=yv, in0=yv, in1=gst[:, 1, 0:ng].unsqueeze(2).to_broadcast([128, ng, 64]), op=ALU.mult), R=[ytm_b, gst_b], W=[ytm_b])
            for m in range(4):
                pb, pbb = next_bank()
                for b in range(nblk):
                    cx.op("pe", lambda e, b=b: e.transpose(pb[:, b * 128:(b + 1) * 128], y_tm[:, b, m * 128:(m + 1) * 128], ident[:]), R=[ytm_b, ident_b], W=[pbb], inc=(b == nblk - 1))
                cx.op("act", lambda e: e.activation(out=f1[:, 0:N], in_=pb[:, 0:N], func=AF.Identity, scale=C("ln_x_w")[:, m:m + 1], bias=C("ln_x_b")[:, m:m + 1]),
                      R=[pbb, Cb("ln_x_w"), Cb("ln_x_b")], W=[f_b[0]])
                cx.op("dve", lambda e: e.tensor_tensor(out=f2[:, 0:N], in0=bs_t[:, m, 0:N], in1=xs[:, 8 + m, 0:N], op=ALU.mult), R=[bs_b, xs_b], W=[f_b[1]])
                cx.op("dve", lambda e: e.tensor_tensor(out=f1[:, 0:N], in0=f1[:, 0:N], in1=f2[:, 0:N], op=ALU.add), R=[f_b[0], f_b[1]], W=[f_b[0]])
                cx.op("dve", lambda e: e.tensor_tensor(out=mixT[:, 4 + m, st["moff"]:st["moff"] + N], in0=f1[:, 0:N], in1=gT[:, m, 0:N], op=ALU.mult), R=[f_b[0], gT_b], W=[mixT_b])

        def wkv_out(dst):
            for m in range(4):
                pb, pbb = next_bank()
                cx.op("pe", lambda e: e.transpose(pb[0:64, 0:128], Af[:, m, :], ident[:]), R=[Af_b, ident_b], W=[pbb])
                cx.op("act", lambda e: e.copy(out=wkvT[:, 2 * m:2 * m + 2, :].rearrange("p h k -> p (h k)"), in_=pb[0:64, 0:128]), R=[pbb], W=[wkvT_b])
            cx.dma("sp", lambda e: e.dma_start(out=dst.rearrange("h v k -> v h k"), in_=wkvT), R=[wkvT_b])

        def out_proj(N):
            for m2 in range(4):
                wp, wpb = load_wpiece(I["w_out"], m2 * 256)
                for hf in range(2):
                    mo = m2 * 2 + hf
                    pb, pbb = next_bank()
                    for c in range(8):
                        cx.op("pe", lambda e, c=c: e.matmul(pb[:, 0:N], lhsT=wp[:, c, hf * 128:(hf + 1) * 128], rhs=mixT[:, c, 0:N], start=(c == 0), stop=(c == 7)),
                              R=[wpb, mixT_b], W=[pbb], inc=(c == 7))
                    cx.op("act", lambda e, mo=mo: e.copy(out=z[:, mo, 0:N], in_=pb[:, 0:N]), R=[pbb], W=[z_b])
            post_norm_add("n_mix_post", N, False)

        cx.dma("sp", lambda e: e.dma_start(out=diag64[:], in_=I["diag64"]), W=[sconst_b])
        cx.dma("sp", lambda e: e.dma_start(out=bones_f[:], in_=I["blockones"]), W=[sconst_b])
        cx.dma("sp", lambda e: e.dma_start(out=ones_f32[:], in_=I["ones"]), W=[sconst_b])
        cx.dma("sp", lambda e: e.dma_start(out=ptb[:], in_=I["pt_own"].broadcast_to([128, NS * NPG])), W=[idx_b])
        cx.op("pool", lambda e: e.iota(iota_c[:], pattern=[[0, 1]], base=0, channel_multiplier=1), W=[idx_b])
        cx.op("pool", lambda e: e.tensor_scalar(out=idx[:], in0=ptb[:], scalar1=128, scalar2=None, op0=ALU.mult), R=[idx_b], W=[idx_b])
        cx.op("pool", lambda e: e.tensor_tensor(out=idx[:], in0=idx[:], in1=iota_c[:].to_broadcast([128, NS * NPG]), op=ALU.add), R=[idx_b], W=[idx_b])

        def gather_page(n, pg):
            i = st.get("pg", 0) % NKR
            st["pg"] = st.get("pg", 0) + 1
            j = n * NPG + pg
            cx.dma("pool", lambda e: e.indirect_dma_start(out=kpg[i][:], out_offset=None, in_=I["cache_k"][:, :],
                                                         in_offset=bass.IndirectOffsetOnAxis(ap=idx[:, j:j + 1], axis=0)), R=[idx_b], W=[kpg_b[i]])
            cx.dma("pool", lambda e: e.indirect_dma_start(out=vpg[i][:], out_offset=None, in_=I["cache_v"][:, :],
                                                         in_offset=bass.IndirectOffsetOnAxis(ap=idx[:, j:j + 1], axis=0)), R=[idx_b], W=[vpg_b[i]])
            return kpg[i], kpg_b[i], vpg[i], vpg_b[i]

        def sample_attention():
            cx.op("act", lambda e: e.activation(out=qs_bf[:], in_=qk_tm[:, 0, 0:512], func=AF.Copy, scale=0.125), R=[qk_b], W=[qs_b])
            cx.op("dve", lambda e: e.tensor_copy(out=knew_bf[:], in_=qk_tm[:, 0, 512:1024]), R=[qk_b], W=[new_b])
            cx.op("dve", lambda e: e.tensor_copy(out=vnew_bf[:], in_=v_tm[:, 0, :]), R=[v_b], W=[new_b])
            po, pob = pbank[6], pbank_b[6]
            pl, plb = pbank[7], pbank_b[7]
            for n in range(NS):
                pq, pqb = next_bank()
                cx.op("dve", lambda e: e.tensor_scalar(out=sel_t[:], in0=ones_bf[:], scalar1=ident[:, n:n + 1], scalar2=None, op0=ALU.mult), R=[ones_b, ident_b], W=[sel_b])
                cx.op("pe", lambda e: e.matmul(pq[:, :], lhsT=sel_t[:], rhs=qs_bf[:], start=True, stop=True), R=[sel_b, qs_b], W=[pqb])
                pages = []
                for pg in range(NPG1):
                    if pg < NPG:
                        kt, ktb, vt, vtb = gather_page(n, pg)
                    else:
                        kt, ktb, vt, vtb = knew_bf, new_b, vnew_bf, new_b
                    pr, prb = prod[pg % 2], prod_b[pg % 2]
                    cx.op("dve", lambda e: e.tensor_tensor(out=pr, in0=kt[:], in1=pq[:, :], op=ALU.mult), R=[ktb, pqb], W=[prb])
                    cx.op("dve", lambda e: e.reduce_sum(out=sT[:, pg, :], in_=pr.rearrange("p (g d) -> p g d", d=64), axis=AX.X), R=[prb], W=[sT_b])
                    pages.append((vt, vtb))
                    if pg % 4 == 3 or pg == NPG1 - 1:
                        lo = (pg // 4) * 4
                        cx.op("act", lambda e: e.activation(out=pT[:, lo:pg + 1, :], in_=sT[:, lo:pg + 1, :], func=AF.Exp), R=[sT_b], W=[pT_b])
                        if pg == NPG1 - 1:
                            cx.op("dve", lambda e: e.tensor_scalar(out=pT[:, NPG, :], in0=pT[:, NPG, :], scalar1=ident[:, n:n + 1], scalar2=None, op0=ALU.mult),
                                  R=[pT_b, ident_b], W=[pT_b])
                        for p2 in range(lo, pg + 1):
                            vt2, vtb2 = pages[p2]
                            for h in range(4):
                                cx.op("pe", lambda e: e.matmul(po[:, n * 8 + 2 * h:n * 8 + 2 * h + 2], lhsT=vt2[:, h * 128:(h + 1) * 128], rhs=pT[:, p2, 2 * h:2 * h + 2],
                                                               start=(n == 0 and p2 == 0 and h == 0), stop=(p2 == NPG1 - 1), skip_group_check=True), R=[vtb2, pT_b], W=[pob], inc=True)
                cx.op("dve", lambda e: e.reduce_sum(out=psum8[:], in_=pT[:].rearrange("p g c -> p c g"), axis=AX.X), R=[pT_b], W=[psum8_b])
                cx.op("pe", lambda e: e.matmul(pl[:, n * 8:(n + 1) * 8], lhsT=ones_f32[:], rhs=psum8[:], start=True, stop=True), R=[sconst_b, psum8_b], W=[plb])
            cx.op("act", lambda e: e.copy(out=oS[:].rearrange("p n c -> p (n c)"), in_=po[:, 0:NS * 8]), R=[pob], W=[oS_b])
            cx.op("dve", lambda e: e.reciprocal(out=lS[:].rearrange("p n c -> p (n c)"), in_=pl[:, 0:NS * 8]), R=[plb], W=[lS_b])
            cx.op("dve", lambda e: e.tensor_tensor(out=oS[:], in0=oS[:], in1=lS[:], op=ALU.mult), R=[oS_b, lS_b], W=[oS_b])
            ov = oS[:].rearrange("p n (h j) -> p h n j", j=2)
            cx.op("dve", lambda e: e.scalar_tensor_tensor(out=yaS[:], in0=ov[:, :, :, 1], scalar=lam_t[:, 0:1], in1=ov[:, :, :, 0], op0=ALU.mult, op1=ALU.add),
                  R=[oS_b, lam_b], W=[yaS_b])
            cx.op("dve", lambda e: e.memset(mixT[:, :, :], 0.0), W=[mixT_b])
            for h in range(4):
                cx.op("dve", lambda e: e.memset(ya[:], 0.0), W=[ya_b])
                cx.op("dve", lambda e: e.tensor_copy(out=ya[:, 0:NS], in_=yaS[:, h, :]), R=[yaS_b], W=[ya_b])
                subln_norm(h, TT)

        def expand(nm, half):
            X, Xbuf = raw[nm]
            n0 = half * HN
            cx.op("dve", lambda e: e.tensor_tensor(out=Xe, in0=X[:, :, n0:n0 + HN].unsqueeze(3).to_broadcast([128, 4, HN, 64]),
                                                   in1=diag64[:].unsqueeze(1).unsqueeze(1).to_broadcast([128, 4, HN, 64]), op=ALU.mult),
                  R=[Xbuf, sconst_b], W=[Xe_b])
            for m in range(4):
                pb, pbb = next_bank()
                cx.op("pe", lambda e: e.matmul(pb[:, 0:HN * 64], lhsT=bones_f[:], rhs=Xe[:, m, :, :].rearrange("p n k -> p (n k)"), start=True, stop=True),
                      R=[sconst_b, Xe_b], W=[pbb])
                cx.op("act", lambda e: e.copy(out=Xb[:, :, m, :], in_=pb[:, 0:HN * 64].rearrange("p (n k) -> p n k", k=64)), R=[pbb], W=[Xb_b])

        def sample_wkv():
            for half in range(NS // HN):
                n0 = half * HN
                cx.dma("sp", lambda e: e.dma_start(out=S_t[:], in_=I["swkv"][n0:n0 + HN].rearrange("n (m hp) v k -> (hp v) n m k", hp=2)), W=[S_b])
                vT = xs[:, 8:12, n0:n0 + HN].rearrange("p m n -> p n m")
                expand("kkn", half)
                cx.op("dve", lambda e: e.tensor_tensor(out=T1, in0=S_t[:], in1=Xb[:], op=ALU.mult), R=[S_b, Xb_b], W=[T1_b])
                cx.op("dve", lambda e: e.reduce_sum(out=red[:], in_=T1, axis=AX.X), R=[T1_b], W=[red_b])
                expand("w", half)
                cx.op("dve", lambda e: e.tensor_tensor(out=S_t[:], in0=S_t[:], in1=Xb[:], op=ALU.mult), R=[S_b, Xb_b], W=[S_b])
                expand("b", half)
                cx.op("dve", lambda e: e.tensor_tensor(out=T1, in0=Xb[:], in1=red[:].unsqueeze(3).to_broadcast([128, HN, 4, 64]), op=ALU.mult), R=[Xb_b, red_b], W=[T1_b])
                cx.op("dve", lambda e: e.tensor_tensor(out=S_t[:], in0=S_t[:], in1=T1, op=ALU.subtract), R=[S_b, T1_b], W=[S_b])
                expand("kf", half)
                cx.op("dve", lambda e: e.tensor_tensor(out=T1, in0=Xb[:], in1=vT.unsqueeze(3).to_broadcast([128, HN, 4, 64]), op=ALU.mult), R=[Xb_b, xs_b], W=[T1_b])
                cx.op("dve", lambda e: e.tensor_tensor(out=S_t[:], in0=S_t[:], in1=T1, op=ALU.add), R=[S_b, T1_b], W=[S_b])
                cx.dma("sp", lambda e: e.dma_start(out=O["wkvs"][n0:n0 + HN].rearrange("n (m hp) v k -> (hp v) n m k", hp=2), in_=S_t[:]), R=[S_b])
                expand("r", half)
                cx.op("dve", lambda e: e.tensor_tensor(out=T1, in0=S_t[:], in1=Xb[:], op=ALU.mult), R=[S_b, Xb_b], W=[T1_b])
                cx.op("dve", lambda e: e.reduce_sum(out=yS[:, :, n0:n0 + HN].rearrange("p m n -> p n m"), in_=T1, axis=AX.X), R=[T1_b], W=[yS_b])
            yf = yS[:].rearrange("p m n -> p (m n)")
            NN = 4 * NS
            pm, pmb = next_bank()
            cx.op("pe", lambda e: e.matmul(pm[:, 0:NN], lhsT=bones_f[:], rhs=yf, start=True, stop=True), R=[sconst_b, yS_b], W=[pmb])
            cx.op("dve", lambda e: e.scalar_tensor_tensor(out=gS[0][:], in0=pm[:, 0:NN], scalar=-1.0 / 64, in1=yf, op0=ALU.mult, op1=ALU.add), R=[pmb, yS_b], W=[gS_b[0]])
            cx.op("dve", lambda e: e.tensor_tensor(out=gS[1][:], in0=gS[0][:], in1=gS[0][:], op=ALU.mult), R=[gS_b[0]], W=[gS_b[1]])
            pv, pvb = next_bank()
            cx.op("pe", lambda e: e.matmul(pv[:, 0:NN], lhsT=bones_f[:], rhs=gS[1][:], start=True, stop=True), R=[sconst_b, gS_b[1]], W=[pvb])
            cx.op("act", lambda e: e.activation(out=gS[2][:], in_=pv[:, 0:NN], func=AF.Sqrt, scale=1.0 / 64, bias=gneps[:]), R=[pvb, eps_b], W=[gS_b[2]])
            cx.op("dve", lambda e: e.reciprocal(out=gS[2][:], in_=gS[2][:]), R=[gS_b[2]], W=[gS_b[2]])
            cx.op("dve", lambda e: e.tensor_tensor(out=gS[0][:], in0=gS[0][:], in1=gS[2][:], op=ALU.mult), R=[gS_b[0], gS_b[2]], W=[gS_b[0]])
            for m in range(4):
                ynm = gS[0][:, m * NS:(m + 1) * NS]
                cx.op("act", lambda e: e.activation(out=f1[:, 0:NS], in_=ynm, func=AF.Identity, scale=C("ln_x_w")[:, m:m + 1], bias=C("ln_x_b")[:, m:m + 1]),
                      R=[gS_b[0], Cb("ln_x_w"), Cb("ln_x_b")], W=[f_b[0]])
                cx.op("dve", lambda e: e.tensor_tensor(out=f2[:, 0:NS], in0=bs_t[:, m, 0:NS], in1=xs[:, 8 + m, 0:NS], op=ALU.mult), R=[bs_b, xs_b], W=[f_b[1]])
                cx.op("dve", lambda e: e.tensor_tensor(out=f1[:, 0:NS], in0=f1[:, 0:NS], in1=f2[:, 0:NS], op=ALU.add), R=[f_b[0], f_b[1]], W=[f_b[0]])
                cx.op("dve", lambda e: e.tensor_tensor(out=mixT[:, 4 + m, 0:NS], in0=f1[:, 0:NS], in1=gT[:, m, 0:NS], op=ALU.mult), R=[f_b[0], gT_b], W=[mixT_b])

        def sample_tile():
            st["moff"] = 0
            load_x_tile(I["xs_pad"], 1)
            ffn(I["ffn1_gate"], I["ffn1_up"], I["ffn1_down"], "n_ffn1_pre", "n_ffn1_post", TT)
            win_stage(TT)
            sub_front(0, I["cosS"], I["sinS"])
            cx.dma("sp", lambda e: e.dma_start(out=O["ks_pad"], in_=qk_tm[:, 0, 512:1024]), R=[qk_b])
            cx.dma("sp", lambda e: e.dma_start(out=O["vs_pad"], in_=v_tm[:, 0, :]), R=[v_b])
            cx.dma("sp", lambda e: e.dma_start(out=O["shs"], in_=pbT[:, :, 1:TT + 1]), R=[pbT_b])
            cx.dma("sp", lambda e: e.dma_start(out=xs[:, :, 0:TT], in_=I["sshT"]), W=[xs_b])
            cx.op("dve", lambda e: e.tensor_tensor(out=xs[:, :, 0:TT], in0=xs[:, :, 0:TT], in1=pbT[:, :, 1:TT + 1], op=ALU.subtract), R=[pbT_b, xs_b], W=[xs_b])
            for j in range(14):
                cx.op("dve", lambda e, j=j: e.scalar_tensor_tensor(out=xs[:, j, 0:TT], in0=xs[:, j, 0:TT], scalar=mu_t[:, j:j + 1], in1=pbT[:, j, 1:TT + 1],
                                                                  op0=ALU.mult, op1=ALU.add), R=[xs_b, mu_b, pbT_b], W=[xs_b])
            sample_attention()
            rwkv_prep(TT, sample=True)
            sample_wkv()
            out_proj(TT)
            ffn(I["ffn2_gate"], I["ffn2_up"], I["ffn2_down"], "n_ffn2_pre", "n_ffn2_post", TT)
            store_x_tile(O["ys_pad"], 1)

        for tf in range(NTF):
            load_x_tile(I["xp"][tf * TF:(tf + 1) * TF, :], TF // 128)
            ffn(I["ffn1_gate"], I["ffn1_up"], I["ffn1_down"], "n_ffn1_pre", "n_ffn1_post", TF)
            win_stage(TF)
            for s in range(NSUB):
                t = tf * NSUB + s
                st["moff"] = s * TT
                nblk = sub_front(s, I["cosT"][t * TT:(t + 1) * TT, :], I["sinT"][t * TT:(t + 1) * TT, :])
                prompt_kv_out(t, nblk)
                shift_mix(TT)
                if t == NT - 1:
                    cx.op("dve", lambda e: e.tensor_copy(out=shc[:], in_=pbT[:, :, TT]), R=[pbT_b], W=[shc_b])
                    cx.dma("sp", lambda e: e.dma_start(out=O["shp"], in_=shc[:]), R=[shc_b])
                cx.op("dve", lambda e: e.tensor_copy(out=pbT[:, :, 0:1], in_=pbT[:, :, TT:TT + 1]), R=[pbT_b], W=[pbT_b])
                prep_head(TT)
                for h in range(4):
                    attention_head(t, h)
                    prep_m(TT, h)
                prep_tail(TT)
                for b in range(NBLK):
                    rwkv_block(b, stage)
                rwkv_finish(TT)
                if t == NT - 1:
                    wkv_out(O["wkvp"])
            out_proj(TF)
            ffn(I["ffn2_gate"], I["ffn2_up"], I["ffn2_down"], "n_ffn2_pre", "n_ffn2_post", TF)
            store_x_tile(O["yp"][tf * TF:(tf + 1) * TF, :], TF // 128)

        if NPOOL > 0:
            sample_tile()
        cx.wait_all_dma("sp")
    return nc


def host_consts(SEQ, pos0=0, past_len=2048):
    TT = 128
    pos = np.arange(SEQ, dtype=np.float32) + pos0
    inv = (500000.0 ** (-np.arange(8, dtype=np.float32) / 8)).astype(np.float32)
    ang = pos[:, None] * inv[None, :]
    cosT = np.tile(np.cos(ang).astype(np.float32), (1, 16))
    sinT = np.tile(np.sin(ang).astype(np.float32), (1, 16))
    kk = np.arange(128)[:, None, None] + 128 * np.arange(TT // 128)[None, :, None]
    qq = np.arange(TT)[None, None, :]
    cmask = (kk <= qq).astype(np.float32)
    resetm = np.ones((128, TT), np.float32); resetm[:, ::64] = 0.0
    i = np.arange(128)[:, None] % 64
    tcol = np.arange(128)[None, :]
    mA = np.where(tcol < 64, i < (tcol % 64), i <= (tcol % 64)).astype(np.float32)
    maskA4 = np.tile(mA, (1, 4))
    mC = (np.arange(64)[None, :] < i).astype(np.float32)
    maskC8 = np.tile(mC, (1, 8))
    I8 = np.tile((np.arange(64)[None, :] == i).astype(np.float32), (1, 8))
    blockones = (np.arange(128)[:, None] // 64 == np.arange(128)[None, :] // 64).astype(np.float32)
    angS = float(past_len) * inv
    cosS = np.tile(np.cos(angS).astype(np.float32)[None, :], (128, 16))
    sinS = np.tile(np.sin(angS).astype(np.float32)[None, :], (128, 16))
    diag64 = (np.arange(64)[None, :] == i).astype(np.float32)
    return {"ident": np.eye(128, dtype=np.float32), "ones": np.ones((128, 128), np.float32),
            "cosS": cosS, "sinS": sinS, "diag64": diag64,
            "cosT": cosT, "sinT": sinT, "cmask": cmask, "resetm": resetm, "maskA4": maskA4, "maskC8": maskC8,
            "I8": I8, "blockones": blockones}


def fm(vec, nchunk):
    return np.ascontiguousarray(np.asarray(vec, np.float32).reshape(nchunk, 128).T)


_SEQ = 4096


def kernel(x_prompt, x_sample, cache_k, cache_v, state_wkv, state_shift, page_table,
           n_ffn1_pre, n_ffn1_post, ffn1_gate, ffn1_up, ffn1_down,
           n_mix_pre, n_mix_post, w_in, w_out,
           lambda_q1, lambda_k1, lambda_q2, lambda_k2, subln,
           mu_shift, w0, w2, a0, a2, g2, k_k, k_a, r_k, ln_x_w, ln_x_b,
           n_ffn2_pre, n_ffn2_post, ffn2_gate, ffn2_up, ffn2_down):
    f = lambda a: np.ascontiguousarray(np.asarray(a, dtype=np.float32))
    x_prompt = f(x_prompt)
    B, SEQ = x_prompt.shape[0], x_prompt.shape[1]
    page_table = np.asarray(page_table).astype(np.int32)
    n_s, NPG = page_table.shape
    NPOOL = np.asarray(cache_k).shape[1]
    assert n_s == 8 * NS and B == 4
    nc = build_program(SEQ, NPOOL, NPG, debug=False)
    consts = host_consts(SEQ, past_len=NPG * PAGE)
    ck = f(cache_k[0]).reshape(NPOOL * PAGE, 512)
    cv_ = f(cache_v[0]).reshape(NPOOL * PAGE, 512)
    xs_all = f(x_sample)[:, 0, :]
    ss_all = f(state_shift[0])
    sw_all = f(state_wkv[0])
    shared = {
        "ffn1_gate": f(ffn1_gate[0]), "ffn1_up": f(ffn1_up[0]), "ffn1_down": f(ffn1_down[0]),
        "ffn2_gate": f(ffn2_gate[0]), "ffn2_up": f(ffn2_up[0]), "ffn2_down": f(ffn2_down[0]),
        "w_in": f(w_in[0]), "w_out": f(w_out[0]),
        "n_ffn1_pre": fm(n_ffn1_pre[0], 8), "n_ffn1_post": fm(n_ffn1_post[0], 8),
        "n_mix_pre": fm(n_mix_pre[0], 8), "n_mix_post": fm(n_mix_post[0], 8),
        "n_ffn2_pre": fm(n_ffn2_pre[0], 8), "n_ffn2_post": fm(n_ffn2_post[0], 8),
        "mu": fm(mu_shift[0], 14),
        "lamv": np.concatenate([f(lambda_q1[0]), f(lambda_k1[0]), f(lambda_q2[0]), f(lambda_k2[0])]).reshape(1, 256),
        "subln": f(subln[0]).reshape(128, 1),
        "w0": fm(w0[0], 4), "a0": fm(a0[0], 4), "k_k": fm(k_k[0], 4), "k_a": fm(k_a[0], 4),
        "r_k": fm(np.asarray(r_k[0]).reshape(-1), 4), "ln_x_w": fm(ln_x_w[0], 4), "ln_x_b": fm(ln_x_b[0], 4),
        "wa2": np.ascontiguousarray(np.concatenate([f(w2[0]), f(a2[0])], axis=0)), "g2": f(g2[0]),
        "cache_k": ck, "cache_v": cv_,
        **consts,
    }
    in_maps = []
    for c in range(8):
        sl = slice(NS * c, NS * (c + 1))
        xs_pad = np.zeros((128, D), np.float32)
        xs_pad[:NS] = xs_all[sl]
        sshT = np.zeros((128, 14, 128), np.float32)
        sshT[:, :, :NS] = ss_all[sl].reshape(NS, 14, 128).transpose(2, 1, 0)
        in_maps.append(dict(shared, xp=x_prompt[c // 2], xs_pad=xs_pad, sshT=sshT,
                            pt_own=np.ascontiguousarray(page_table[sl].reshape(1, NS * NPG)),
                            swkv=np.ascontiguousarray(sw_all[sl])))
    res = run_bass_kernel_spmd(nc, in_maps, core_ids=list(range(8)))
    R_ = res.results
    r = [R_[2 * s] for s in range(B)]
    yp = np.stack([q["yp"] for q in r]).astype(np.float32)
    kp = np.stack([q["kp"].reshape(SEQ, 4, 128) for q in r])[None].astype(np.float32)
    vp = np.stack([q["vp"].reshape(SEQ, 4, 128) for q in r])[None].astype(np.float32)
    wkvp = np.stack([q["wkvp"] for q in r])[None].astype(np.float32)
    shp = np.stack([q["shp"].T.reshape(-1) for q in r])[None].astype(np.float32)
    ys = np.concatenate([R_[c]["ys_pad"][:NS] for c in range(8)])[:, None, :].astype(np.float32)
    ks = np.concatenate([R_[c]["ks_pad"][:NS] for c in range(8)]).reshape(1, n_s, 1, 4, 128).astype(np.float32)
    vs = np.concatenate([R_[c]["vs_pad"][:NS] for c in range(8)]).reshape(1, n_s, 1, 4, 128).astype(np.float32)
    wkvs = np.concatenate([R_[c]["wkvs"] for c in range(8)])[None].astype(np.float32)
    shs = np.concatenate([R_[c]["shs"][:, :, :NS].transpose(2, 1, 0).reshape(NS, SHIFT) for c in range(8)])[None].astype(np.float32)
    return (yp, ys, kp, vp, wkvp, shp, ks, vs, wkvs, shs)
```

```python
import numpy as np
import concourse.bass as bass
import concourse.mybir as mybir
from concourse.bass_utils import run_bass_kernel_spmd

F32 = mybir.dt.float32
BF16 = mybir.dt.bfloat16
I32 = mybir.dt.int32
AF = mybir.ActivationFunctionType
ALU = mybir.AluOpType
AX = mybir.AxisListType

D = 1024
DFF = 2816
NFF = DFF // 128
H_A, DH_A = 4, 64
D_A = 512
H_B, DH_B = 8, 64
D_B = 512
SHIFT = 1792
D_IN = 3 * D_A + SHIFT
PAGE = 128
NORM_EPS = 1e-6
GN_EPS = 64e-5
LAM_INIT = 0.8 - 0.6 * 1.0
NS = 16
CH = 64


class Buf:
    __slots__ = ("name", "w", "r")

    def __init__(self, name):
        self.name = name
        self.w = None
        self.r = {}


class Eng:
    def __init__(self, name, eng, sem, is_pe=False, is_dma=False):
        self.name, self.eng, self.sem = name, eng, sem
        self.cnt = 0
        self.known = {}
        self.is_pe = is_pe
        self.is_dma = is_dma


class Ctx:
    def __init__(self, nc, sems, dma_sems):
        self.nc = nc
        self.semobj = {}
        self.E = {}
        for nm, eng, pe in (("pe", None, True), ("act", None, False), ("dve", None, False), ("pool", None, False), ("sp", None, False)):
            self.semobj[nm] = sems[nm]
            self.E[nm] = Eng(nm, None, sems[nm], is_pe=pe)
        self.dma_pool = dma_sems
        self.dma_i = {q: 0 for q in dma_sems}
        for q, lst in dma_sems.items():
            for i, s in enumerate(lst):
                self.semobj[(q, i)] = s

    def bind(self, name, eng):
        self.E[name].eng = eng

    def _waits(self, E, R, W):
        need = {}
        for b in R:
            if b.w is not None:
                k, v = b.w
                need[k] = max(need.get(k, 0), v)
        for b in W:
            if b.w is not None:
                k, v = b.w
                need[k] = max(need.get(k, 0), v)
            for k, v in b.r.items():
                need[k] = max(need.get(k, 0), v)
        for k, v in need.items():
            if E.known.get(k, 0) >= v:
                continue
            if E.is_pe and k == "pe":
                continue
            E.eng.wait_ge(self.semobj[k], v)
            E.known[k] = v

    def op(self, en, fn, R=(), W=(), inc=True):
        E = self.E[en]
        self._waits(E, R, W)
        ins = fn(E.eng)
        if inc:
            ins.then_inc(E.sem, 1)
            E.cnt += 1
            ev = (en, E.cnt)
        else:
            ev = (en, E.cnt + 1)
        for b in R:
            b.r[ev[0]] = max(b.r.get(ev[0], 0), ev[1])
        for b in W:
            b.w = ev
            b.r = {}
        return ins

    def dma(self, q, fn, R=(), W=()):
        E = self.E[q]
        pool = self.dma_pool[q]
        i = self.dma_i[q]
        self.dma_i[q] += 1
        slot = i % len(pool)
        val = 16 * (i // len(pool) + 1)
        key = (q, slot)
        if val > 16 and E.known.get(key, 0) < val - 16:
            E.eng.wait_ge(pool[slot], val - 16)
            E.known[key] = val - 16
        self._waits(E, R, W)
        ins = fn(E.eng)
        ins.then_inc(pool[slot], 16)
        ev = (key, val)
        for b in R:
            b.r[key] = max(b.r.get(key, 0), val)
        for b in W:
            b.w = ev
            b.r = {}
        return ev

    def wait_all_dma(self, q_wait="sp"):
        E = self.E[q_wait]
        for q, pool in self.dma_pool.items():
            n = self.dma_i[q]
            for slot in range(len(pool)):
                cnt = (n - slot + len(pool) - 1) // len(pool) if n > slot else 0
                if cnt > 0 and E.known.get((q, slot), 0) < 16 * cnt:
                    E.eng.wait_ge(pool[slot], 16 * cnt)

    def barrier(self):
        names = ["pe", "act", "dve", "pool"]
        for a in names:
            Ea = self.E[a]
            for b in names:
                if a == b:
                    continue
                v = self.E[b].cnt
                if v > 0 and Ea.known.get(b, 0) < v:
                    Ea.eng.wait_ge(self.semobj[b], v)
                    Ea.known[b] = v


def build_program(SEQ, NPOOL, NPAGES, debug=False, stage=9):
    TT = 128
    TF = min(512, SEQ)
    NSUB = TF // TT
    NTF = SEQ // TF
    NT = SEQ // TT
    NBLK = TT // 128
    nc = bass.Bass("TRN2", target_bir_lowering=False)
    qkv_s = nc.dram_tensor("qkv_s", [TF, 1536], F32).ap()
    pb_s = nc.dram_tensor("pb_s", [128, 14, TF], F32).ap()
    KT_s = nc.dram_tensor("KT_s", [4, 128, SEQ], BF16).ap()
    V_s = nc.dram_tensor("V_s", [4, 128, SEQ // 128, 128], BF16).ap()
    dt_in = lambda name, shape, dt=F32: nc.dram_tensor(name, list(shape), dt, kind="ExternalInput").ap()
    dt_out = lambda name, shape, dt=F32: nc.dram_tensor(name, list(shape), dt, kind="ExternalOutput").ap()

    I = {}
    I["xp"] = dt_in("xp", [SEQ, D])
    for nm in ("ffn1_gate", "ffn1_up", "ffn2_gate", "ffn2_up"):
        I[nm] = dt_in(nm, [D, DFF])
    for nm in ("ffn1_down", "ffn2_down"):
        I[nm] = dt_in(nm, [DFF, D])
    I["w_in"] = dt_in("w_in", [D, D_IN])
    I["w_out"] = dt_in("w_out", [D, D])
    for nm in ("n_ffn1_pre", "n_ffn1_post", "n_mix_pre", "n_mix_post", "n_ffn2_pre", "n_ffn2_post"):
        I[nm] = dt_in(nm, [128, 8])
    I["ident"] = dt_in("ident", [128, 128])
    I["ones"] = dt_in("ones", [128, 128])

    I["cosT"] = dt_in("cosT", [SEQ, 128])
    I["sinT"] = dt_in("sinT", [SEQ, 128])
    I["mu"] = dt_in("mu", [128, 14])
    I["cmask"] = dt_in("cmask", [128, TT // 128, TT])
    I["lamv"] = dt_in("lamv", [1, 4 * 64])
    I["subln"] = dt_in("subln", [128, 1])
    for nm in ("w0", "a0", "k_k", "k_a", "r_k", "ln_x_w", "ln_x_b"):
        I[nm] = dt_in(nm, [128, 4])
    I["wa2"] = dt_in("wa2", [128, 512])
    I["g2"] = dt_in("g2", [128, 512])
    for nm in ("resetm", "maskA4", "maskC8", "I8", "blockones"):
        I[nm] = dt_in(nm, {"resetm": [128, TT], "maskA4": [128, 512], "maskC8": [128, 512], "I8": [128, 512], "blockones": [128, 128]}[nm])

    NPG = NPAGES
    I["xs_pad"] = dt_in("xs_pad", [128, D])
    I["sshT"] = dt_in("sshT", [128, 14, 128])
    I["pt_own"] = dt_in("pt_own", [1, NS * NPG], I32)
    I["cache_k"] = dt_in("cache_k", [NPOOL * 128, 512])
    I["cache_v"] = dt_in("cache_v", [NPOOL * 128, 512])
    I["swkv"] = dt_in("swkv", [NS, 8, 64, 64])
    I["cosS"] = dt_in("cosS", [128, 128])
    I["sinS"] = dt_in("sinS", [128, 128])
    I["diag64"] = dt_in("diag64", [128, 64])

    O = {}
    O["ys_pad"] = dt_out("ys_pad", [128, D])
    O["ks_pad"] = dt_out("ks_pad", [128, 512])
    O["vs_pad"] = dt_out("vs_pad", [128, 512])
    O["shs"] = dt_out("shs", [128, 14, 128])
    O["wkvs"] = dt_out("wkvs", [NS, 8, 64, 64])
    O["yp"] = dt_out("yp", [SEQ, D])
    O["kp"] = dt_out("kp", [SEQ, 512])
    O["vp"] = dt_out("vp", [SEQ, 512])
    O["shp"] = dt_out("shp", [128, 14])
    O["wkvp"] = dt_out("wkvp", [8, 64, 64])

    from contextlib import ExitStack
    es = ExitStack()
    with es:
        sem = {nm: es.enter_context(nc.semaphore("s_" + nm)) for nm in ("pe", "act", "dve", "pool", "sp")}
        dsem = {q: [es.enter_context(nc.semaphore(f"d_{q}{i}")) for i in range(n)] for q, n in (("sp", 12), ("pool", 12))}
        sb = lambda name, shape, dt=F32: es.enter_context(nc.sbuf_tensor(name, list(shape), dt))
        ps = lambda name, shape, dt=F32: es.enter_context(nc.psum_tensor(name, list(shape), dt))

        ident = sb("ident_t", [128, 128]); ident_b = Buf("ident")
        ones_bf = sb("ones_bf", [128, 128], BF16); ones_b = Buf("ones")
        gains = {nm: (sb("g_" + nm, [128, 8]), Buf(nm)) for nm in ("n_ffn1_pre", "n_ffn1_post", "n_mix_pre", "n_mix_post", "n_ffn2_pre", "n_ffn2_post")}
        eps_t = sb("eps_t", [128, 1]); eps_b = Buf("eps")
        xT = sb("xT", [128, 8, TF]); xT_b = Buf("xT")
        xtok = sb("xtok", [128, NBLK, D]); xtok_b = Buf("xtok")
        z = sb("z", [128, 8, TF]); z_b = Buf("z")
        u = sb("u", [128, 8, TF], BF16); u_b = Buf("u")
        hid = sb("hid", [128, NFF, TF], BF16); hid_b = [Buf(f"hid{f}") for f in range(NFF)]
        sq = hid[:, 0:8, :]; sq_b = Buf("sq")
        rstd = sb("rstd", [128, TF]); rstd_b = Buf("rstd")
        sg = [sb(f"sg{i}", [128, TF], BF16) for i in range(2)]; sg_b = [Buf(f"sg{i}") for i in range(2)]
        NW = 4
        wring = [sb(f"wr{i}", [128, 8, 256], BF16) for i in range(NW)]; wring_b = [Buf(f"wr{i}") for i in range(NW)]
        ND = 2
        dring = [sb(f"dr{i}", [128, NFF, 128], BF16) for i in range(ND)]; dring_b = [Buf(f"dr{i}") for i in range(ND)]
        pbank = [ps(f"pb{i}", [128, 512]) for i in range(8)]; pbank_b = [Buf(f"pb{i}") for i in range(8)]

        NKB = SEQ // 128
        qk_tm = xtok; qk_b = xtok_b
        v_tm = sb("v_tm", [128, NBLK, 512]); v_b = Buf("v_tm")
        cos_t = sb("cos_t", [128, NBLK, 128]); sin_t = sb("sin_t", [128, NBLK, 128]); cs_b = Buf("cs")
        rt = [sb(f"rt{i}", [128, 16, 8]) for i in range(4)]; rt_b = [Buf(f"rt{i}") for i in range(4)]
        KC = 1024
        kring = [sb(f"kring{i}", [128, KC], BF16) for i in range(2)]; kring_b = [Buf(f"kring{i}") for i in range(2)]
        vring = [sb(f"vring{i}", [128, KC // 128, 128], BF16) for i in range(2)]; vring_b = [Buf(f"vring{i}") for i in range(2)]
        ktn = sb("ktn", [128, 4, 128], BF16); ktn_b = Buf("ktn")
        vbn = sb("vbn", [128, 4, 128], BF16); vbn_b = Buf("vbn")
        stg = [sb(f"stg{i}", [128, 256]) for i in range(3)]; stg_b = [Buf(f"stg{i}") for i in range(3)]
        stg2 = [sb("stg20", [128, TF])] * 2; stg2_b = [Buf("stg20")] * 2
        qkv_sb = Buf("qkv_s"); pb_sb = Buf("pb_s"); KTs_b = Buf("KT_s"); Vs_b = Buf("V_s")
        QT = sb("QT", [128, 4, TT], BF16); QT_b = Buf("QT")
        pbT = sb("pbT", [128, 14, TT + 1]); pbT_b = Buf("pbT")
        xs = sb("xs", [128, 14, TT]); xs_b = Buf("xs")
        mu_t = sb("mu_t", [128, 14]); mu_b = Buf("mu")
        cmask = sb("cmask_t", [128, NBLK, TT], BF16); cmask_b = Buf("cmask")
        PT = [sb(f"PT{i}", [128, 4 * TT], BF16) for i in range(2)]; PT_b = [Buf(f"PT{i}") for i in range(2)]
        o0 = sb("o0", [128, TT]); o0_b = Buf("o0")
        rl = sb("rl", [128, TT]); rl_b = Buf("rl")
        ya = sb("ya", [128, TT]); ya_b = Buf("ya")
        mixT = hid[:, 8:16, :]; mixT_b = Buf("mixT")
        lamrow = sb("lamrow", [1, 4 * 64]); lamrow_b = Buf("lamrow")
        lam1 = sb("lam1", [1, 4]); lam1_b = Buf("lam1")
        ones_f = sb("ones_f", [1, 128]); onesf_b = Buf("ones_f")
        lam_t = sb("lam_t", [128, 1]); lam_b = Buf("lam_t")
        subln_t = sb("subln_t", [128, 1]); subln_b = Buf("subln")
        eps128 = sb("eps128", [128, 1])
        shc = sb("shc", [128, 14]); shc_b = Buf("shc")
        NCH = TT // 64
        cv = {nm: (sb("c_" + nm, [128, 4]), Buf("c_" + nm)) for nm in ("w0", "a0", "k_k", "k_a", "r_k", "ln_x_w", "ln_x_b", "nw0", "omka", "na0")}
        wa2 = sb("wa2_t", [128, 512], BF16); wa2_b = Buf("wa2")
        g2t = sb("g2_t", [128, 512], BF16); g2_b = Buf("g2")
        resetm = sb("resetm_t", [128, TT]); maskA4 = sb("maskA4_t", [128, 512]); maskC8 = sb("maskC8_t", [128, 512])
        I8 = sb("I8_t", [128, 512]); bones = sb("bones_t", [128, 128], BF16); cst_b = Buf("scan_consts")
        lora_in = sb("lora_in", [128, TT], BF16); lora_b = Buf("lora_in")
        sgx = sb("sgx", [128, TT], BF16); sgx_b = Buf("sgx")
        ld = sb("ld", [128, 4, TT]); ld_b = Buf("ld")
        a_t = sb("a_t", [128, 4, TT]); a_b = Buf("a_t")
        gT = sb("gT", [128, 4, TT]); gT_b = Buf("gT")
        bs_t = sb("bs_t", [128, 4, TT]); bs_b = Buf("bs_t")
        f1 = sb("f1", [128, TT]); f2 = sb("f2", [128, TT]); f3 = sb("f3", [128, TT]); f4 = sb("f4", [128, TT]); f5 = sb("f5", [128, TT]); f6 = sb("f6", [128, TT])
        f_b = [Buf(f"f{i}") for i in range(6)]
        fbf = sb("fbf", [128, TT], BF16); fbf_b = Buf("fbf")
        gC = sb("gC", [128, 4, NCH]); gC_b = Buf("gC")
        kr = sb("kr", [128, 4, NCH, 128], BF16); kr_b = Buf("kr")
        kb_ = sb("kb_", [128, 4, NCH, 128], BF16); kb_b = Buf("kb")
        khat = z[:, 4:8, :]; khat_b = z_b
        bhat = v_tm[:, 0, :].rearrange("p (a b) -> p a b", a=4); bhat_b = v_b
        V_tm = sb("V_tm", [128, NBLK, 512], BF16); Vtm_b = Buf("V_tm")
        Kh_tm = sb("Kh_tm", [128, NBLK, 512], BF16); Kh_b = Buf("Kh_tm")
        Bh_tm = sb("Bh_tm", [128, NBLK, 512], BF16); Bh_b = Buf("Bh_tm")
        Am = sb("Am", [128, 8, 128], BF16); Am_b = Buf("Am")
        Bm = sb("Bm", [128, 8, 128], BF16); Bm_b = Buf("Bm")
        Zb = [sb(f"Zb{i}", [128, 8, 64], BF16) for i in range(2)]; Zb_b = [Buf(f"Zb{i}") for i in range(2)]
        ZTb = [sb(f"ZTb{i}", [128, 8, 64], BF16) for i in range(2)]; ZTb_b = [Buf(f"ZTb{i}") for i in range(2)]
        Wf = sb("Wf", [128, 8, 64]); Wf_b = Buf("Wf")
        Wb = sb("Wb", [128, 8, 64], BF16); Wb_b = Buf("Wb")
        Rb = sb("Rb", [128, 8, 64], BF16); Rb_b = Buf("Rb")
        Ub = sb("Ub", [128, 8, 64], BF16); Ub_b = Buf("Ub")
        Af = sb("Af", [128, 4, 64]); Af_b = Buf("Af")
        Ab = sb("Ab", [128, 4, 64], BF16); Ab_b = Buf("Ab")
        y_tm = sb("y_tm", [128, NBLK, 512]); ytm_b = Buf("y_tm")
        ysq = z[:, 0:4, :].rearrange("p a (b c) -> p b (a c)", b=NBLK) if False else None; ysq_b = z_b
        gst = sb("gst", [128, 4, NBLK * 8]); gst_b = Buf("gst")
        gneps = sb("gneps", [128, 1])
        wkvT = y_tm[0:64, 0, :].rearrange("p (h k) -> p h k", k=64); wkvT_b = ytm_b

        NPG1 = NPG + 1
        sel_t = sb("sel_t", [128, 128], BF16); sel_b = Buf("sel")
        diag64 = sb("diag64_t", [128, 64]); bones_f = sb("bones_f", [128, 128]); sconst_b = Buf("sconst")
        ptb = sb("ptb", [128, NS * NPG], I32); idx = sb("idx", [128, NS * NPG], I32); iota_c = sb("iota_c", [128, 1], I32); idx_b = Buf("idx")
        NKR = 4
        kpg = [sb(f"kpg{i}", [128, 512], BF16) for i in range(NKR)]; kpg_b = [Buf(f"kpg{i}") for i in range(NKR)]
        vpg = [sb(f"vpg{i}", [128, 512], BF16) for i in range(NKR)]; vpg_b = [Buf(f"vpg{i}") for i in range(NKR)]
        qs_bf = sb("qs_bf", [128, 512], BF16); qs_b = Buf("qs_bf")
        knew_bf = sb("knew_bf", [128, 512], BF16); vnew_bf = sb("vnew_bf", [128, 512], BF16); new_b = Buf("newkv")
        prod = [ld[:].rearrange("p a b -> p (a b)"), a_t[:].rearrange("p a b -> p (a b)")]; prod_b = [ld_b, a_b]
        sT = sb("sT", [128, NPG1, 8]); sT_b = Buf("sT")
        pT = sb("pT", [128, NPG1, 8], BF16); pT_b = Buf("pT")
        psum8 = sb("psum8", [128, 8]); psum8_b = Buf("psum8")
        ones_f32 = sb("ones_f32", [128, 128])
        oS = sb("oS", [128, NS, 8]); oS_b = Buf("oS")
        lS = sb("lS", [128, NS, 8]); lS_b = Buf("lS")
        yaS = sb("yaS", [128, 4, NS]); yaS_b = Buf("yaS")
        raw = {nm: (sb("raw_" + nm, [128, 4, NS]), Buf("raw_" + nm)) for nm in ("kkn", "kf", "b", "w", "r")}
        HN = NS // 8
        S_t = sb("S_t", [128, HN, 4, 64]); S_b = Buf("S_t")
        Xb = sb("Xb", [128, HN, 4, 64]); Xb_b = Buf("Xb")
        Xe = xtok[:, 0, 0:4 * HN * 64].rearrange("p (m n k) -> p m n k", m=4, k=64); Xe_b = xtok_b
        T1 = z[:].rearrange("p a b -> p (a b)")[:, 0:HN * 256].rearrange("p (n m k) -> p n m k", m=4, k=64); T1_b = z_b
        red = sb("red", [128, HN, 4]); red_b = Buf("red")
        yS = sb("yS", [128, 4, NS]); yS_b = Buf("yS")
        gS = [sb(f"gS{i}", [128, 4 * NS]) for i in range(3)]; gS_b = [Buf(f"gS{i}") for i in range(3)]

        blk = es.enter_context(nc.Block())
        cx = Ctx(nc, sem, dsem)
        cx.bind("pe", nc.tensor); cx.bind("act", nc.scalar); cx.bind("dve", nc.vector)
        cx.bind("pool", nc.gpsimd); cx.bind("sp", nc.sync)
        st = {"w": 0, "d": 0, "pb": 0}

        cx.dma("sp", lambda e: e.dma_start(out=ident[:], in_=I["ident"]), W=[ident_b])
        cx.dma("pool", lambda e: e.dma_start(out=ones_bf[:], in_=I["ones"]), W=[ones_b])
        for nm, (t, b) in gains.items():
            cx.dma("sp", lambda e, t=t, nm=nm: e.dma_start(out=t[:], in_=I[nm]), W=[b])
        cx.op("dve", lambda e: e.memset(eps_t[:], NORM_EPS), W=[eps_b])

        def load_wpiece(W, col0, ncols=256, row0=0):
            i = st["w"] % NW
            st["w"] += 1
            src = W[row0:row0 + 1024, col0:col0 + ncols].rearrange("(c p) n -> p c n", p=128)
            cx.dma("pool", lambda e: e.dma_start(out=wring[i][:, :, 0:ncols], in_=src), W=[wring_b[i]])
            return wring[i], wring_b[i]

        def load_dpiece(W, col0):
            i = st["d"] % ND
            st["d"] += 1
            src = W[:, col0:col0 + 128].rearrange("(c p) n -> p c n", p=128)
            cx.dma("pool", lambda e: e.dma_start(out=dring[i][:], in_=src), W=[dring_b[i]])
            return dring[i], dring_b[i]

        def next_bank():
            i = st["pb"] % 6
            st["pb"] += 1
            return pbank[i], pbank_b[i]

        def rms_stats(src, src_b, N):
            cx.op("act", lambda e: e.activation(out=sq[:, :, 0:N], in_=src[:, :, 0:N], func=AF.Square), R=[src_b], W=[sq_b])
            pb, pbb = next_bank()
            for c in range(8):
                cx.op("pe", lambda e, c=c: e.matmul(pb[:, 0:N], lhsT=ones_bf[:], rhs=sq[:, c, 0:N], start=(c == 0), stop=(c == 7)),
                      R=[ones_b, sq_b], W=[pbb], inc=(c == 7))
            cx.op("act", lambda e: e.activation(out=rstd[:, 0:N], in_=pb[:, 0:N], func=AF.Ln, scale=1.0 / D, bias=eps_t[:]),
                  R=[pbb, eps_b], W=[rstd_b])
            cx.op("act", lambda e: e.activation(out=rstd[:, 0:N], in_=rstd[:, 0:N], func=AF.Exp, scale=-0.5), R=[rstd_b], W=[rstd_b])

        def pre_norm(src, src_b, gname, N):
            rms_stats(src, src_b, N)
            g, gb = gains[gname]
            for c in range(8):
                cx.op("dve", lambda e, c=c: e.scalar_tensor_tensor(out=u[:, c, 0:N], in0=src[:, c, 0:N], scalar=g[:, c:c + 1],
                                                                  in1=rstd[:, 0:N], op0=ALU.mult, op1=ALU.mult),
                      R=[src_b, gb, rstd_b], W=[u_b])

        def post_norm_add(gname, N, half):
            rms_stats(z, z_b, N)
            g, gb = gains[gname]
            for c in range(8):
                cx.op("dve", lambda e, c=c: e.scalar_tensor_tensor(out=z[:, c, 0:N], in0=z[:, c, 0:N], scalar=g[:, c:c + 1],
                                                                  in1=rstd[:, 0:N], op0=ALU.mult, op1=ALU.mult),
                      R=[z_b, gb, rstd_b], W=[z_b])
            cx.op("dve", lambda e: e.scalar_tensor_tensor(out=xT[:, :, 0:N], in0=z[:, :, 0:N], scalar=(0.5 if half else 1.0),
                                                          in1=xT[:, :, 0:N], op0=ALU.mult, op1=ALU.add),
                  R=[z_b, xT_b], W=[xT_b])

        def ffn(Wg, Wu, Wd, gpre, gpost, N):
            pre_norm(xT, xT_b, gpre, N)
            for f2 in range(NFF // 2):
                wg, wgb = load_wpiece(Wg, f2 * 256)
                wu, wub = load_wpiece(Wu, f2 * 256)
                for ff in range(2):
                    f = f2 * 2 + ff
                    pg, pgb = next_bank()
                    pu, pub = next_bank()
                    for c in range(8):
                        cx.op("pe", lambda e, c=c: e.matmul(pg[:, 0:N], lhsT=wg[:, c, ff * 128:(ff + 1) * 128], rhs=u[:, c, 0:N],
                                                            start=(c == 0), stop=(c == 7)), R=[wgb, u_b], W=[pgb], inc=(c == 7))
                    for c in range(8):
                        cx.op("pe", lambda e, c=c: e.matmul(pu[:, 0:N], lhsT=wu[:, c, ff * 128:(ff + 1) * 128], rhs=u[:, c, 0:N],
                                                            start=(c == 0), stop=(c == 7)), R=[wub, u_b], W=[pub], inc=(c == 7))
                    s, sbb = sg[f % 2], sg_b[f % 2]
                    cx.op("act", lambda e: e.activation(out=s[:, 0:N], in_=pg[:, 0:N], func=AF.Silu), R=[pgb], W=[sbb])
                    cx.op("dve", lambda e, f=f: e.tensor_tensor(out=hid[:, f, 0:N], in0=s[:, 0:N], in1=pu[:, 0:N], op=ALU.mult),
                          R=[sbb, pub], W=[hid_b[f]] + ([sq_b] if f < 8 else []) + ([mixT_b] if 8 <= f < 16 else []))
            for m in range(8):
                wd, wdb = load_dpiece(Wd, m * 128)
                pz, pzb = next_bank()
                for f in range(NFF):
                    cx.op("pe", lambda e, f=f: e.matmul(pz[:, 0:N], lhsT=wd[:, f, :], rhs=hid[:, f, 0:N], start=(f == 0), stop=(f == NFF - 1)),
                          R=[wdb, hid_b[f]] + ([sq_b] if f < 8 else []) + ([mixT_b] if 8 <= f < 16 else []), W=[pzb], inc=(f == NFF - 1))
                cx.op("act", lambda e, m=m: e.copy(out=z[:, m, 0:N], in_=pz[:, 0:N]), R=[pzb], W=[z_b])
            post_norm_add(gpost, N, True)

        def load_x_tile(src_rows, nblk):
            for b in range(nblk):
                cx.dma("sp", lambda e: e.dma_start(out=xtok[:, 0, :], in_=src_rows[b * 128:(b + 1) * 128, :]), W=[xtok_b])
                for c2 in range(2):
                    pb, pbb = next_bank()
                    for cc in range(4):
                        c = c2 * 4 + cc
                        cx.op("pe", lambda e, c=c, cc=cc: e.transpose(pb[:, cc * 128:(cc + 1) * 128], xtok[:, 0, c * 128:(c + 1) * 128], ident[:]),
                              R=[xtok_b, ident_b], W=[pbb], inc=(cc == 3))
                    cx.op("act", lambda e: e.copy(out=xT[:, c2 * 4:(c2 + 1) * 4, b * 128:(b + 1) * 128], in_=pb[:, :].rearrange("p (c t) -> p c t", t=128)), R=[pbb], W=[xT_b])

        def store_x_tile(dst_rows, nblk):
            for b in range(nblk):
                for c2 in range(2):
                    pb, pbb = next_bank()
                    for cc in range(4):
                        c = c2 * 4 + cc
                        cx.op("pe", lambda e, c=c, cc=cc: e.transpose(pb[:, cc * 128:(cc + 1) * 128], xT[:, c, b * 128:(b + 1) * 128], ident[:]),
                              R=[xT_b, ident_b], W=[pbb], inc=(cc == 3))
                    cx.op("act", lambda e: e.copy(out=xtok[:, 0, c2 * 512:(c2 + 1) * 512], in_=pb[:, :]), R=[pbb], W=[xtok_b])
                cx.dma("sp", lambda e: e.dma_start(out=dst_rows[b * 128:(b + 1) * 128, :], in_=xtok[:, 0, :]), R=[xtok_b])

        cx.dma("sp", lambda e: e.dma_start(out=mu_t[:], in_=I["mu"]), W=[mu_b])
        cx.dma("pool", lambda e: e.dma_start(out=cmask[:], in_=I["cmask"]), W=[cmask_b])
        cx.dma("sp", lambda e: e.dma_start(out=lamrow[:], in_=I["lamv"]), W=[lamrow_b])
        cx.dma("sp", lambda e: e.dma_start(out=subln_t[:], in_=I["subln"]), W=[subln_b])
        cx.op("dve", lambda e: e.memset(ones_f[:], 1.0), W=[onesf_b])
        cx.op("dve", lambda e: e.memset(eps128[:], NORM_EPS), W=[eps_b])
        cx.op("dve", lambda e: e.memset(pbT[:, :, 0:1], 0.0), W=[pbT_b])
        lr = lamrow[:].rearrange("p (a d) -> p a d", d=64)
        cx.op("dve", lambda e: e.tensor_tensor(out=lr[:, 0:1, :], in0=lr[:, 0:1, :], in1=lr[:, 1:2, :], op=ALU.mult), R=[lamrow_b], W=[lamrow_b])
        cx.op("dve", lambda e: e.tensor_tensor(out=lr[:, 2:3, :], in0=lr[:, 2:3, :], in1=lr[:, 3:4, :], op=ALU.mult), R=[lamrow_b], W=[lamrow_b])
        cx.op("dve", lambda e: e.reduce_sum(out=lam1[:, 0:1], in_=lr[:, 0, :], axis=AX.X), R=[lamrow_b], W=[lam1_b])
        cx.op("dve", lambda e: e.reduce_sum(out=lam1[:, 1:2], in_=lr[:, 2, :], axis=AX.X), R=[lamrow_b], W=[lam1_b])
        cx.op("act", lambda e: e.activation(out=lam1[:, 2:4], in_=lam1[:, 0:2], func=AF.Exp), R=[lam1_b], W=[lam1_b])
        cx.op("dve", lambda e: e.tensor_tensor(out=lam1[:, 0:1], in0=lam1[:, 2:3], in1=lam1[:, 3:4], op=ALU.subtract), R=[lam1_b], W=[lam1_b])
        cx.op("dve", lambda e: e.tensor_scalar(out=lam1[:, 0:1], in0=lam1[:, 0:1], scalar1=LAM_INIT, scalar2=-1.0, op0=ALU.add, op1=ALU.mult), R=[lam1_b], W=[lam1_b])
        _pb, _pbb = next_bank()
        cx.op("pe", lambda e: e.matmul(_pb[:, 0:1], lhsT=ones_f[:], rhs=lam1[:, 0:1], start=True, stop=True), R=[onesf_b, lam1_b], W=[_pbb])
        cx.op("act", lambda e: e.copy(out=lam_t[:], in_=_pb[:, 0:1]), R=[_pbb], W=[lam_b])

        def win_stage(N):
            nb = N // 128
            pre_norm(xT, xT_b, "n_mix_pre", N)
            for pi in range(6):
                wp, wpb = load_wpiece(I["w_in"], pi * 256)
                for b in range(nb):
                    pb, pbb = next_bank()
                    for c in range(8):
                        cx.op("pe", lambda e, c=c: e.matmul(pb[:, 0:256], lhsT=u[:, c, b * 128:(b + 1) * 128], rhs=wp[:, c, :], start=(c == 0), stop=(c == 7)),
                              R=[wpb, u_b], W=[pbb], inc=(c == 7))
                    k = st.get("stg", 0) % 3
                    st["stg"] = st.get("stg", 0) + 1
                    cx.op("act", lambda e: e.copy(out=stg[k][:], in_=pb[:, 0:256]), R=[pbb], W=[stg_b[k]])
                    cx.dma("sp", lambda e: e.dma_start(out=qkv_s[b * 128:(b + 1) * 128, pi * 256:(pi + 1) * 256], in_=stg[k][:]), R=[stg_b[k]], W=[qkv_sb])
            for pi in range(7):
                wp, wpb = load_wpiece(I["w_in"], 1536 + pi * 256)
                for hf in range(2):
                    j = pi * 2 + hf
                    pb, pbb = next_bank()
                    for c in range(8):
                        cx.op("pe", lambda e, c=c: e.matmul(pb[:, 0:N], lhsT=wp[:, c, hf * 128:(hf + 1) * 128], rhs=u[:, c, 0:N], start=(c == 0), stop=(c == 7)),
                              R=[wpb, u_b], W=[pbb], inc=(c == 7))
                    k = j % 2
                    cx.op("act", lambda e: e.copy(out=stg2[k][:, 0:N], in_=pb[:, 0:N]), R=[pbb], W=[stg2_b[k]])
                    cx.dma("sp", lambda e: e.dma_start(out=pb_s[:, j, 0:N], in_=stg2[k][:, 0:N]), R=[stg2_b[k]], W=[pb_sb])

        def sub_front(s, cos_src, sin_src):
            N = TT
            nblk = 1
            cx.dma("sp", lambda e: e.dma_start(out=qk_tm[:, 0, :], in_=qkv_s[s * 128:(s + 1) * 128, 0:1024]), R=[qkv_sb], W=[qk_b])
            cx.dma("sp", lambda e: e.dma_start(out=v_tm[:, 0, :], in_=qkv_s[s * 128:(s + 1) * 128, 1024:1536]), R=[qkv_sb], W=[v_b])
            cx.dma("sp", lambda e: e.dma_start(out=pbT[:, :, 1:N + 1], in_=pb_s[:, :, s * 128:(s + 1) * 128]), R=[pb_sb], W=[pbT_b])
            cx.dma("sp", lambda e: e.dma_start(out=cos_t[:, 0:nblk, :], in_=cos_src.rearrange("(b p) d -> p b d", p=128)), W=[cs_b])
            cx.dma("sp", lambda e: e.dma_start(out=sin_t[:, 0:nblk, :], in_=sin_src.rearrange("(b p) d -> p b d", p=128)), W=[cs_b])
            for b in range(nblk):
                xv = qk_tm[:, b, :].rearrange("p (g d) -> p g d", d=64)
                cv_ = cos_t[:, b, :].rearrange("p (g d) -> p g d", d=8)
                sv = sin_t[:, b, :].rearrange("p (g d) -> p g d", d=8)
                x1, x2 = xv[:, :, 0:8], xv[:, :, 8:16]
                cx.op("dve", lambda e: e.tensor_tensor(out=rt[0][:], in0=x1, in1=cv_, op=ALU.mult), R=[qk_b, cs_b], W=[rt_b[0]])
                cx.op("dve", lambda e: e.tensor_tensor(out=rt[1][:], in0=x2, in1=sv, op=ALU.mult), R=[qk_b, cs_b], W=[rt_b[1]])
                cx.op("dve", lambda e: e.tensor_tensor(out=rt[2][:], in0=x2, in1=cv_, op=ALU.mult), R=[qk_b, cs_b], W=[rt_b[2]])
                cx.op("dve", lambda e: e.tensor_tensor(out=rt[3][:], in0=x1, in1=sv, op=ALU.mult), R=[qk_b, cs_b], W=[rt_b[3]])
                cx.op("dve", lambda e: e.tensor_tensor(out=x1, in0=rt[0][:], in1=rt[1][:], op=ALU.subtract), R=[rt_b[0], rt_b[1]], W=[qk_b])
                cx.op("dve", lambda e: e.tensor_tensor(out=x2, in0=rt[2][:], in1=rt[3][:], op=ALU.add), R=[rt_b[2], rt_b[3]], W=[qk_b])
            return nblk

        def prompt_kv_out(t, nblk):
            cx.dma("sp", lambda e: e.dma_start(out=O["kp"][t * TT:(t + 1) * TT, :].rearrange("(b p) d -> p b d", p=128), in_=qk_tm[:, 0:nblk, 512:1024]), R=[qk_b])
            cx.dma("sp", lambda e: e.dma_start(out=O["vp"][t * TT:(t + 1) * TT, :].rearrange("(b p) d -> p b d", p=128), in_=v_tm[:, 0:nblk, :]), R=[v_b])
            for h in range(4):
                pb, pbb = next_bank()
                for b in range(nblk):
                    cx.op("pe", lambda e, b=b: e.transpose(pb[:, b * 128:(b + 1) * 128], qk_tm[:, b, h * 128:(h + 1) * 128], ident[:]),
                          R=[qk_b, ident_b], W=[pbb], inc=(b == nblk - 1))
                cx.op("act", lambda e: e.activation(out=QT[:, h, 0:nblk * 128], in_=pb[:, 0:nblk * 128], func=AF.Copy, scale=0.125), R=[pbb], W=[QT_b])
                pb2, pbb2 = next_bank()
                for b in range(nblk):
                    cx.op("pe", lambda e, b=b: e.transpose(pb2[:, b * 128:(b + 1) * 128], qk_tm[:, b, 512 + h * 128:512 + (h + 1) * 128], ident[:]),
                          R=[qk_b, ident_b], W=[pbb2], inc=(b == nblk - 1))
                cx.op("act", lambda e: e.copy(out=ktn[:, h, :], in_=pb2[:, 0:128]), R=[pbb2], W=[ktn_b])
            cx.op("pool", lambda e: e.tensor_copy(out=vbn[:].rearrange("p h e -> p (h e)"), in_=v_tm[:, 0, :]), R=[v_b], W=[vbn_b])
            cx.dma("sp", lambda e: e.dma_start(out=KT_s[:, :, t * 128:(t + 1) * 128].rearrange("h p k -> p h k"), in_=ktn[:]), R=[ktn_b], W=[KTs_b])
            cx.dma("sp", lambda e: e.dma_start(out=V_s[:, :, t, :].rearrange("h p e -> p h e"), in_=vbn[:]), R=[vbn_b], W=[Vs_b])

        def pb_proj(N):
            for pi in range(7):
                wp, wpb = load_wpiece(I["w_in"], 1536 + pi * 256)
                for hf in range(2):
                    j = pi * 2 + hf
                    pb, pbb = next_bank()
                    for c in range(8):
                        cx.op("pe", lambda e, c=c: e.matmul(pb[:, 0:N], lhsT=wp[:, c, hf * 128:(hf + 1) * 128], rhs=u[:, c, 0:N], start=(c == 0), stop=(c == 7)),
                              R=[wpb, u_b], W=[pbb], inc=(c == 7))
                    cx.op("act", lambda e, j=j: e.copy(out=pbT[:, j, 1:N + 1], in_=pb[:, 0:N]), R=[pbb], W=[pbT_b])

        def shift_mix(N):
            cx.op("dve", lambda e: e.tensor_tensor(out=xs[:, :, 0:N], in0=pbT[:, :, 0:N], in1=pbT[:, :, 1:N + 1], op=ALU.subtract), R=[pbT_b], W=[xs_b])
            for j in range(14):
                cx.op("dve", lambda e, j=j: e.scalar_tensor_tensor(out=xs[:, j, 0:N], in0=xs[:, j, 0:N], scalar=mu_t[:, j:j + 1], in1=pbT[:, j, 1:N + 1],
                                                                  op0=ALU.mult, op1=ALU.add), R=[xs_b, mu_b, pbT_b], W=[xs_b])

        def prompt_attention(t):
            for h in range(4):
                attention_head(t, h)

        def attention_head(t, h):
            nkb = (t + 1) * NBLK
            nch = (nkb + 7) // 8
            if True:
                for j in range(2):
                    po, pob = pbank[6], pbank_b[6]
                    pl, plb = pbank[7], pbank_b[7]
                    for ch in range(nch):
                        nb = min(8, nkb - ch * 8)
                        ri = st.get("kv", 0) % 2
                        st["kv"] = st.get("kv", 0) + 1
                        kc, kcb, vc, vcb = kring[ri], kring_b[ri], vring[ri], vring_b[ri]
                        cx.dma("sp", lambda e: e.dma_start(out=kc[:, 0:nb * 128], in_=KT_s[h, :, ch * KC:ch * KC + nb * 128]), R=[KTs_b], W=[kcb])
                        cx.dma("sp", lambda e: e.dma_start(out=vc[:, 0:nb, :], in_=V_s[h, :, ch * 8:ch * 8 + nb, :]), R=[Vs_b], W=[vcb])
                        ngr = (nb + 3) // 4
                        prev = None
                        for g in range(ngr + 1):
                            if g < ngr:
                                k0 = g * 4
                                ng = min(4, nb - k0)
                                pS, pSb = next_bank()
                                for i in range(ng):
                                    kl = k0 + i
                                    cx.op("pe", lambda e: e.matmul(pS[:, i * TT:(i + 1) * TT], lhsT=kc[64 * j:64 * j + 64, kl * 128:(kl + 1) * 128], rhs=QT[64 * j:64 * j + 64, h, :],
                                                                   start=True, stop=True), R=[kcb, QT_b], W=[pSb], inc=(i == ng - 1))
                                pi_ = st.get("pt", 0) % 2
                                st["pt"] = st.get("pt", 0) + 1
                                p, pbf = PT[pi_], PT_b[pi_]
                                cx.op("act", lambda e: e.activation(out=p[:, 0:ng * TT], in_=pS[:, 0:ng * TT], func=AF.Exp), R=[pSb], W=[pbf])
                                for i in range(ng):
                                    r = ch * 8 + k0 + i - t * NBLK
                                    if r >= 0:
                                        cx.op("pool", lambda e: e.tensor_tensor(out=p[:, i * TT:(i + 1) * TT], in0=p[:, i * TT:(i + 1) * TT], in1=cmask[:, r, :], op=ALU.mult),
                                              R=[pbf, cmask_b], W=[pbf])
                                cur = (k0, ng, p, pbf)
                            else:
                                cur = None
                            if prev is not None:
                                k00, ng0, p0, pbf0 = prev
                                for i in range(ng0):
                                    kl0 = k00 + i
                                    kb0 = ch * 8 + kl0
                                    cx.op("pe", lambda e: e.matmul(po[:, 0:TT], lhsT=vc[:, kl0, :], rhs=p0[:, i * TT:(i + 1) * TT], start=(kb0 == 0), stop=(kb0 == nkb - 1)),
                                          R=[vcb, pbf0], W=[pob], inc=False)
                                    cx.op("pe", lambda e: e.matmul(pl[:, 0:TT], lhsT=ones_bf[:], rhs=p0[:, i * TT:(i + 1) * TT], start=(kb0 == 0), stop=(kb0 == nkb - 1)),
                                          R=[ones_b, pbf0], W=[plb], inc=True)
                            prev = cur
                    cx.op("dve", lambda e: e.reciprocal(out=rl[:], in_=pl[:, 0:TT]), R=[plb], W=[rl_b])
                    if j == 0:
                        cx.op("dve", lambda e: e.tensor_tensor(out=o0[:], in0=po[:, 0:TT], in1=rl[:], op=ALU.mult), R=[pob, rl_b], W=[o0_b])
                    else:
                        cx.op("dve", lambda e: e.tensor_tensor(out=ya[:], in0=po[:, 0:TT], in1=rl[:], op=ALU.mult), R=[pob, rl_b], W=[ya_b])
                        cx.op("dve", lambda e: e.scalar_tensor_tensor(out=ya[:], in0=ya[:], scalar=lam_t[:, 0:1], in1=o0[:], op0=ALU.mult, op1=ALU.add),
                              R=[ya_b, lam_b, o0_b], W=[ya_b])
                subln_norm(h, TT)

        def subln_norm(h, N):
            cx.op("act", lambda e: e.activation(out=sq[:, 0, 0:N], in_=ya[:, 0:N], func=AF.Square), R=[ya_b], W=[sq_b])
            pb, pbb = next_bank()
            cx.op("pe", lambda e: e.matmul(pb[:, 0:N], lhsT=ones_bf[:], rhs=sq[:, 0, 0:N], start=True, stop=True), R=[ones_b, sq_b], W=[pbb])
            cx.op("act", lambda e: e.activation(out=rstd[:, 0:N], in_=pb[:, 0:N], func=AF.Ln, scale=1.0 / 128, bias=eps128[:]), R=[pbb, eps_b], W=[rstd_b])
            cx.op("act", lambda e: e.activation(out=rstd[:, 0:N], in_=rstd[:, 0:N], func=AF.Exp, scale=-0.5), R=[rstd_b], W=[rstd_b])
            cx.op("dve", lambda e: e.scalar_tensor_tensor(out=ya[:, 0:N], in0=ya[:, 0:N], scalar=subln_t[:, 0:1], in1=rstd[:, 0:N], op0=ALU.mult, op1=ALU.mult),
                  R=[ya_b, subln_b, rstd_b], W=[ya_b])
            cx.op("dve", lambda e: e.tensor_scalar(out=mixT[:, h, st["moff"]:st["moff"] + N], in0=ya[:, 0:N], scalar1=1.0 - LAM_INIT, scalar2=None, op0=ALU.mult), R=[ya_b], W=[mixT_b])

        for nm in ("w0", "a0", "k_k", "k_a", "r_k", "ln_x_w", "ln_x_b"):
            cx.dma("sp", lambda e, nm=nm: e.dma_start(out=cv[nm][0][:], in_=I[nm]), W=[cv[nm][1]])
        cx.dma("pool", lambda e: e.dma_start(out=wa2[:], in_=I["wa2"]), W=[wa2_b])
        cx.dma("pool", lambda e: e.dma_start(out=g2t[:], in_=I["g2"]), W=[g2_b])
        cx.dma("pool", lambda e: e.dma_start(out=bones[:], in_=I["blockones"]), W=[cst_b])
        for tl, nm in ((resetm, "resetm"), (maskA4, "maskA4"), (maskC8, "maskC8"), (I8, "I8")):
            cx.dma("sp", lambda e, tl=tl, nm=nm: e.dma_start(out=tl[:], in_=I[nm]), W=[cst_b])
        cx.op("dve", lambda e: e.tensor_scalar(out=cv["nw0"][0][:], in0=cv["w0"][0][:], scalar1=-1.0, scalar2=None, op0=ALU.mult), R=[cv["w0"][1]], W=[cv["nw0"][1]])
        cx.op("dve", lambda e: e.tensor_scalar(out=cv["na0"][0][:], in0=cv["a0"][0][:], scalar1=-1.0, scalar2=None, op0=ALU.mult), R=[cv["a0"][1]], W=[cv["na0"][1]])
        cx.op("dve", lambda e: e.tensor_scalar(out=cv["omka"][0][:], in0=cv["k_a"][0][:], scalar1=-1.0, scalar2=1.0, op0=ALU.mult, op1=ALU.add), R=[cv["k_a"][1]], W=[cv["omka"][1]])
        cx.op("dve", lambda e: e.memset(gneps[:], GN_EPS), W=[eps_b])
        cx.op("dve", lambda e: e.memset(Af[:], 0.0), W=[Af_b])
        cx.op("dve", lambda e: e.memset(Ab[:], 0.0), W=[Ab_b])

        def C(nm):
            return cv[nm][0]

        def Cb(nm):
            return cv[nm][1]

        def prep_head(N):
            nch = N // 64
            nblk = N // 128
            cx.op("act", lambda e: e.activation(out=f1[0:64, 0:N], in_=xs[0:64, 12, 0:N], func=AF.Exp, scale=2.0), R=[xs_b], W=[f_b[0]])
            cx.op("dve", lambda e: e.tensor_scalar(out=f1[0:64, 0:N], in0=f1[0:64, 0:N], scalar1=1.0, scalar2=None, op0=ALU.add), R=[f_b[0]], W=[f_b[0]])
            cx.op("dve", lambda e: e.reciprocal(out=f1[0:64, 0:N], in_=f1[0:64, 0:N]), R=[f_b[0]], W=[f_b[0]])
            cx.op("dve", lambda e: e.tensor_scalar(out=lora_in[0:64, 0:N], in0=f1[0:64, 0:N], scalar1=-2.0, scalar2=1.0, op0=ALU.mult, op1=ALU.add), R=[f_b[0]], W=[lora_b])
            cx.op("dve", lambda e: e.tensor_copy(out=lora_in[64:128, 0:N], in_=xs[64:128, 12, 0:N]), R=[xs_b], W=[lora_b])
            cx.op("act", lambda e: e.activation(out=f2[:, 0:N], in_=xs[:, 13, 0:N], func=AF.Exp, scale=-1.0), R=[xs_b], W=[f_b[1]])
            cx.op("dve", lambda e: e.tensor_scalar(out=f2[:, 0:N], in0=f2[:, 0:N], scalar1=1.0, scalar2=None, op0=ALU.add), R=[f_b[1]], W=[f_b[1]])
            cx.op("dve", lambda e: e.reciprocal(out=f2[:, 0:N], in_=f2[:, 0:N]), R=[f_b[1]], W=[f_b[1]])
            cx.op("dve", lambda e: e.tensor_copy(out=sgx[:, 0:N], in_=f2[:, 0:N]), R=[f_b[1]], W=[sgx_b])

        def prep_m(N, m, sample=False):
            nch = N // 64
            ms = slice(m * 128, (m + 1) * 128)
            pa, pab = next_bank()
            cx.op("pe", lambda e: e.matmul(pa[:, 0:N], lhsT=wa2[64:128, ms], rhs=lora_in[64:128, 0:N], start=True, stop=True), R=[wa2_b, lora_b], W=[pab])
            cx.op("act", lambda e: e.activation(out=a_t[:, m, 0:N], in_=pa[:, 0:N], func=AF.Exp, scale=-1.0, bias=C("na0")[:, m:m + 1]), R=[pab, Cb("na0")], W=[a_b])
            cx.op("dve", lambda e: e.tensor_scalar(out=a_t[:, m, 0:N], in0=a_t[:, m, 0:N], scalar1=1.0, scalar2=None, op0=ALU.add), R=[a_b], W=[a_b])
            cx.op("dve", lambda e: e.reciprocal(out=a_t[:, m, 0:N], in_=a_t[:, m, 0:N]), R=[a_b], W=[a_b])
            pg, pgb = next_bank()
            cx.op("pe", lambda e: e.matmul(pg[:, 0:N], lhsT=g2t[:, ms], rhs=sgx[:, 0:N], start=True, stop=True), R=[g2_b, sgx_b], W=[pgb])
            cx.op("act", lambda e: e.copy(out=gT[:, m, 0:N], in_=pg[:, 0:N]), R=[pgb], W=[gT_b])
            ms = slice(m * 128, (m + 1) * 128)
            pw, pwb = next_bank()
            cx.op("pe", lambda e: e.matmul(pw[:, 0:N], lhsT=wa2[0:64, ms], rhs=lora_in[0:64, 0:N], start=True, stop=True), R=[wa2_b, lora_b], W=[pwb])
            cx.op("act", lambda e: e.activation(out=f1[:, 0:N], in_=pw[:, 0:N], func=AF.Exp, scale=-1.0, bias=C("nw0")[:, m:m + 1]), R=[pwb, Cb("nw0")], W=[f_b[0]])
            cx.op("dve", lambda e: e.tensor_scalar(out=f1[:, 0:N], in0=f1[:, 0:N], scalar1=1.0, scalar2=None, op0=ALU.add), R=[f_b[0]], W=[f_b[0]])
            cx.op("act", lambda e: e.activation(out=f1[:, 0:N], in_=f1[:, 0:N], func=AF.Ln), R=[f_b[0]], W=[f_b[0]])
            cx.op("dve", lambda e: e.tensor_scalar(out=f1[:, 0:N], in0=f1[:, 0:N], scalar1=-1.0, scalar2=-0.5, op0=ALU.mult, op1=ALU.add), R=[f_b[0]], W=[f_b[0]])
            cx.op("act", lambda e: e.activation(out=f1[:, 0:N], in_=f1[:, 0:N], func=AF.Exp), R=[f_b[0]], W=[f_b[0]])
            cx.op("dve", lambda e: e.tensor_scalar(out=ld[:, m, 0:N], in0=f1[:, 0:N], scalar1=-1.0, scalar2=None, op0=ALU.mult), R=[f_b[0]], W=[ld_b])
            r_, k_ = xs[:, m, 0:N], xs[:, 4 + m, 0:N]
            A_ = a_t[:, m, 0:N]
            cx.op("dve", lambda e: e.tensor_scalar(out=f1[:, 0:N], in0=k_, scalar1=C("k_k")[:, m:m + 1], scalar2=None, op0=ALU.mult), R=[xs_b, Cb("k_k")], W=[f_b[0]])
            cx.op("act", lambda e: e.activation(out=fbf[:, 0:N], in_=f1[:, 0:N], func=AF.Square), R=[f_b[0]], W=[fbf_b])
            pn, pnb = next_bank()
            cx.op("pe", lambda e: e.matmul(pn[:, 0:N], lhsT=bones[:], rhs=fbf[:, 0:N], start=True, stop=True), R=[cst_b, fbf_b], W=[pnb])
            cx.op("dve", lambda e: e.tensor_scalar(out=f2[:, 0:N], in0=pn[:, 0:N], scalar1=1e-24, scalar2=None, op0=ALU.max), R=[pnb], W=[f_b[1]])
            cx.op("act", lambda e: e.activation(out=f2[:, 0:N], in_=f2[:, 0:N], func=AF.Ln), R=[f_b[1]], W=[f_b[1]])
            cx.op("act", lambda e: e.activation(out=f2[:, 0:N], in_=f2[:, 0:N], func=AF.Exp, scale=-0.5), R=[f_b[1]], W=[f_b[1]])
            cx.op("dve", lambda e: e.tensor_tensor(out=f1[:, 0:N], in0=f1[:, 0:N], in1=f2[:, 0:N], op=ALU.mult), R=[f_b[0], f_b[1]], W=[f_b[0]])
            cx.op("dve", lambda e: e.tensor_scalar(out=f2[:, 0:N], in0=A_, scalar1=C("k_a")[:, m:m + 1], scalar2=C("omka")[:, m:m + 1], op0=ALU.mult, op1=ALU.add),
                  R=[a_b, Cb("k_a"), Cb("omka")], W=[f_b[1]])
            cx.op("dve", lambda e: e.tensor_tensor(out=f2[:, 0:N], in0=f2[:, 0:N], in1=k_, op=ALU.mult), R=[f_b[1], xs_b], W=[f_b[1]])
            cx.op("dve", lambda e: e.tensor_tensor(out=f3[:, 0:N], in0=f1[:, 0:N], in1=A_, op=ALU.mult), R=[f_b[0], a_b], W=[f_b[2]])
            if sample:
                cx.op("pool", lambda e: e.tensor_copy(out=raw["kkn"][0][:, m, :], in_=f1[:, 0:NS]), R=[f_b[0]], W=[raw["kkn"][1]])
                cx.op("pool", lambda e: e.tensor_copy(out=raw["kf"][0][:, m, :], in_=f2[:, 0:NS]), R=[f_b[1]], W=[raw["kf"][1]])
                cx.op("pool", lambda e: e.tensor_copy(out=raw["b"][0][:, m, :], in_=f3[:, 0:NS]), R=[f_b[2]], W=[raw["b"][1]])
                cx.op("pool", lambda e: e.tensor_copy(out=raw["r"][0][:, m, :], in_=xs[:, m, 0:NS]), R=[xs_b], W=[raw["r"][1]])
                cx.op("act", lambda e: e.activation(out=raw["w"][0][:, m, :], in_=ld[:, m, 0:NS], func=AF.Exp), R=[ld_b], W=[raw["w"][1]])
            cx.op("dve", lambda e: e.scalar_tensor_tensor(out=fbf[:, 0:N], in0=r_, scalar=C("r_k")[:, m:m + 1], in1=f2[:, 0:N], op0=ALU.mult, op1=ALU.mult),
                  R=[xs_b, Cb("r_k"), f_b[1]], W=[fbf_b])
            pq, pqb = next_bank()
            cx.op("pe", lambda e: e.matmul(pq[:, 0:N], lhsT=bones[:], rhs=fbf[:, 0:N], start=True, stop=True), R=[cst_b, fbf_b], W=[pqb])
            cx.op("act", lambda e: e.copy(out=bs_t[:, m, 0:N], in_=pq[:, 0:N]), R=[pqb], W=[bs_b])
            if sample:
                return
            cx.op("dve", lambda e: e.tensor_tensor_scan(out=f4[:, 0:N], data0=resetm[:, 0:N], data1=ld[:, m, 0:N], initial=0.0, op0=ALU.mult, op1=ALU.add),
                  R=[cst_b, ld_b], W=[f_b[3]])
            c3 = f4[:, 0:N].rearrange("p (c t) -> p c t", t=64)
            cx.op("act", lambda e: e.activation(out=gC[:, m, 0:nch], in_=c3[:, :, 63], func=AF.Exp), R=[f_b[3]], W=[gC_b])
            cx.op("act", lambda e: e.activation(out=f5[:, 0:N], in_=f4[:, 0:N], func=AF.Exp), R=[f_b[3]], W=[f_b[4]])
            cx.op("dve", lambda e: e.tensor_tensor(out=f6[:, 0:N], in0=f4[:, 0:N], in1=ld[:, m, 0:N], op=ALU.subtract), R=[f_b[3], ld_b], W=[f_b[5]])
            cx.op("act", lambda e: e.activation(out=f6[:, 0:N], in_=f6[:, 0:N], func=AF.Exp), R=[f_b[5]], W=[f_b[5]])
            krv = kr[:, m, 0:nch, :]
            kbv = kb_[:, m, 0:nch, :]
            v3 = lambda tl: tl[:, 0:N].rearrange("p (c t) -> p c t", t=64)
            cx.op("dve", lambda e: e.tensor_tensor(out=krv[:, :, 0:64], in0=v3(f1), in1=v3(f6), op=ALU.mult), R=[f_b[0], f_b[5]], W=[kr_b])
            cx.op("dve", lambda e: e.tensor_tensor(out=krv[:, :, 64:128], in0=xs[:, m, 0:N].rearrange("p (c t) -> p c t", t=64), in1=v3(f5), op=ALU.mult),
                  R=[xs_b, f_b[4]], W=[kr_b])
            cx.op("dve", lambda e: e.tensor_tensor(out=v3(f6), in0=c3[:, :, 63:64].to_broadcast([128, nch, 64]), in1=c3, op=ALU.subtract), R=[f_b[3]], W=[f_b[5]])
            cx.op("act", lambda e: e.activation(out=f6[:, 0:N], in_=f6[:, 0:N], func=AF.Exp), R=[f_b[5]], W=[f_b[5]])
            cx.op("dve", lambda e: e.tensor_tensor(out=khat[:, m, 0:N], in0=f2[:, 0:N], in1=f6[:, 0:N], op=ALU.mult), R=[f_b[1], f_b[5]], W=[khat_b])
            cx.op("dve", lambda e: e.scalar_tensor_tensor(out=bhat[:, m, 0:N], in0=f3[:, 0:N], scalar=-1.0, in1=f6[:, 0:N], op0=ALU.mult, op1=ALU.mult),
                  R=[f_b[2], f_b[5]], W=[bhat_b])
            cx.op("act", lambda e: e.activation(out=f5[:, 0:N], in_=f4[:, 0:N], func=AF.Exp, scale=-1.0), R=[f_b[3]], W=[f_b[4]])
            cx.op("dve", lambda e: e.tensor_tensor(out=f2[:, 0:N], in0=f2[:, 0:N], in1=f5[:, 0:N], op=ALU.mult), R=[f_b[1], f_b[4]], W=[f_b[1]])
            cx.op("dve", lambda e: e.tensor_tensor(out=f3[:, 0:N], in0=f3[:, 0:N], in1=f5[:, 0:N], op=ALU.mult), R=[f_b[2], f_b[4]], W=[f_b[2]])
            cx.op("pool", lambda e: e.tensor_copy(out=kbv[:, :, 0:64], in_=v3(f2)), R=[f_b[1]], W=[kb_b])
            cx.op("pool", lambda e: e.tensor_copy(out=kbv[:, :, 64:128], in_=v3(f3)), R=[f_b[2]], W=[kb_b])

        def prep_tail(N, sample=False):
            nblk = N // 128
            if sample:
                return
            for (srcf, srcb, dst, dstb) in ((lambda m, b: xs[:, 8 + m, b * 128:(b + 1) * 128], xs_b, V_tm, Vtm_b),
                                            (lambda m, b: khat[:, m, b * 128:(b + 1) * 128], khat_b, Kh_tm, Kh_b),
                                            (lambda m, b: bhat[:, m, b * 128:(b + 1) * 128], bhat_b, Bh_tm, Bh_b)):
                for b in range(nblk):
                    pb, pbb = next_bank()
                    for m in range(4):
                        cx.op("pe", lambda e, m=m: e.transpose(pb[:, m * 128:(m + 1) * 128], srcf(m, b), ident[:]), R=[srcb, ident_b], W=[pbb], inc=(m == 3))
                    cx.op("act", lambda e: e.copy(out=dst[:, b, :], in_=pb[:, :]), R=[pbb], W=[dstb])


        def rwkv_prep(N, sample=False):
            prep_head(N)
            for m in range(4):
                prep_m(N, m, sample)
            prep_tail(N, sample)

        def mm64(out, lhsT, rhs, rowb, colb, R, W, start=True, stop=True, inc=True):
            Epe = cx.E["pe"]
            if st.get("rowb") is not None and st["rowb"] != rowb and Epe.cnt > 0:
                Epe.eng.wait_ge(Epe.sem, Epe.cnt)
            st["rowb"] = rowb
            cx.op("pe", lambda e: e.matmul(out, lhsT=lhsT, rhs=rhs, start=start, stop=stop, tile_position=(rowb, colb)), R=R, W=W, inc=True)

        def rwkv_block(b, stage=9):
            pA0, pA0b = next_bank(); pA1, pA1b = next_bank()
            pB0, pB0b = next_bank(); pB1, pB1b = next_bank()
            pC, pCb = next_bank()
            for cp in range(2):
                ci = 2 * b + cp
                cb = 64 * cp
                for h in range(8):
                    hb, m = 64 * (h % 2), h // 2
                    pa, pab = (pA0, pA0b) if h < 4 else (pA1, pA1b)
                    pbk, pbkb = (pB0, pB0b) if h < 4 else (pB1, pB1b)
                    hs = slice((h % 4) * 128, (h % 4 + 1) * 128)
                    mm64(pa[cb:cb + 64, hs], kb_[hb:hb + 64, m, ci, 0:64], kr[hb:hb + 64, m, ci, :], hb, cb, [kb_b, kr_b], [pab])
                    mm64(pbk[cb:cb + 64, hs], kb_[hb:hb + 64, m, ci, 64:128], kr[hb:hb + 64, m, ci, :], hb, cb, [kb_b, kr_b], [pbkb])
                    mm64(pC[cb:cb + 64, h * 64:(h + 1) * 64], kr[hb:hb + 64, m, ci, 0:64], kb_[hb:hb + 64, m, ci, 64:128], hb, cb, [kb_b, kr_b], [pCb])
            Amf = Am[:].rearrange("p h c -> p (h c)")
            Bmf = Bm[:].rearrange("p h c -> p (h c)")
            cx.op("dve", lambda e: e.tensor_tensor(out=Amf[:, 0:512], in0=pA0[:, :], in1=maskA4[:], op=ALU.mult), R=[pA0b, cst_b], W=[Am_b])
            cx.op("dve", lambda e: e.tensor_tensor(out=Amf[:, 512:1024], in0=pA1[:, :], in1=maskA4[:], op=ALU.mult), R=[pA1b, cst_b], W=[Am_b])
            cx.op("dve", lambda e: e.scalar_tensor_tensor(out=Bmf[:, 0:512], in0=pB0[:, :], scalar=-1.0, in1=maskA4[:], op0=ALU.mult, op1=ALU.mult), R=[pB0b, cst_b], W=[Bm_b])
            cx.op("dve", lambda e: e.scalar_tensor_tensor(out=Bmf[:, 512:1024], in0=pB1[:, :], scalar=-1.0, in1=maskA4[:], op0=ALU.mult, op1=ALU.mult), R=[pB1b, cst_b], W=[Bm_b])
            if stage < 3:
                return
            zc, zcb, ztc, ztcb = Zb[0], Zb_b[0], ZTb[0], ZTb_b[0]
            fl = lambda tl: tl[:].rearrange("p h c -> p (h c)")
            cx.op("dve", lambda e: e.tensor_copy(out=zc[:], in_=Bm[:, :, 0:64]), R=[Bm_b], W=[zcb])
            cx.op("dve", lambda e: e.scalar_tensor_tensor(out=fl(ztc), in0=pC[:, :], scalar=-1.0, in1=maskC8[:], op0=ALU.mult, op1=ALU.mult), R=[pCb, cst_b], W=[ztcb])
            cx.op("dve", lambda e: e.tensor_tensor(out=fl(Wf), in0=fl(zc), in1=I8[:], op=ALU.add), R=[zcb, cst_b], W=[Wf_b])
            cx.op("act", lambda e: e.copy(out=Wb[:], in_=Wf[:]), R=[Wf_b], W=[Wb_b])
            cur = 0
            for lvl in range(5):
                nz, nzb, nzt, nztb = Zb[1 - cur], Zb_b[1 - cur], ZTb[1 - cur], ZTb_b[1 - cur]
                pz, pzb = next_bank(); pzt, pztb = next_bank()
                for cp in range(2):
                    cb = 64 * cp
                    for h in range(8):
                        hs = slice(h * 64, (h + 1) * 64)
                        if lvl < 4:
                            mm64(pz[cb:cb + 64, hs], ztc[cb:cb + 64, h, :], zc[cb:cb + 64, h, :], cb, cb, [ztcb, zcb], [pzb])
                        mm64(pzt[cb:cb + 64, hs], zc[cb:cb + 64, h, :], ztc[cb:cb + 64, h, :], cb, cb, [ztcb, zcb], [pztb])
                if lvl < 4:
                    cx.op("act", lambda e: e.copy(out=fl(nz), in_=pz[:, :]), R=[pzb], W=[nzb])
                cx.op("dve", lambda e: e.tensor_copy(out=fl(nzt), in_=pzt[:, :]), R=[pztb], W=[nztb])
                pw, pwb = next_bank()
                for cp in range(2):
                    cb = 64 * cp
                    for h in range(8):
                        hs = slice(h * 64, (h + 1) * 64)
                        mm64(pw[cb:cb + 64, hs], nzt[cb:cb + 64, h, :], Wb[cb:cb + 64, h, :], cb, cb, [nztb, Wb_b], [pwb])
                cx.op("dve", lambda e: e.tensor_tensor(out=fl(Wf), in0=fl(Wf), in1=pw[:, :], op=ALU.add), R=[Wf_b, pwb], W=[Wf_b])
                cx.op("act", lambda e: e.copy(out=Wb[:], in_=Wf[:]), R=[Wf_b], W=[Wb_b])
                cur = 1 - cur
                zc, zcb, ztc, ztcb = Zb[cur], Zb_b[cur], ZTb[cur], ZTb_b[cur]
            if stage < 4:
                return
            for cp in range(2):
                ci = 2 * b + cp
                cb = 64 * cp
                pR, pRb = next_bank()
                for h in range(8):
                    hb, m = 64 * (h % 2), h // 2
                    hs = slice(h * 64, (h + 1) * 64)
                    mm64(pR[cb:cb + 64, hs], kr[hb:hb + 64, m, ci, 0:64], Ab[hb:hb + 64, m, :], hb, cb, [kr_b, Ab_b], [pRb], start=True, stop=False, inc=False)
                    mm64(pR[cb:cb + 64, hs], Am[cb:cb + 64, h, 0:64], V_tm[cb:cb + 64, b, hs], cb, cb, [Am_b, Vtm_b], [pRb], start=False, stop=True, inc=(h == 7))
                cx.op("act", lambda e: e.copy(out=Rb[cb:cb + 64].rearrange("p h c -> p (h c)"), in_=pR[cb:cb + 64, :]), R=[pRb], W=[Rb_b])
                pU, pUb = next_bank()
                for h in range(8):
                    hs = slice(h * 64, (h + 1) * 64)
                    mm64(pU[cb:cb + 64, hs], Wb[cb:cb + 64, h, :], Rb[cb:cb + 64, h, :], cb, cb, [Wb_b, Rb_b], [pUb], inc=(h == 7))
                cx.op("act", lambda e: e.copy(out=Ub[cb:cb + 64].rearrange("p h c -> p (h c)"), in_=pU[cb:cb + 64, :]), R=[pUb], W=[Ub_b])
                pY, pYb = next_bank()
                pAn, pAnb = next_bank()
                for h in range(8):
                    hb, m = 64 * (h % 2), h // 2
                    hs = slice(h * 64, (h + 1) * 64)
                    mm64(pY[cb:cb + 64, hs], kr[hb:hb + 64, m, ci, 64:128], Ab[hb:hb + 64, m, :], hb, cb, [kr_b, Ab_b], [pYb], start=True, stop=False, inc=False)
                    mm64(pY[cb:cb + 64, hs], Am[cb:cb + 64, h, 64:128], V_tm[cb:cb + 64, b, hs], cb, cb, [Am_b, Vtm_b], [pYb], start=False, stop=False, inc=False)
                    mm64(pY[cb:cb + 64, hs], Bm[cb:cb + 64, h, 64:128], Ub[cb:cb + 64, h, :], cb, cb, [Bm_b, Ub_b], [pYb], start=False, stop=True, inc=(h == 7))
                cx.op("act", lambda e: e.copy(out=y_tm[cb:cb + 64, b, :], in_=pY[cb:cb + 64, :]), R=[pYb], W=[ytm_b])
                for h in range(8):
                    hb, m = 64 * (h % 2), h // 2
                    hs = slice(h * 64, (h + 1) * 64)
                    ms_ = slice(m * 64, (m + 1) * 64)
                    mm64(pAn[hb:hb + 64, ms_], Kh_tm[cb:cb + 64, b, hs], V_tm[cb:cb + 64, b, hs], cb, hb, [Kh_b, Vtm_b], [pAnb], start=True, stop=False, inc=False)
                    mm64(pAn[hb:hb + 64, ms_], Bh_tm[cb:cb + 64, b, hs], Ub[cb:cb + 64, h, :], cb, hb, [Bh_b, Ub_b], [pAnb], start=False, stop=True, inc=(h == 7))
                for m in range(4):
                    cx.op("dve", lambda e, m=m: e.scalar_tensor_tensor(out=Af[:, m, :], in0=Af[:, m, :], scalar=gC[:, m, ci:ci + 1], in1=pAn[:, m * 64:(m + 1) * 64],
                                                                      op0=ALU.mult, op1=ALU.add), R=[Af_b, gC_b, pAnb], W=[Af_b])
                cx.op("act", lambda e: e.copy(out=Ab[:], in_=Af[:]), R=[Af_b], W=[Ab_b])

        def rwkv_finish(N):
            nblk = N // 128
            ng = nblk * 8
            yv = y_tm[:, 0:nblk, :].rearrange("p b (h v) -> p (b h) v", v=64)
            ysq_v = z[:, 0:4, :].rearrange("p a t -> p (a t)")[:, 0:nblk * 512]
            sv = ysq_v.rearrange("p (g v) -> p g v", v=64)
            cx.op("dve", lambda e: e.reduce_sum(out=gst[:, 0, 0:ng], in_=yv, axis=AX.X), R=[ytm_b], W=[gst_b])
            cx.op("pool", lambda e: e.tensor_tensor(out=ysq_v, in0=y_tm[:, 0:nblk, :].rearrange("p b c -> p (b c)"), in1=y_tm[:, 0:nblk, :].rearrange("p b c -> p (b c)"), op=ALU.mult), R=[ytm_b], W=[ysq_b])
            cx.op("dve", lambda e: e.reduce_sum(out=gst[:, 1, 0:ng], in_=sv, axis=AX.X), R=[ysq_b], W=[gst_b])
            cx.op("dve", lambda e: e.tensor_scalar(out=gst[:, 0, 0:ng], in0=gst[:, 0, 0:ng], scalar1=1.0 / 64, scalar2=None, op0=ALU.mult), R=[gst_b], W=[gst_b])
            cx.op("dve", lambda e: e.tensor_tensor(out=gst[:, 2, 0:ng], in0=gst[:, 0, 0:ng], in1=gst[:, 0, 0:ng], op=ALU.mult), R=[gst_b], W=[gst_b])
            cx.op("dve", lambda e: e.scalar_tensor_tensor(out=gst[:, 1, 0:ng], in0=gst[:, 1, 0:ng], scalar=1.0 / 64, in1=gst[:, 2, 0:ng], op0=ALU.mult, op1=ALU.subtract),
                  R=[gst_b], W=[gst_b])
            cx.op("act", lambda e: e.activation(out=gst[:, 1, 0:ng], in_=gst[:, 1, 0:ng], func=AF.Ln, bias=gneps[:]), R=[gst_b, eps_b], W=[gst_b])
            cx.op("act", lambda e: e.activation(out=gst[:, 1, 0:ng], in_=gst[:, 1, 0:ng], func=AF.Exp, scale=-0.5), R=[gst_b], W=[gst_b])
            cx.op("dve", lambda e: e.tensor_tensor(out=yv, in0=yv, in1=gst[:, 0, 0:ng].unsqueeze(2).to_broadcast([128, ng, 64]), op=ALU.subtract), R=[ytm_b, gst_b], W=[ytm_b])
            cx.op("dve", lambda e: e.tensor_tensor(out=yv, in0=yv, in1=gst[:, 1, 0:ng].unsqueeze(2).to_broadcast([128, ng, 64]), op=ALU.mult), R=[ytm_b, gst_b], W=[ytm_b])
            for m in range(4):
                pb, pbb = next_bank()
                for b in range(nblk):
                    cx.op("pe", lambda e, b=b: e.transpose(pb[:, b * 128:(b + 1) * 128], y_tm[:, b, m * 128:(m + 1) * 128], ident[:]), R=[ytm_b, ident_b], W=[pbb], inc=(b == nblk - 1))
                cx.op("act", lambda e: e.activation(out=f1[:, 0:N], in_=pb[:, 0:N], func=AF.Identity, scale=C("ln_x_w")[:, m:m + 1], bias=C("ln_x_b")[:, m:m + 1]),
                      R=[pbb, Cb("ln_x_w"), Cb("ln_x_b")], W=[f_b[0]])
                cx.op("dve", lambda e: e.tensor_tensor(out=f2[:, 0:N], in0=bs_t[:, m, 0:N], in1=xs[:, 8 + m, 0:N], op=ALU.mult), R=[bs_b, xs_b], W=[f_b[1]])
                cx.op("dve", lambda e: e.tensor_tensor(out=f1[:, 0:N], in0=f1[:, 0:N], in1=f2[:, 0:N], op=ALU.add), R=[f_b[0], f_b[1]], W=[f_b[0]])
                cx.op("dve", lambda e: e.tensor_tensor(out=mixT[:, 4 + m, st["moff"]:st["moff"] + N], in0=f1[:, 0:N], in1=gT[:, m, 0:N], op=ALU.mult), R=[f_b[0], gT_b], W=[mixT_b])

        def wkv_out(dst):
            for m in range(4):
                pb, pbb = next_bank()
                cx.op("pe", lambda e: e.transpose(pb[0:64, 0:128], Af[:, m, :], ident[:]), R=[Af_b, ident_b], W=[pbb])
                cx.op("act", lambda e: e.copy(out=wkvT[:, 2 * m:2 * m + 2, :].rearrange("p h k -> p (h k)"), in_=pb[0:64, 0:128]), R=[pbb], W=[wkvT_b])
            cx.dma("sp", lambda e: e.dma_start(out=dst.rearrange("h v k -> v h k"), in_=wkvT), R=[wkvT_b])

        def out_proj(N):
            for m2 in range(4):
                wp, wpb = load_wpiece(I["w_out"], m2 * 256)
                for hf in range(2):
                    mo = m2 * 2 + hf
                    pb, pbb = next_bank()
                    for c in range(8):
                        cx.op("pe", lambda e, c=c: e.matmul(pb[:, 0:N], lhsT=wp[:, c, hf * 128:(hf + 1) * 128], rhs=mixT[:, c, 0:N], start=(c == 0), stop=(c == 7)),
                              R=[wpb, mixT_b], W=[pbb], inc=(c == 7))
                    cx.op("act", lambda e, mo=mo: e.copy(out=z[:, mo, 0:N], in_=pb[:, 0:N]), R=[pbb], W=[z_b])
            post_norm_add("n_mix_post", N, False)

        cx.dma("sp", lambda e: e.dma_start(out=diag64[:], in_=I["diag64"]), W=[sconst_b])
        cx.dma("sp", lambda e: e.dma_start(out=bones_f[:], in_=I["blockones"]), W=[sconst_b])
        cx.dma("sp", lambda e: e.dma_start(out=ones_f32[:], in_=I["ones"]), W=[sconst_b])
        cx.dma("sp", lambda e: e.dma_start(out=ptb[:], in_=I["pt_own"].broadcast_to([128, NS * NPG])), W=[idx_b])
        cx.op("pool", lambda e: e.iota(iota_c[:], pattern=[[0, 1]], base=0, channel_multiplier=1), W=[idx_b])
        cx.op("pool", lambda e: e.tensor_scalar(out=idx[:], in0=ptb[:], scalar1=128, scalar2=None, op0=ALU.mult), R=[idx_b], W=[idx_b])
        cx.op("pool", lambda e: e.tensor_tensor(out=idx[:], in0=idx[:], in1=iota_c[:].to_broadcast([128, NS * NPG]), op=ALU.add), R=[idx_b], W=[idx_b])

        def gather_page(n, pg):
            i = st.get("pg", 0) % NKR
            st["pg"] = st.get("pg", 0) + 1
            j = n * NPG + pg
            cx.dma("pool", lambda e: e.indirect_dma_start(out=kpg[i][:], out_offset=None, in_=I["cache_k"][:, :],
                                                         in_offset=bass.IndirectOffsetOnAxis(ap=idx[:, j:j + 1], axis=0)), R=[idx_b], W=[kpg_b[i]])
            cx.dma("pool", lambda e: e.indirect_dma_start(out=vpg[i][:], out_offset=None, in_=I["cache_v"][:, :],
                                                         in_offset=bass.IndirectOffsetOnAxis(ap=idx[:, j:j + 1], axis=0)), R=[idx_b], W=[vpg_b[i]])
            return kpg[i], kpg_b[i], vpg[i], vpg_b[i]

        def sample_attention():
            cx.op("act", lambda e: e.activation(out=qs_bf[:], in_=qk_tm[:, 0, 0:512], func=AF.Copy, scale=0.125), R=[qk_b], W=[qs_b])
            cx.op("dve", lambda e: e.tensor_copy(out=knew_bf[:], in_=qk_tm[:, 0, 512:1024]), R=[qk_b], W=[new_b])
            cx.op("dve", lambda e: e.tensor_copy(out=vnew_bf[:], in_=v_tm[:, 0, :]), R=[v_b], W=[new_b])
            po, pob = pbank[6], pbank_b[6]
            pl, plb = pbank[7], pbank_b[7]
            for n in range(NS):
                pq, pqb = next_bank()
                cx.op("dve", lambda e: e.tensor_scalar(out=sel_t[:], in0=ones_bf[:], scalar1=ident[:, n:n + 1], scalar2=None, op0=ALU.mult), R=[ones_b, ident_b], W=[sel_b])
                cx.op("pe", lambda e: e.matmul(pq[:, :], lhsT=sel_t[:], rhs=qs_bf[:], start=True, stop=True), R=[sel_b, qs_b], W=[pqb])
                pages = []
                for pg in range(NPG1):
                    if pg < NPG:
                        kt, ktb, vt, vtb = gather_page(n, pg)
                    else:
                        kt, ktb, vt, vtb = knew_bf, new_b, vnew_bf, new_b
                    pr, prb = prod[pg % 2], prod_b[pg % 2]
                    cx.op("dve", lambda e: e.tensor_tensor(out=pr, in0=kt[:], in1=pq[:, :], op=ALU.mult), R=[ktb, pqb], W=[prb])
                    cx.op("dve", lambda e: e.reduce_sum(out=sT[:, pg, :], in_=pr.rearrange("p (g d) -> p g d", d=64), axis=AX.X), R=[prb], W=[sT_b])
                    pages.append((vt, vtb))
                    if pg % 4 == 3 or pg == NPG1 - 1:
                        lo = (pg // 4) * 4
                        cx.op("act", lambda e: e.activation(out=pT[:, lo:pg + 1, :], in_=sT[:, lo:pg + 1, :], func=AF.Exp), R=[sT_b], W=[pT_b])
                        if pg == NPG1 - 1:
                            cx.op("dve", lambda e: e.tensor_scalar(out=pT[:, NPG, :], in0=pT[:, NPG, :], scalar1=ident[:, n:n + 1], scalar2=None, op0=ALU.mult),
                                  R=[pT_b, ident_b], W=[pT_b])
                        for p2 in range(lo, pg + 1):
                            vt2, vtb2 = pages[p2]
                            for h in range(4):
                                cx.op("pe", lambda e: e.matmul(po[:, n * 8 + 2 * h:n * 8 + 2 * h + 2], lhsT=vt2[:, h * 128:(h + 1) * 128], rhs=pT[:, p2, 2 * h:2 * h + 2],
                                                               start=(n == 0 and p2 == 0 and h == 0), stop=(p2 == NPG1 - 1), skip_group_check=True), R=[vtb2, pT_b], W=[pob], inc=True)
                cx.op("dve", lambda e: e.reduce_sum(out=psum8[:], in_=pT[:].rearrange("p g c -> p c g"), axis=AX.X), R=[pT_b], W=[psum8_b])
                cx.op("pe", lambda e: e.matmul(pl[:, n * 8:(n + 1) * 8], lhsT=ones_f32[:], rhs=psum8[:], start=True, stop=True), R=[sconst_b, psum8_b], W=[plb])
            cx.op("act", lambda e: e.copy(out=oS[:].rearrange("p n c -> p (n c)"), in_=po[:, 0:NS * 8]), R=[pob], W=[oS_b])
            cx.op("dve", lambda e: e.reciprocal(out=lS[:].rearrange("p n c -> p (n c)"), in_=pl[:, 0:NS * 8]), R=[plb], W=[lS_b])
            cx.op("dve", lambda e: e.tensor_tensor(out=oS[:], in0=oS[:], in1=lS[:], op=ALU.mult), R=[oS_b, lS_b], W=[oS_b])
            ov = oS[:].rearrange("p n (h j) -> p h n j", j=2)
            cx.op("dve", lambda e: e.scalar_tensor_tensor(out=yaS[:], in0=ov[:, :, :, 1], scalar=lam_t[:, 0:1], in1=ov[:, :, :, 0], op0=ALU.mult, op1=ALU.add),
                  R=[oS_b, lam_b], W=[yaS_b])
            cx.op("dve", lambda e: e.memset(mixT[:, :, :], 0.0), W=[mixT_b])
            for h in range(4):
                cx.op("dve", lambda e: e.memset(ya[:], 0.0), W=[ya_b])
                cx.op("dve", lambda e: e.tensor_copy(out=ya[:, 0:NS], in_=yaS[:, h, :]), R=[yaS_b], W=[ya_b])
                subln_norm(h, TT)

        def expand(nm, half):
            X, Xbuf = raw[nm]
            n0 = half * HN
            cx.op("dve", lambda e: e.tensor_tensor(out=Xe, in0=X[:, :, n0:n0 + HN].unsqueeze(3).to_broadcast([128, 4, HN, 64]),
                                                   in1=diag64[:].unsqueeze(1).unsqueeze(1).to_broadcast([128, 4, HN, 64]), op=ALU.mult),
                  R=[Xbuf, sconst_b], W=[Xe_b])
            for m in range(4):
                pb, pbb = next_bank()
                cx.op("pe", lambda e: e.matmul(pb[:, 0:HN * 64], lhsT=bones_f[:], rhs=Xe[:, m, :, :].rearrange("p n k -> p (n k)"), start=True, stop=True),
                      R=[sconst_b, Xe_b], W=[pbb])
                cx.op("act", lambda e: e.copy(out=Xb[:, :, m, :], in_=pb[:, 0:HN * 64].rearrange("p (n k) -> p n k", k=64)), R=[pbb], W=[Xb_b])

        def sample_wkv():
            for half in range(NS // HN):
                n0 = half * HN
                cx.dma("sp", lambda e: e.dma_start(out=S_t[:], in_=I["swkv"][n0:n0 + HN].rearrange("n (m hp) v k -> (hp v) n m k", hp=2)), W=[S_b])
                vT = xs[:, 8:12, n0:n0 + HN].rearrange("p m n -> p n m")
                expand("kkn", half)
                cx.op("dve", lambda e: e.tensor_tensor(out=T1, in0=S_t[:], in1=Xb[:], op=ALU.mult), R=[S_b, Xb_b], W=[T1_b])
                cx.op("dve", lambda e: e.reduce_sum(out=red[:], in_=T1, axis=AX.X), R=[T1_b], W=[red_b])
                expand("w", half)
                cx.op("dve", lambda e: e.tensor_tensor(out=S_t[:], in0=S_t[:], in1=Xb[:], op=ALU.mult), R=[S_b, Xb_b], W=[S_b])
                expand("b", half)
                cx.op("dve", lambda e: e.tensor_tensor(out=T1, in0=Xb[:], in1=red[:].unsqueeze(3).to_broadcast([128, HN, 4, 64]), op=ALU.mult), R=[Xb_b, red_b], W=[T1_b])
                cx.op("dve", lambda e: e.tensor_tensor(out=S_t[:], in0=S_t[:], in1=T1, op=ALU.subtract), R=[S_b, T1_b], W=[S_b])
                expand("kf", half)
                cx.op("dve", lambda e: e.tensor_tensor(out=T1, in0=Xb[:], in1=vT.unsqueeze(3).to_broadcast([128, HN, 4, 64]), op=ALU.mult), R=[Xb_b, xs_b], W=[T1_b])
                cx.op("dve", lambda e: e.tensor_tensor(out=S_t[:], in0=S_t[:], in1=T1, op=ALU.add), R=[S_b, T1_b], W=[S_b])
                cx.dma("sp", lambda e: e.dma_start(out=O["wkvs"][n0:n0 + HN].rearrange("n (m hp) v k -> (hp v) n m k", hp=2), in_=S_t[:]), R=[S_b])
                expand("r", half)
                cx.op("dve", lambda e: e.tensor_tensor(out=T1, in0=S_t[:], in1=Xb[:], op=ALU.mult), R=[S_b, Xb_b], W=[T1_b])
                cx.op("dve", lambda e: e.reduce_sum(out=yS[:, :, n0:n0 + HN].rearrange("p m n -> p n m"), in_=T1, axis=AX.X), R=[T1_b], W=[yS_b])
            yf = yS[:].rearrange("p m n -> p (m n)")
            NN = 4 * NS
            pm, pmb = next_bank()
            cx.op("pe", lambda e: e.matmul(pm[:, 0:NN], lhsT=bones_f[:], rhs=yf, start=True, stop=True), R=[sconst_b, yS_b], W=[pmb])
            cx.op("dve", lambda e: e.scalar_tensor_tensor(out=gS[0][:], in0=pm[:, 0:NN], scalar=-1.0 / 64, in1=yf, op0=ALU.mult, op1=ALU.add), R=[pmb, yS_b], W=[gS_b[0]])
            cx.op("dve", lambda e: e.tensor_tensor(out=gS[1][:], in0=gS[0][:], in1=gS[0][:], op=ALU.mult), R=[gS_b[0]], W=[gS_b[1]])
            pv, pvb = next_bank()
            cx.op("pe", lambda e: e.matmul(pv[:, 0:NN], lhsT=bones_f[:], rhs=gS[1][:], start=True, stop=True), R=[sconst_b, gS_b[1]], W=[pvb])
            cx.op("act", lambda e: e.activation(out=gS[2][:], in_=pv[:, 0:NN], func=AF.Ln, scale=1.0 / 64, bias=gneps[:]), R=[pvb, eps_b], W=[gS_b[2]])
            cx.op("act", lambda e: e.activation(out=gS[2][:], in_=gS[2][:], func=AF.Exp, scale=-0.5), R=[gS_b[2]], W=[gS_b[2]])
            cx.op("dve", lambda e: e.tensor_tensor(out=gS[0][:], in0=gS[0][:], in1=gS[2][:], op=ALU.mult), R=[gS_b[0], gS_b[2]], W=[gS_b[0]])
            for m in range(4):
                ynm = gS[0][:, m * NS:(m + 1) * NS]
                cx.op("act", lambda e: e.activation(out=f1[:, 0:NS], in_=ynm, func=AF.Identity, scale=C("ln_x_w")[:, m:m + 1], bias=C("ln_x_b")[:, m:m + 1]),
                      R=[gS_b[0], Cb("ln_x_w"), Cb("ln_x_b")], W=[f_b[0]])
                cx.op("dve", lambda e: e.tensor_tensor(out=f2[:, 0:NS], in0=bs_t[:, m, 0:NS], in1=xs[:, 8 + m, 0:NS], op=ALU.mult), R=[bs_b, xs_b], W=[f_b[1]])
                cx.op("dve", lambda e: e.tensor_tensor(out=f1[:, 0:NS], in0=f1[:, 0:NS], in1=f2[:, 0:NS], op=ALU.add), R=[f_b[0], f_b[1]], W=[f_b[0]])
                cx.op("dve", lambda e: e.tensor_tensor(out=mixT[:, 4 + m, 0:NS], in0=f1[:, 0:NS], in1=gT[:, m, 0:NS], op=ALU.mult), R=[f_b[0], gT_b], W=[mixT_b])

        def sample_tile():
            st["moff"] = 0
            load_x_tile(I["xs_pad"], 1)
            ffn(I["ffn1_gate"], I["ffn1_up"], I["ffn1_down"], "n_ffn1_pre", "n_ffn1_post", TT)
            win_stage(TT)
            sub_front(0, I["cosS"], I["sinS"])
            cx.dma("sp", lambda e: e.dma_start(out=O["ks_pad"], in_=qk_tm[:, 0, 512:1024]), R=[qk_b])
            cx.dma("sp", lambda e: e.dma_start(out=O["vs_pad"], in_=v_tm[:, 0, :]), R=[v_b])
            cx.dma("sp", lambda e: e.dma_start(out=O["shs"], in_=pbT[:, :, 1:TT + 1]), R=[pbT_b])
            cx.dma("sp", lambda e: e.dma_start(out=xs[:, :, 0:TT], in_=I["sshT"]), W=[xs_b])
            cx.op("dve", lambda e: e.tensor_tensor(out=xs[:, :, 0:TT], in0=xs[:, :, 0:TT], in1=pbT[:, :, 1:TT + 1], op=ALU.subtract), R=[pbT_b, xs_b], W=[xs_b])
            for j in range(14):
                cx.op("dve", lambda e, j=j: e.scalar_tensor_tensor(out=xs[:, j, 0:TT], in0=xs[:, j, 0:TT], scalar=mu_t[:, j:j + 1], in1=pbT[:, j, 1:TT + 1],
                                                                  op0=ALU.mult, op1=ALU.add), R=[xs_b, mu_b, pbT_b], W=[xs_b])
            sample_attention()
            rwkv_prep(TT, sample=True)
            sample_wkv()
            out_proj(TT)
            ffn(I["ffn2_gate"], I["ffn2_up"], I["ffn2_down"], "n_ffn2_pre", "n_ffn2_post", TT)
            store_x_tile(O["ys_pad"], 1)

        for tf in range(NTF):
            load_x_tile(I["xp"][tf * TF:(tf + 1) * TF, :], TF // 128)
            ffn(I["ffn1_gate"], I["ffn1_up"], I["ffn1_down"], "n_ffn1_pre", "n_ffn1_post", TF)
            win_stage(TF)
            for s in range(NSUB):
                t = tf * NSUB + s
                st["moff"] = s * TT
                nblk = sub_front(s, I["cosT"][t * TT:(t + 1) * TT, :], I["sinT"][t * TT:(t + 1) * TT, :])
                prompt_kv_out(t, nblk)
                shift_mix(TT)
                if t == NT - 1:
                    cx.op("dve", lambda e: e.tensor_copy(out=shc[:], in_=pbT[:, :, TT]), R=[pbT_b], W=[shc_b])
                    cx.dma("sp", lambda e: e.dma_start(out=O["shp"], in_=shc[:]), R=[shc_b])
                cx.op("dve", lambda e: e.tensor_copy(out=pbT[:, :, 0:1], in_=pbT[:, :, TT:TT + 1]), R=[pbT_b], W=[pbT_b])
                prep_head(TT)
                for h in range(4):
                    attention_head(t, h)
                    prep_m(TT, h)
                prep_tail(TT)
                for b in range(NBLK):
                    rwkv_block(b, stage)
                rwkv_finish(TT)
                if t == NT - 1:
                    wkv_out(O["wkvp"])
            out_proj(TF)
            ffn(I["ffn2_gate"], I["ffn2_up"], I["ffn2_down"], "n_ffn2_pre", "n_ffn2_post", TF)
            store_x_tile(O["yp"][tf * TF:(tf + 1) * TF, :], TF // 128)

        if NPOOL > 0:
            sample_tile()
        cx.wait_all_dma("sp")
    return nc


def host_consts(SEQ, pos0=0, past_len=2048):
    TT = 128
    pos = np.arange(SEQ, dtype=np.float32) + pos0
    inv = (500000.0 ** (-np.arange(8, dtype=np.float32) / 8)).astype(np.float32)
    ang = pos[:, None] * inv[None, :]
    cosT = np.tile(np.cos(ang).astype(np.float32), (1, 16))
    sinT = np.tile(np.sin(ang).astype(np.float32), (1, 16))
    kk = np.arange(128)[:, None, None] + 128 * np.arange(TT // 128)[None, :, None]
    qq = np.arange(TT)[None, None, :]
    cmask = (kk <= qq).astype(np.float32)
    resetm = np.ones((128, TT), np.float32); resetm[:, ::64] = 0.0
    i = np.arange(128)[:, None] % 64
    tcol = np.arange(128)[None, :]
    mA = np.where(tcol < 64, i < (tcol % 64), i <= (tcol % 64)).astype(np.float32)
    maskA4 = np.tile(mA, (1, 4))
    mC = (np.arange(64)[None, :] < i).astype(np.float32)
    maskC8 = np.tile(mC, (1, 8))
    I8 = np.tile((np.arange(64)[None, :] == i).astype(np.float32), (1, 8))
    blockones = (np.arange(128)[:, None] // 64 == np.arange(128)[None, :] // 64).astype(np.float32)
    angS = float(past_len) * inv
    cosS = np.tile(np.cos(angS).astype(np.float32)[None, :], (128, 16))
    sinS = np.tile(np.sin(angS).astype(np.float32)[None, :], (128, 16))
    diag64 = (np.arange(64)[None, :] == i).astype(np.float32)
    return {"ident": np.eye(128, dtype=np.float32), "ones": np.ones((128, 128), np.float32),
            "cosS": cosS, "sinS": sinS, "diag64": diag64,
            "cosT": cosT, "sinT": sinT, "cmask": cmask, "resetm": resetm, "maskA4": maskA4, "maskC8": maskC8,
            "I8": I8, "blockones": blockones}


def fm(vec, nchunk):
    return np.ascontiguousarray(np.asarray(vec, np.float32).reshape(nchunk, 128).T)


_SEQ = 4096


def kernel(x_prompt, x_sample, cache_k, cache_v, state_wkv, state_shift, page_table,
           n_ffn1_pre, n_ffn1_post, ffn1_gate, ffn1_up, ffn1_down,
           n_mix_pre, n_mix_post, w_in, w_out,
           lambda_q1, lambda_k1, lambda_q2, lambda_k2, subln,
           mu_shift, w0, w2, a0, a2, g2, k_k, k_a, r_k, ln_x_w, ln_x_b,
           n_ffn2_pre, n_ffn2_post, ffn2_gate, ffn2_up, ffn2_down):
    f = lambda a: np.ascontiguousarray(np.asarray(a, dtype=np.float32))
    x_prompt = f(x_prompt)
    B, SEQ = x_prompt.shape[0], x_prompt.shape[1]
    page_table = np.asarray(page_table).astype(np.int32)
    n_s, NPG = page_table.shape
    NPOOL = np.asarray(cache_k).shape[1]
    assert n_s == 8 * NS and B == 4
    nc = build_program(SEQ, NPOOL, NPG, debug=False)
    consts = host_consts(SEQ, past_len=NPG * PAGE)
    ck = f(cache_k[0]).reshape(NPOOL * PAGE, 512)
    cv_ = f(cache_v[0]).reshape(NPOOL * PAGE, 512)
    xs_all = f(x_sample)[:, 0, :]
    ss_all = f(state_shift[0])
    sw_all = f(state_wkv[0])
    shared = {
        "ffn1_gate": f(ffn1_gate[0]), "ffn1_up": f(ffn1_up[0]), "ffn1_down": f(ffn1_down[0]),
        "ffn2_gate": f(ffn2_gate[0]), "ffn2_up": f(ffn2_up[0]), "ffn2_down": f(ffn2_down[0]),
        "w_in": f(w_in[0]), "w_out": f(w_out[0]),
        "n_ffn1_pre": fm(n_ffn1_pre[0], 8), "n_ffn1_post": fm(n_ffn1_post[0], 8),
        "n_mix_pre": fm(n_mix_pre[0], 8), "n_mix_post": fm(n_mix_post[0], 8),
        "n_ffn2_pre": fm(n_ffn2_pre[0], 8), "n_ffn2_post": fm(n_ffn2_post[0], 8),
        "mu": fm(mu_shift[0], 14),
        "lamv": np.concatenate([f(lambda_q1[0]), f(lambda_k1[0]), f(lambda_q2[0]), f(lambda_k2[0])]).reshape(1, 256),
        "subln": f(subln[0]).reshape(128, 1),
        "w0": fm(w0[0], 4), "a0": fm(a0[0], 4), "k_k": fm(k_k[0], 4), "k_a": fm(k_a[0], 4),
        "r_k": fm(np.asarray(r_k[0]).reshape(-1), 4), "ln_x_w": fm(ln_x_w[0], 4), "ln_x_b": fm(ln_x_b[0], 4),
        "wa2": np.ascontiguousarray(np.concatenate([f(w2[0]), f(a2[0])], axis=0)), "g2": f(g2[0]),
        "cache_k": ck, "cache_v": cv_,
        **consts,
    }
    in_maps = []
    for c in range(8):
        sl = slice(NS * c, NS * (c + 1))
        xs_pad = np.zeros((128, D), np.float32)
        xs_pad[:NS] = xs_all[sl]
        sshT = np.zeros((128, 14, 128), np.float32)
        sshT[:, :, :NS] = ss_all[sl].reshape(NS, 14, 128).transpose(2, 1, 0)
        in_maps.append(dict(shared, xp=x_prompt[c // 2], xs_pad=xs_pad, sshT=sshT,
                            pt_own=np.ascontiguousarray(page_table[sl].reshape(1, NS * NPG)),
                            swkv=np.ascontiguousarray(sw_all[sl])))
    res = run_bass_kernel_spmd(nc, in_maps, core_ids=list(range(8)))
    R_ = res.results
    r = [R_[2 * s] for s in range(B)]
    yp = np.stack([q["yp"] for q in r]).astype(np.float32)
    kp = np.stack([q["kp"].reshape(SEQ, 4, 128) for q in r])[None].astype(np.float32)
    vp = np.stack([q["vp"].reshape(SEQ, 4, 128) for q in r])[None].astype(np.float32)
    wkvp = np.stack([q["wkvp"] for q in r])[None].astype(np.float32)
    shp = np.stack([q["shp"].T.reshape(-1) for q in r])[None].astype(np.float32)
    ys = np.concatenate([R_[c]["ys_pad"][:NS] for c in range(8)])[:, None, :].astype(np.float32)
    ks = np.concatenate([R_[c]["ks_pad"][:NS] for c in range(8)]).reshape(1, n_s, 1, 4, 128).astype(np.float32)
    vs = np.concatenate([R_[c]["vs_pad"][:NS] for c in range(8)]).reshape(1, n_s, 1, 4, 128).astype(np.float32)
    wkvs = np.concatenate([R_[c]["wkvs"] for c in range(8)])[None].astype(np.float32)
    shs = np.concatenate([R_[c]["shs"][:, :, :NS].transpose(2, 1, 0).reshape(NS, SHIFT) for c in range(8)])[None].astype(np.float32)
    return (yp, ys, kp, vp, wkvp, shp, ks, vs, wkvs, shs)
```

```python
import numpy as np
import concourse.bass as bass
import concourse.mybir as mybir
from concourse.bass_utils import run_bass_kernel_spmd

F32 = mybir.dt.float32
BF16 = mybir.dt.bfloat16
I32 = mybir.dt.int32
AF = mybir.ActivationFunctionType
ALU = mybir.AluOpType
AX = mybir.AxisListType

D = 1024
DFF = 2816
NFF = DFF // 128
H_A, DH_A = 4, 64
D_A = 512
H_B, DH_B = 8, 64
D_B = 512
SHIFT = 1792
D_IN = 3 * D_A + SHIFT
PAGE = 128
NORM_EPS = 1e-6
GN_EPS = 64e-5
LAM_INIT = 0.8 - 0.6 * 1.0
NS = 16
CH = 64


class Buf:
    __slots__ = ("name", "w", "r")

    def __init__(self, name):
        self.name = name
        self.w = None
        self.r = {}


class Eng:
    def __init__(self, name, eng, sem, is_pe=False, is_dma=False):
        self.name, self.eng, self.sem = name, eng, sem
        self.cnt = 0
        self.known = {}
        self.is_pe = is_pe
        self.is_dma = is_dma


class Ctx:
    def __init__(self, nc, sems, dma_sems):
        self.nc = nc
        self.semobj = {}
        self.E = {}
        for nm, eng, pe in (("pe", None, True), ("act", None, False), ("dve", None, False), ("pool", None, False), ("sp", None, False)):
            self.semobj[nm] = sems[nm]
            self.E[nm] = Eng(nm, None, sems[nm], is_pe=pe)
        self.dma_pool = dma_sems
        self.dma_i = {q: 0 for q in dma_sems}
        for q, lst in dma_sems.items():
            for i, s in enumerate(lst):
                self.semobj[(q, i)] = s

    def bind(self, name, eng):
        self.E[name].eng = eng

    def _waits(self, E, R, W):
        need = {}
        for b in R:
            if b.w is not None:
                k, v = b.w
                need[k] = max(need.get(k, 0), v)
        for b in W:
            if b.w is not None:
                k, v = b.w
                need[k] = max(need.get(k, 0), v)
            for k, v in b.r.items():
                need[k] = max(need.get(k, 0), v)
        for k, v in need.items():
            if E.known.get(k, 0) >= v:
                continue
            if E.is_pe and k == "pe":
                continue
            E.eng.wait_ge(self.semobj[k], v)
            E.known[k] = v

    def op(self, en, fn, R=(), W=(), inc=True):
        E = self.E[en]
        self._waits(E, R, W)
        ins = fn(E.eng)
        if inc:
            ins.then_inc(E.sem, 1)
            E.cnt += 1
            ev = (en, E.cnt)
        else:
            ev = (en, E.cnt + 1)
        for b in R:
            b.r[ev[0]] = max(b.r.get(ev[0], 0), ev[1])
        for b in W:
            b.w = ev
            b.r = {}
        return ins

    def dma(self, q, fn, R=(), W=()):
        E = self.E[q]
        pool = self.dma_pool[q]
        i = self.dma_i[q]
        self.dma_i[q] += 1
        slot = i % len(pool)
        val = 16 * (i // len(pool) + 1)
        key = (q, slot)
        if val > 16 and E.known.get(key, 0) < val - 16:
            E.eng.wait_ge(pool[slot], val - 16)
            E.known[key] = val - 16
        self._waits(E, R, W)
        ins = fn(E.eng)
        ins.then_inc(pool[slot], 16)
        ev = (key, val)
        for b in R:
            b.r[key] = max(b.r.get(key, 0), val)
        for b in W:
            b.w = ev
            b.r = {}
        return ev

    def wait_all_dma(self, q_wait="sp"):
        E = self.E[q_wait]
        for q, pool in self.dma_pool.items():
            n = self.dma_i[q]
            for slot in range(len(pool)):
                cnt = (n - slot + len(pool) - 1) // len(pool) if n > slot else 0
                if cnt > 0 and E.known.get((q, slot), 0) < 16 * cnt:
                    E.eng.wait_ge(pool[slot], 16 * cnt)

    def barrier(self):
        names = ["pe", "act", "dve", "pool"]
        for a in names:
            Ea = self.E[a]
            for b in names:
                if a == b:
                    continue
                v = self.E[b].cnt
                if v > 0 and Ea.known.get(b, 0) < v:
                    Ea.eng.wait_ge(self.semobj[b], v)
                    Ea.known[b] = v


def build_program(SEQ, NPOOL, NPAGES, debug=False, stage=9):
    TT = 128
    TF = min(512, SEQ)
    NSUB = TF // TT
    NTF = SEQ // TF
    NT = SEQ // TT
    NBLK = TT // 128
    nc = bass.Bass("TRN2", target_bir_lowering=False)
    qkv_s = nc.dram_tensor("qkv_s", [TF, 1536], F32).ap()
    pb_s = nc.dram_tensor("pb_s", [128, 14, TF], F32).ap()
    KT_s = nc.dram_tensor("KT_s", [4, 128, SEQ], BF16).ap()
    V_s = nc.dram_tensor("V_s", [4, 128, SEQ // 128, 128], BF16).ap()
    dt_in = lambda name, shape, dt=F32: nc.dram_tensor(name, list(shape), dt, kind="ExternalInput").ap()
    dt_out = lambda name, shape, dt=F32: nc.dram_tensor(name, list(shape), dt, kind="ExternalOutput").ap()

    I = {}
    I["xp"] = dt_in("xp", [SEQ, D])
    for nm in ("ffn1_gate", "ffn1_up", "ffn2_gate", "ffn2_up"):
        I[nm] = dt_in(nm, [D, DFF])
    for nm in ("ffn1_down", "ffn2_down"):
        I[nm] = dt_in(nm, [DFF, D])
    I["w_in"] = dt_in("w_in", [D, D_IN])
    I["w_out"] = dt_in("w_out", [D, D])
    for nm in ("n_ffn1_pre", "n_ffn1_post", "n_mix_pre", "n_mix_post", "n_ffn2_pre", "n_ffn2_post"):
        I[nm] = dt_in(nm, [128, 8])
    I["ident"] = dt_in("ident", [128, 128])
    I["ones"] = dt_in("ones", [128, 128])

    I["cosT"] = dt_in("cosT", [SEQ, 128])
    I["sinT"] = dt_in("sinT", [SEQ, 128])
    I["mu"] = dt_in("mu", [128, 14])
    I["cmask"] = dt_in("cmask", [128, TT // 128, TT])
    I["lamv"] = dt_in("lamv", [1, 4 * 64])
    I["subln"] = dt_in("subln", [128, 1])
    for nm in ("w0", "a0", "k_k", "k_a", "r_k", "ln_x_w", "ln_x_b"):
        I[nm] = dt_in(nm, [128, 4])
    I["wa2"] = dt_in("wa2", [128, 512])
    I["g2"] = dt_in("g2", [128, 512])
    for nm in ("resetm", "maskA4", "maskC8", "I8", "blockones"):
        I[nm] = dt_in(nm, {"resetm": [128, TT], "maskA4": [128, 512], "maskC8": [128, 512], "I8": [128, 512], "blockones": [128, 128]}[nm])

    NPG = NPAGES
    I["xs_pad"] = dt_in("xs_pad", [128, D])
    I["sshT"] = dt_in("sshT", [128, 14, 128])
    I["pt_own"] = dt_in("pt_own", [1, NS * NPG], I32)
    I["cache_k"] = dt_in("cache_k", [NPOOL * 128, 512])
    I["cache_v"] = dt_in("cache_v", [NPOOL * 128, 512])
    I["swkv"] = dt_in("swkv", [NS, 8, 64, 64])
    I["cosS"] = dt_in("cosS", [128, 128])
    I["sinS"] = dt_in("sinS", [128, 128])
    I["diag64"] = dt_in("diag64", [128, 64])

    O = {}
    O["ys_pad"] = dt_out("ys_pad", [128, D])
    O["ks_pad"] = dt_out("ks_pad", [128, 512])
    O["vs_pad"] = dt_out("vs_pad", [128, 512])
    O["shs"] = dt_out("shs", [128, 14, 128])
    O["wkvs"] = dt_out("wkvs", [NS, 8, 64, 64])
    O["yp"] = dt_out("yp", [SEQ, D])
    O["kp"] = dt_out("kp", [SEQ, 512])
    O["vp"] = dt_out("vp", [SEQ, 512])
    O["shp"] = dt_out("shp", [128, 14])
    O["wkvp"] = dt_out("wkvp", [8, 64, 64])

    from contextlib import ExitStack
    es = ExitStack()
    with es:
        sem = {nm: es.enter_context(nc.semaphore("s_" + nm)) for nm in ("pe", "act", "dve", "pool", "sp")}
        dsem = {q: [es.enter_context(nc.semaphore(f"d_{q}{i}")) for i in range(n)] for q, n in (("sp", 12), ("pool", 12))}
        sb = lambda name, shape, dt=F32: es.enter_context(nc.sbuf_tensor(name, list(shape), dt))
        ps = lambda name, shape, dt=F32: es.enter_context(nc.psum_tensor(name, list(shape), dt))

        ident = sb("ident_t", [128, 128]); ident_b = Buf("ident")
        ones_bf = sb("ones_bf", [128, 128], BF16); ones_b = Buf("ones")
        gains = {nm: (sb("g_" + nm, [128, 8]), Buf(nm)) for nm in ("n_ffn1_pre", "n_ffn1_post", "n_mix_pre", "n_mix_post", "n_ffn2_pre", "n_ffn2_post")}
        eps_t = sb("eps_t", [128, 1]); eps_b = Buf("eps")
        xT = sb("xT", [128, 8, TF]); xT_b = Buf("xT")
        xtok = sb("xtok", [128, NBLK, D]); xtok_b = Buf("xtok")
        z = sb("z", [128, 8, TF]); z_b = Buf("z")
        u = sb("u", [128, 8, TF], BF16); u_b = Buf("u")
        hid = sb("hid", [128, NFF, TF], BF16); hid_b = [Buf(f"hid{f}") for f in range(NFF)]
        sq = hid[:, 0:8, :]; sq_b = Buf("sq")
        rstd = sb("rstd", [128, TF]); rstd_b = Buf("rstd")
        sg = [sb(f"sg{i}", [128, TF], BF16) for i in range(2)]; sg_b = [Buf(f"sg{i}") for i in range(2)]
        NW = 4
        wring = [sb(f"wr{i}", [128, 8, 256], BF16) for i in range(NW)]; wring_b = [Buf(f"wr{i}") for i in range(NW)]
        ND = 2
        dring = [sb(f"dr{i}", [128, NFF, 128], BF16) for i in range(ND)]; dring_b = [Buf(f"dr{i}") for i in range(ND)]
        pbank = [ps(f"pb{i}", [128, 512]) for i in range(8)]; pbank_b = [Buf(f"pb{i}") for i in range(8)]

        NKB = SEQ // 128
        qk_tm = xtok; qk_b = xtok_b
        v_tm = sb("v_tm", [128, NBLK, 512]); v_b = Buf("v_tm")
        cos_t = sb("cos_t", [128, NBLK, 128]); sin_t = sb("sin_t", [128, NBLK, 128]); cs_b = Buf("cs")
        rt = [sb(f"rt{i}", [128, 16, 8]) for i in range(4)]; rt_b = [Buf(f"rt{i}") for i in range(4)]
        KC = 1024
        kring = [sb(f"kring{i}", [128, KC], BF16) for i in range(2)]; kring_b = [Buf(f"kring{i}") for i in range(2)]
        vring = [sb(f"vring{i}", [128, KC // 128, 128], BF16) for i in range(2)]; vring_b = [Buf(f"vring{i}") for i in range(2)]
        ktn = sb("ktn", [128, 4, 128], BF16); ktn_b = Buf("ktn")
        vbn = sb("vbn", [128, 4, 128], BF16); vbn_b = Buf("vbn")
        stg = [sb(f"stg{i}", [128, 256]) for i in range(3)]; stg_b = [Buf(f"stg{i}") for i in range(3)]
        stg2 = [sb("stg20", [128, TF])] * 2; stg2_b = [Buf("stg20")] * 2
        qkv_sb = Buf("qkv_s"); pb_sb = Buf("pb_s"); KTs_b = Buf("KT_s"); Vs_b = Buf("V_s")
        QT = sb("QT", [128, 4, TT], BF16); QT_b = Buf("QT")
        pbT = sb("pbT", [128, 14, TT + 1]); pbT_b = Buf("pbT")
        xs = sb("xs", [128, 14, TT]); xs_b = Buf("xs")
        mu_t = sb("mu_t", [128, 14]); mu_b = Buf("mu")
        cmask = sb("cmask_t", [128, NBLK, TT], BF16); cmask_b = Buf("cmask")
        PT = [sb(f"PT{i}", [128, 4 * TT], BF16) for i in range(2)]; PT_b = [Buf(f"PT{i}") for i in range(2)]
        o0 = sb("o0", [128, TT]); o0_b = Buf("o0")
        rl = sb("rl", [128, TT]); rl_b = Buf("rl")
        ya = sb("ya", [128, TT]); ya_b = Buf("ya")
        mixT = hid[:, 8:16, :]; mixT_b = Buf("mixT")
        lamrow = sb("lamrow", [1, 4 * 64]); lamrow_b = Buf("lamrow")
        lam1 = sb("lam1", [1, 4]); lam1_b = Buf("lam1")
        ones_f = sb("ones_f", [1, 128]); onesf_b = Buf("ones_f")
        lam_t = sb("lam_t", [128, 1]); lam_b = Buf("lam_t")
        subln_t = sb("subln_t", [128, 1]); subln_b = Buf("subln")
        eps128 = sb("eps128", [128, 1])
        shc = sb("shc", [128, 14]); shc_b = Buf("shc")
        NCH = TT // 64
        cv = {nm: (sb("c_" + nm, [128, 4]), Buf("c_" + nm)) for nm in ("w0", "a0", "k_k", "k_a", "r_k", "ln_x_w", "ln_x_b", "nw0", "omka", "na0")}
        wa2 = sb("wa2_t", [128, 512], BF16); wa2_b = Buf("wa2")
        g2t = sb("g2_t", [128, 512], BF16); g2_b = Buf("g2")
        resetm = sb("resetm_t", [128, TT]); maskA4 = sb("maskA4_t", [128, 512]); maskC8 = sb("maskC8_t", [128, 512])
        I8 = sb("I8_t", [128, 512]); bones = sb("bones_t", [128, 128], BF16); cst_b = Buf("scan_consts")
        lora_in = sb("lora_in", [128, TT], BF16); lora_b = Buf("lora_in")
        sgx = sb("sgx", [128, TT], BF16); sgx_b = Buf("sgx")
        ld = sb("ld", [128, 4, TT]); ld_b = Buf("ld")
        a_t = sb("a_t", [128, 4, TT]); a_b = Buf("a_t")
        gT = sb("gT", [128, 4, TT]); gT_b = Buf("gT")
        bs_t = sb("bs_t", [128, 4, TT]); bs_b = Buf("bs_t")
        f1 = sb("f1", [128, TT]); f2 = sb("f2", [128, TT]); f3 = sb("f3", [128, TT]); f4 = sb("f4", [128, TT]); f5 = sb("f5", [128, TT]); f6 = sb("f6", [128, TT])
        f_b = [Buf(f"f{i}") for i in range(6)]
        fbf = sb("fbf", [128, TT], BF16); fbf_b = Buf("fbf")
        gC = sb("gC", [128, 4, NCH]); gC_b = Buf("gC")
        kr = sb("kr", [128, 4, NCH, 128], BF16); kr_b = Buf("kr")
        kb_ = sb("kb_", [128, 4, NCH, 128], BF16); kb_b = Buf("kb")
        khat = z[:, 4:8, :]; khat_b = z_b
        bhat = v_tm[:, 0, :].rearrange("p (a b) -> p a b", a=4); bhat_b = v_b
        V_tm = sb("V_tm", [128, NBLK, 512], BF16); Vtm_b = Buf("V_tm")
        Kh_tm = sb("Kh_tm", [128, NBLK, 512], BF16); Kh_b = Buf("Kh_tm")
        Bh_tm = sb("Bh_tm", [128, NBLK, 512], BF16); Bh_b = Buf("Bh_tm")
        Am = sb("Am", [128, 8, 128], BF16); Am_b = Buf("Am")
        Bm = sb("Bm", [128, 8, 128], BF16); Bm_b = Buf("Bm")
        Zb = [sb(f"Zb{i}", [128, 8, 64], BF16) for i in range(2)]; Zb_b = [Buf(f"Zb{i}") for i in range(2)]
        ZTb = [sb(f"ZTb{i}", [128, 8, 64], BF16) for i in range(2)]; ZTb_b = [Buf(f"ZTb{i}") for i in range(2)]
        Wf = sb("Wf", [128, 8, 64]); Wf_b = Buf("Wf")
        Wb = sb("Wb", [128, 8, 64], BF16); Wb_b = Buf("Wb")
        Rb = sb("Rb", [128, 8, 64], BF16); Rb_b = Buf("Rb")
        Ub = sb("Ub", [128, 8, 64], BF16); Ub_b = Buf("Ub")
        Af = sb("Af", [128, 4, 64]); Af_b = Buf("Af")
        Ab = sb("Ab", [128, 4, 64], BF16); Ab_b = Buf("Ab")
        y_tm = sb("y_tm", [128, NBLK, 512]); ytm_b = Buf("y_tm")
        ysq = z[:, 0:4, :].rearrange("p a (b c) -> p b (a c)", b=NBLK) if False else None; ysq_b = z_b
        gst = sb("gst", [128, 4, NBLK * 8]); gst_b = Buf("gst")
        gneps = sb("gneps", [128, 1])
        wkvT = y_tm[0:64, 0, :].rearrange("p (h k) -> p h k", k=64); wkvT_b = ytm_b

        NPG1 = NPG + 1
        sel_t = sb("sel_t", [128, 128], BF16); sel_b = Buf("sel")
        diag64 = sb("diag64_t", [128, 64]); bones_f = sb("bones_f", [128, 128]); sconst_b = Buf("sconst")
        ptb = sb("ptb", [128, NS * NPG], I32); idx = sb("idx", [128, NS * NPG], I32); iota_c = sb("iota_c", [128, 1], I32); idx_b = Buf("idx")
        NKR = 4
        kpg = [sb(f"kpg{i}", [128, 512], BF16) for i in range(NKR)]; kpg_b = [Buf(f"kpg{i}") for i in range(NKR)]
        vpg = [sb(f"vpg{i}", [128, 512], BF16) for i in range(NKR)]; vpg_b = [Buf(f"vpg{i}") for i in range(NKR)]
        qs_bf = sb("qs_bf", [128, 512], BF16); qs_b = Buf("qs_bf")
        knew_bf = sb("knew_bf", [128, 512], BF16); vnew_bf = sb("vnew_bf", [128, 512], BF16); new_b = Buf("newkv")
        prod = [ld[:].rearrange("p a b -> p (a b)"), a_t[:].rearrange("p a b -> p (a b)")]; prod_b = [ld_b, a_b]
        sT = sb("sT", [128, NPG1, 8]); sT_b = Buf("sT")
        pT = sb("pT", [128, NPG1, 8], BF16); pT_b = Buf("pT")
        psum8 = sb("psum8", [128, 8]); psum8_b = Buf("psum8")
        ones_f32 = sb("ones_f32", [128, 128])
        oS = sb("oS", [128, NS, 8]); oS_b = Buf("oS")
        lS = sb("lS", [128, NS, 8]); lS_b = Buf("lS")
        yaS = sb("yaS", [128, 4, NS]); yaS_b = Buf("yaS")
        raw = {nm: (sb("raw_" + nm, [128, 4, NS]), Buf("raw_" + nm)) for nm in ("kkn", "kf", "b", "w", "r")}
        HN = NS // 8
        S_t = sb("S_t", [128, HN, 4, 64]); S_b = Buf("S_t")
        Xb = sb("Xb", [128, HN, 4, 64]); Xb_b = Buf("Xb")
        Xe = xtok[:, 0, 0:4 * HN * 64].rearrange("p (m n k) -> p m n k", m=4, k=64); Xe_b = xtok_b
        T1 = z[:].rearrange("p a b -> p (a b)")[:, 0:HN * 256].rearrange("p (n m k) -> p n m k", m=4, k=64); T1_b = z_b
        red = sb("red", [128, HN, 4]); red_b = Buf("red")
        yS = sb("yS", [128, 4, NS]); yS_b = Buf("yS")
        gS = [sb(f"gS{i}", [128, 4 * NS]) for i in range(3)]; gS_b = [Buf(f"gS{i}") for i in range(3)]

        blk = es.enter_context(nc.Block())
        cx = Ctx(nc, sem, dsem)
        cx.bind("pe", nc.tensor); cx.bind("act", nc.scalar); cx.bind("dve", nc.vector)
        cx.bind("pool", nc.gpsimd); cx.bind("sp", nc.sync)
        st = {"w": 0, "d": 0, "pb": 0}

        cx.dma("sp", lambda e: e.dma_start(out=ident[:], in_=I["ident"]), W=[ident_b])
        cx.dma("pool", lambda e: e.dma_start(out=ones_bf[:], in_=I["ones"]), W=[ones_b])
        for nm, (t, b) in gains.items():
            cx.dma("sp", lambda e, t=t, nm=nm: e.dma_start(out=t[:], in_=I[nm]), W=[b])
        cx.op("dve", lambda e: e.memset(eps_t[:], NORM_EPS), W=[eps_b])

        def load_wpiece(W, col0, ncols=256, row0=0):
            i = st["w"] % NW
            st["w"] += 1
            src = W[row0:row0 + 1024, col0:col0 + ncols].rearrange("(c p) n -> p c n", p=128)
            cx.dma("pool", lambda e: e.dma_start(out=wring[i][:, :, 0:ncols], in_=src), W=[wring_b[i]])
            return wring[i], wring_b[i]

        def load_dpiece(W, col0):
            i = st["d"] % ND
            st["d"] += 1
            src = W[:, col0:col0 + 128].rearrange("(c p) n -> p c n", p=128)
            cx.dma("pool", lambda e: e.dma_start(out=dring[i][:], in_=src), W=[dring_b[i]])
            return dring[i], dring_b[i]

        def next_bank():
            i = st["pb"] % 6
            st["pb"] += 1
            return pbank[i], pbank_b[i]

        def rms_stats(src, src_b, N):
            cx.op("act", lambda e: e.activation(out=sq[:, :, 0:N], in_=src[:, :, 0:N], func=AF.Square), R=[src_b], W=[sq_b])
            pb, pbb = next_bank()
            for c in range(8):
                cx.op("pe", lambda e, c=c: e.matmul(pb[:, 0:N], lhsT=ones_bf[:], rhs=sq[:, c, 0:N], start=(c == 0), stop=(c == 7)),
                      R=[ones_b, sq_b], W=[pbb], inc=(c == 7))
            cx.op("act", lambda e: e.activation(out=rstd[:, 0:N], in_=pb[:, 0:N], func=AF.Ln, scale=1.0 / D, bias=eps_t[:]),
                  R=[pbb, eps_b], W=[rstd_b])
            cx.op("act", lambda e: e.activation(out=rstd[:, 0:N], in_=rstd[:, 0:N], func=AF.Exp, scale=-0.5), R=[rstd_b], W=[rstd_b])

        def pre_norm(src, src_b, gname, N):
            rms_stats(src, src_b, N)
            g, gb = gains[gname]
            for c in range(8):
                cx.op("dve", lambda e, c=c: e.scalar_tensor_tensor(out=u[:, c, 0:N], in0=src[:, c, 0:N], scalar=g[:, c:c + 1],
                                                                  in1=rstd[:, 0:N], op0=ALU.mult, op1=ALU.mult),
                      R=[src_b, gb, rstd_b], W=[u_b])

        def post_norm_add(gname, N, half):
            rms_stats(z, z_b, N)
            g, gb = gains[gname]
            for c in range(8):
                cx.op("dve", lambda e, c=c: e.scalar_tensor_tensor(out=z[:, c, 0:N], in0=z[:, c, 0:N], scalar=g[:, c:c + 1],
                                                                  in1=rstd[:, 0:N], op0=ALU.mult, op1=ALU.mult),
                      R=[z_b, gb, rstd_b], W=[z_b])
            cx.op("dve", lambda e: e.scalar_tensor_tensor(out=xT[:, :, 0:N], in0=z[:, :, 0:N], scalar=(0.5 if half else 1.0),
                                                          in1=xT[:, :, 0:N], op0=ALU.mult, op1=ALU.add),
                  R=[z_b, xT_b], W=[xT_b])

        def ffn(Wg, Wu, Wd, gpre, gpost, N):
            pre_norm(xT, xT_b, gpre, N)
            for f2 in range(NFF // 2):
                wg, wgb = load_wpiece(Wg, f2 * 256)
                wu, wub = load_wpiece(Wu, f2 * 256)
                for ff in range(2):
                    f = f2 * 2 + ff
                    pg, pgb = next_bank()
                    pu, pub = next_bank()
                    for c in range(8):
                        cx.op("pe", lambda e, c=c: e.matmul(pg[:, 0:N], lhsT=wg[:, c, ff * 128:(ff + 1) * 128], rhs=u[:, c, 0:N],
                                                            start=(c == 0), stop=(c == 7)), R=[wgb, u_b], W=[pgb], inc=(c == 7))
                    for c in range(8):
                        cx.op("pe", lambda e, c=c: e.matmul(pu[:, 0:N], lhsT=wu[:, c, ff * 128:(ff + 1) * 128], rhs=u[:, c, 0:N],
                                                            start=(c == 0), stop=(c == 7)), R=[wub, u_b], W=[pub], inc=(c == 7))
                    s, sbb = sg[f % 2], sg_b[f % 2]
                    cx.op("act", lambda e: e.activation(out=s[:, 0:N], in_=pg[:, 0:N], func=AF.Silu), R=[pgb], W=[sbb])
                    cx.op("dve", lambda e, f=f: e.tensor_tensor(out=hid[:, f, 0:N], in0=s[:, 0:N], in1=pu[:, 0:N], op=ALU.mult),
                          R=[sbb, pub], W=[hid_b[f]] + ([sq_b] if f < 8 else []) + ([mixT_b] if 8 <= f < 16 else []))
            for m in range(8):
                wd, wdb = load_dpiece(Wd, m * 128)
                pz, pzb = next_bank()
                for f in range(NFF):
                    cx.op("pe", lambda e, f=f: e.matmul(pz[:, 0:N], lhsT=wd[:, f, :], rhs=hid[:, f, 0:N], start=(f == 0), stop=(f == NFF - 1)),
                          R=[wdb, hid_b[f]] + ([sq_b] if f < 8 else []) + ([mixT_b] if 8 <= f < 16 else []), W=[pzb], inc=(f == NFF - 1))
                cx.op("act", lambda e, m=m: e.copy(out=z[:, m, 0:N], in_=pz[:, 0:N]), R=[pzb], W=[z_b])
            post_norm_add(gpost, N, True)

        def load_x_tile(src_rows, nblk):
            for b in range(nblk):
                cx.dma("sp", lambda e: e.dma_start(out=xtok[:, 0, :], in_=src_rows[b * 128:(b + 1) * 128, :]), W=[xtok_b])
                for c2 in range(2):
                    pb, pbb = next_bank()
                    for cc in range(4):
                        c = c2 * 4 + cc
                        cx.op("pe", lambda e, c=c, cc=cc: e.transpose(pb[:, cc * 128:(cc + 1) * 128], xtok[:, 0, c * 128:(c + 1) * 128], ident[:]),
                              R=[xtok_b, ident_b], W=[pbb], inc=(cc == 3))
                    cx.op("act", lambda e: e.copy(out=xT[:, c2 * 4:(c2 + 1) * 4, b * 128:(b + 1) * 128], in_=pb[:, :].rearrange("p (c t) -> p c t", t=128)), R=[pbb], W=[xT_b])

        def store_x_tile(dst_rows, nblk):
            for b in range(nblk):
                for c2 in range(2):
                    pb, pbb = next_bank()
                    for cc in range(4):
                        c = c2 * 4 + cc
                        cx.op("pe", lambda e, c=c, cc=cc: e.transpose(pb[:, cc * 128:(cc + 1) * 128], xT[:, c, b * 128:(b + 1) * 128], ident[:]),
                              R=[xT_b, ident_b], W=[pbb], inc=(cc == 3))
                    cx.op("act", lambda e: e.copy(out=xtok[:, 0, c2 * 512:(c2 + 1) * 512], in_=pb[:, :]), R=[pbb], W=[xtok_b])
                cx.dma("sp", lambda e: e.dma_start(out=dst_rows[b * 128:(b + 1) * 128, :], in_=xtok[:, 0, :]), R=[xtok_b])

        cx.dma("sp", lambda e: e.dma_start(out=mu_t[:], in_=I["mu"]), W=[mu_b])
        cx.dma("pool", lambda e: e.dma_start(out=cmask[:], in_=I["cmask"]), W=[cmask_b])
        cx.dma("sp", lambda e: e.dma_start(out=lamrow[:], in_=I["lamv"]), W=[lamrow_b])
        cx.dma("sp", lambda e: e.dma_start(out=subln_t[:], in_=I["subln"]), W=[subln_b])
        cx.op("dve", lambda e: e.memset(ones_f[:], 1.0), W=[onesf_b])
        cx.op("dve", lambda e: e.memset(eps128[:], NORM_EPS), W=[eps_b])
        cx.op("dve", lambda e: e.memset(pbT[:, :, 0:1], 0.0), W=[pbT_b])
        lr = lamrow[:].rearrange("p (a d) -> p a d", d=64)
        cx.op("dve", lambda e: e.tensor_tensor(out=lr[:, 0:1, :], in0=lr[:, 0:1, :], in1=lr[:, 1:2, :], op=ALU.mult), R=[lamrow_b], W=[lamrow_b])
        cx.op("dve", lambda e: e.tensor_tensor(out=lr[:, 2:3, :], in0=lr[:, 2:3, :], in1=lr[:, 3:4, :], op=ALU.mult), R=[lamrow_b], W=[lamrow_b])
        cx.op("dve", lambda e: e.reduce_sum(out=lam1[:, 0:1], in_=lr[:, 0, :], axis=AX.X), R=[lamrow_b], W=[lam1_b])
        cx.op("dve", lambda e: e.reduce_sum(out=lam1[:, 1:2], in_=lr[:, 2, :], axis=AX.X), R=[lamrow_b], W=[lam1_b])
        cx.op("act", lambda e: e.activation(out=lam1[:, 2:4], in_=lam1[:, 0:2], func=AF.Exp), R=[lam1_b], W=[lam1_b])
        cx.op("dve", lambda e: e.tensor_tensor(out=lam1[:, 0:1], in0=lam1[:, 2:3], in1=lam1[:, 3:4], op=ALU.subtract), R=[lam1_b], W=[lam1_b])
        cx.op("dve", lambda e: e.tensor_scalar(out=lam1[:, 0:1], in0=lam1[:, 0:1], scalar1=LAM_INIT, scalar2=-1.0, op0=ALU.add, op1=ALU.mult), R=[lam1_b], W=[lam1_b])
        _pb, _pbb = next_bank()
        cx.op("pe", lambda e: e.matmul(_pb[:, 0:1], lhsT=ones_f[:], rhs=lam1[:, 0:1], start=True, stop=True), R=[onesf_b, lam1_b], W=[_pbb])
        cx.op("act", lambda e: e.copy(out=lam_t[:], in_=_pb[:, 0:1]), R=[_pbb], W=[lam_b])

        def win_stage(N):
            nb = N // 128
            pre_norm(xT, xT_b, "n_mix_pre", N)
            for pi in range(6):
                wp, wpb = load_wpiece(I["w_in"], pi * 256)
                for b in range(nb):
                    pb, pbb = next_bank()
                    for c in range(8):
                        cx.op("pe", lambda e, c=c: e.matmul(pb[:, 0:256], lhsT=u[:, c, b * 128:(b + 1) * 128], rhs=wp[:, c, :], start=(c == 0), stop=(c == 7)),
                              R=[wpb, u_b], W=[pbb], inc=(c == 7))
                    k = st.get("stg", 0) % 3
                    st["stg"] = st.get("stg", 0) + 1
                    cx.op("act", lambda e: e.copy(out=stg[k][:], in_=pb[:, 0:256]), R=[pbb], W=[stg_b[k]])
                    cx.dma("sp", lambda e: e.dma_start(out=qkv_s[b * 128:(b + 1) * 128, pi * 256:(pi + 1) * 256], in_=stg[k][:]), R=[stg_b[k]], W=[qkv_sb])
            for pi in range(7):
                wp, wpb = load_wpiece(I["w_in"], 1536 + pi * 256)
                for hf in range(2):
                    j = pi * 2 + hf
                    pb, pbb = next_bank()
                    for c in range(8):
                        cx.op("pe", lambda e, c=c: e.matmul(pb[:, 0:N], lhsT=wp[:, c, hf * 128:(hf + 1) * 128], rhs=u[:, c, 0:N], start=(c == 0), stop=(c == 7)),
                              R=[wpb, u_b], W=[pbb], inc=(c == 7))
                    k = j % 2
                    cx.op("act", lambda e: e.copy(out=stg2[k][:, 0:N], in_=pb[:, 0:N]), R=[pbb], W=[stg2_b[k]])
                    cx.dma("sp", lambda e: e.dma_start(out=pb_s[:, j, 0:N], in_=stg2[k][:, 0:N]), R=[stg2_b[k]], W=[pb_sb])

        def sub_front(s, cos_src, sin_src):
            N = TT
            nblk = 1
            cx.dma("sp", lambda e: e.dma_start(out=qk_tm[:, 0, :], in_=qkv_s[s * 128:(s + 1) * 128, 0:1024]), R=[qkv_sb], W=[qk_b])
            cx.dma("sp", lambda e: e.dma_start(out=v_tm[:, 0, :], in_=qkv_s[s * 128:(s + 1) * 128, 1024:1536]), R=[qkv_sb], W=[v_b])
            cx.dma("sp", lambda e: e.dma_start(out=pbT[:, :, 1:N + 1], in_=pb_s[:, :, s * 128:(s + 1) * 128]), R=[pb_sb], W=[pbT_b])
            cx.dma("sp", lambda e: e.dma_start(out=cos_t[:, 0:nblk, :], in_=cos_src.rearrange("(b p) d -> p b d", p=128)), W=[cs_b])
            cx.dma("sp", lambda e: e.dma_start(out=sin_t[:, 0:nblk, :], in_=sin_src.rearrange("(b p) d -> p b d", p=128)), W=[cs_b])
            for b in range(nblk):
                xv = qk_tm[:, b, :].rearrange("p (g d) -> p g d", d=64)
                cv_ = cos_t[:, b, :].rearrange("p (g d) -> p g d", d=8)
                sv = sin_t[:, b, :].rearrange("p (g d) -> p g d", d=8)
                x1, x2 = xv[:, :, 0:8], xv[:, :, 8:16]
                cx.op("dve", lambda e: e.tensor_tensor(out=rt[0][:], in0=x1, in1=cv_, op=ALU.mult), R=[qk_b, cs_b], W=[rt_b[0]])
                cx.op("dve", lambda e: e.tensor_tensor(out=rt[1][:], in0=x2, in1=sv, op=ALU.mult), R=[qk_b, cs_b], W=[rt_b[1]])
                cx.op("dve", lambda e: e.tensor_tensor(out=rt[2][:], in0=x2, in1=cv_, op=ALU.mult), R=[qk_b, cs_b], W=[rt_b[2]])
                cx.op("dve", lambda e: e.tensor_tensor(out=rt[3][:], in0=x1, in1=sv, op=ALU.mult), R=[qk_b, cs_b], W=[rt_b[3]])
                cx.op("dve", lambda e: e.tensor_tensor(out=x1, in0=rt[0][:], in1=rt[1][:], op=ALU.subtract), R=[rt_b[0], rt_b[1]], W=[qk_b])
                cx.op("dve", lambda e: e.tensor_tensor(out=x2, in0=rt[2][:], in1=rt[3][:], op=ALU.add), R=[rt_b[2], rt_b[3]], W=[qk_b])
            return nblk

        def prompt_kv_out(t, nblk):
            cx.dma("sp", lambda e: e.dma_start(out=O["kp"][t * TT:(t + 1) * TT, :].rearrange("(b p) d -> p b d", p=128), in_=qk_tm[:, 0:nblk, 512:1024]), R=[qk_b])
            cx.dma("sp", lambda e: e.dma_start(out=O["vp"][t * TT:(t + 1) * TT, :].rearrange("(b p) d -> p b d", p=128), in_=v_tm[:, 0:nblk, :]), R=[v_b])
            for h in range(4):
                pb, pbb = next_bank()
                for b in range(nblk):
                    cx.op("pe", lambda e, b=b: e.transpose(pb[:, b * 128:(b + 1) * 128], qk_tm[:, b, h * 128:(h + 1) * 128], ident[:]),
                          R=[qk_b, ident_b], W=[pbb], inc=(b == nblk - 1))
                cx.op("act", lambda e: e.activation(out=QT[:, h, 0:nblk * 128], in_=pb[:, 0:nblk * 128], func=AF.Copy, scale=0.125), R=[pbb], W=[QT_b])
                pb2, pbb2 = next_bank()
                for b in range(nblk):
                    cx.op("pe", lambda e, b=b: e.transpose(pb2[:, b * 128:(b + 1) * 128], qk_tm[:, b, 512 + h * 128:512 + (h + 1) * 128], ident[:]),
                          R=[qk_b, ident_b], W=[pbb2], inc=(b == nblk - 1))
                cx.op("act", lambda e: e.copy(out=ktn[:, h, :], in_=pb2[:, 0:128]), R=[pbb2], W=[ktn_b])
            cx.op("dve", lambda e: e.tensor_copy(out=vbn[:].rearrange("p h e -> p (h e)"), in_=v_tm[:, 0, :]), R=[v_b], W=[vbn_b])
            cx.dma("sp", lambda e: e.dma_start(out=KT_s[:, :, t * 128:(t + 1) * 128].rearrange("h p k -> p h k"), in_=ktn[:]), R=[ktn_b], W=[KTs_b])
            cx.dma("sp", lambda e: e.dma_start(out=V_s[:, :, t, :].rearrange("h p e -> p h e"), in_=vbn[:]), R=[vbn_b], W=[Vs_b])

        def pb_proj(N):
            for pi in range(7):
                wp, wpb = load_wpiece(I["w_in"], 1536 + pi * 256)
                for hf in range(2):
                    j = pi * 2 + hf
                    pb, pbb = next_bank()
                    for c in range(8):
                        cx.op("pe", lambda e, c=c: e.matmul(pb[:, 0:N], lhsT=wp[:, c, hf * 128:(hf + 1) * 128], rhs=u[:, c, 0:N], start=(c == 0), stop=(c == 7)),
                              R=[wpb, u_b], W=[pbb], inc=(c == 7))
                    cx.op("act", lambda e, j=j: e.copy(out=pbT[:, j, 1:N + 1], in_=pb[:, 0:N]), R=[pbb], W=[pbT_b])

        def shift_mix(N):
            cx.op("dve", lambda e: e.tensor_tensor(out=xs[:, :, 0:N], in0=pbT[:, :, 0:N], in1=pbT[:, :, 1:N + 1], op=ALU.subtract), R=[pbT_b], W=[xs_b])
            for j in range(14):
                cx.op("dve", lambda e, j=j: e.scalar_tensor_tensor(out=xs[:, j, 0:N], in0=xs[:, j, 0:N], scalar=mu_t[:, j:j + 1], in1=pbT[:, j, 1:N + 1],
                                                                  op0=ALU.mult, op1=ALU.add), R=[xs_b, mu_b, pbT_b], W=[xs_b])

        def prompt_attention(t):
            for h in range(4):
                attention_head(t, h)

        def attention_head(t, h):
            nkb = (t + 1) * NBLK
            nch = (nkb + 7) // 8
            if True:
                for j in range(2):
                    po, pob = pbank[6], pbank_b[6]
                    pl, plb = pbank[7], pbank_b[7]
                    for ch in range(nch):
                        nb = min(8, nkb - ch * 8)
                        ri = st.get("kv", 0) % 2
                        st["kv"] = st.get("kv", 0) + 1
                        kc, kcb, vc, vcb = kring[ri], kring_b[ri], vring[ri], vring_b[ri]
                        cx.dma("sp", lambda e: e.dma_start(out=kc[:, 0:nb * 128], in_=KT_s[h, :, ch * KC:ch * KC + nb * 128]), R=[KTs_b], W=[kcb])
                        cx.dma("sp", lambda e: e.dma_start(out=vc[:, 0:nb, :], in_=V_s[h, :, ch * 8:ch * 8 + nb, :]), R=[Vs_b], W=[vcb])
                        ngr = (nb + 3) // 4
                        prev = None
                        for g in range(ngr + 1):
                            if g < ngr:
                                k0 = g * 4
                                ng = min(4, nb - k0)
                                pS, pSb = next_bank()
                                for i in range(ng):
                                    kl = k0 + i
                                    cx.op("pe", lambda e: e.matmul(pS[:, i * TT:(i + 1) * TT], lhsT=kc[64 * j:64 * j + 64, kl * 128:(kl + 1) * 128], rhs=QT[64 * j:64 * j + 64, h, :],
                                                                   start=True, stop=True), R=[kcb, QT_b], W=[pSb], inc=(i == ng - 1))
                                pi_ = st.get("pt", 0) % 2
                                st["pt"] = st.get("pt", 0) + 1
                                p, pbf = PT[pi_], PT_b[pi_]
                                cx.op("act", lambda e: e.activation(out=p[:, 0:ng * TT], in_=pS[:, 0:ng * TT], func=AF.Exp), R=[pSb], W=[pbf])
                                for i in range(ng):
                                    r = ch * 8 + k0 + i - t * NBLK
                                    if r >= 0:
                                        cx.op("dve", lambda e: e.tensor_tensor(out=p[:, i * TT:(i + 1) * TT], in0=p[:, i * TT:(i + 1) * TT], in1=cmask[:, r, :], op=ALU.mult),
                                              R=[pbf, cmask_b], W=[pbf])
                                cur = (k0, ng, p, pbf)
                            else:
                                cur = None
                            if prev is not None:
                                k00, ng0, p0, pbf0 = prev
                                for i in range(ng0):
                                    kl0 = k00 + i
                                    kb0 = ch * 8 + kl0
                                    cx.op("pe", lambda e: e.matmul(po[:, 0:TT], lhsT=vc[:, kl0, :], rhs=p0[:, i * TT:(i + 1) * TT], start=(kb0 == 0), stop=(kb0 == nkb - 1)),
                                          R=[vcb, pbf0], W=[pob], inc=False)
                                    cx.op("pe", lambda e: e.matmul(pl[:, 0:TT], lhsT=ones_bf[:], rhs=p0[:, i * TT:(i + 1) * TT], start=(kb0 == 0), stop=(kb0 == nkb - 1)),
                                          R=[ones_b, pbf0], W=[plb], inc=True)
                            prev = cur
                    cx.op("dve", lambda e: e.reciprocal(out=rl[:], in_=pl[:, 0:TT]), R=[plb], W=[rl_b])
                    if j == 0:
                        cx.op("dve", lambda e: e.tensor_tensor(out=o0[:], in0=po[:, 0:TT], in1=rl[:], op=ALU.mult), R=[pob, rl_b], W=[o0_b])
                    else:
                        cx.op("dve", lambda e: e.tensor_tensor(out=ya[:], in0=po[:, 0:TT], in1=rl[:], op=ALU.mult), R=[pob, rl_b], W=[ya_b])
                        cx.op("dve", lambda e: e.scalar_tensor_tensor(out=ya[:], in0=ya[:], scalar=lam_t[:, 0:1], in1=o0[:], op0=ALU.mult, op1=ALU.add),
                              R=[ya_b, lam_b, o0_b], W=[ya_b])
                subln_norm(h, TT)

        def subln_norm(h, N):
            cx.op("act", lambda e: e.activation(out=sq[:, 0, 0:N], in_=ya[:, 0:N], func=AF.Square), R=[ya_b], W=[sq_b])
            pb, pbb = next_bank()
            cx.op("pe", lambda e: e.matmul(pb[:, 0:N], lhsT=ones_bf[:], rhs=sq[:, 0, 0:N], start=True, stop=True), R=[ones_b, sq_b], W=[pbb])
            cx.op("act", lambda e: e.activation(out=rstd[:, 0:N], in_=pb[:, 0:N], func=AF.Ln, scale=1.0 / 128, bias=eps128[:]), R=[pbb, eps_b], W=[rstd_b])
            cx.op("act", lambda e: e.activation(out=rstd[:, 0:N], in_=rstd[:, 0:N], func=AF.Exp, scale=-0.5), R=[rstd_b], W=[rstd_b])
            cx.op("dve", lambda e: e.scalar_tensor_tensor(out=ya[:, 0:N], in0=ya[:, 0:N], scalar=subln_t[:, 0:1], in1=rstd[:, 0:N], op0=ALU.mult, op1=ALU.mult),
                  R=[ya_b, subln_b, rstd_b], W=[ya_b])
            cx.op("dve", lambda e: e.tensor_scalar(out=mixT[:, h, st["moff"]:st["moff"] + N], in0=ya[:, 0:N], scalar1=1.0 - LAM_INIT, scalar2=None, op0=ALU.mult), R=[ya_b], W=[mixT_b])

        for nm in ("w0", "a0", "k_k", "k_a", "r_k", "ln_x_w", "ln_x_b"):
            cx.dma("sp", lambda e, nm=nm: e.dma_start(out=cv[nm][0][:], in_=I[nm]), W=[cv[nm][1]])
        cx.dma("pool", lambda e: e.dma_start(out=wa2[:], in_=I["wa2"]), W=[wa2_b])
        cx.dma("pool", lambda e: e.dma_start(out=g2t[:], in_=I["g2"]), W=[g2_b])
        cx.dma("pool", lambda e: e.dma_start(out=bones[:], in_=I["blockones"]), W=[cst_b])
        for tl, nm in ((resetm, "resetm"), (maskA4, "maskA4"), (maskC8, "maskC8"), (I8, "I8")):
            cx.dma("sp", lambda e, tl=tl, nm=nm: e.dma_start(out=tl[:], in_=I[nm]), W=[cst_b])
        cx.op("dve", lambda e: e.tensor_scalar(out=cv["nw0"][0][:], in0=cv["w0"][0][:], scalar1=-1.0, scalar2=None, op0=ALU.mult), R=[cv["w0"][1]], W=[cv["nw0"][1]])
        cx.op("dve", lambda e: e.tensor_scalar(out=cv["na0"][0][:], in0=cv["a0"][0][:], scalar1=-1.0, scalar2=None, op0=ALU.mult), R=[cv["a0"][1]], W=[cv["na0"][1]])
        cx.op("dve", lambda e: e.tensor_scalar(out=cv["omka"][0][:], in0=cv["k_a"][0][:], scalar1=-1.0, scalar2=1.0, op0=ALU.mult, op1=ALU.add), R=[cv["k_a"][1]], W=[cv["omka"][1]])
        cx.op("dve", lambda e: e.memset(gneps[:], GN_EPS), W=[eps_b])
        cx.op("dve", lambda e: e.memset(Af[:], 0.0), W=[Af_b])
        cx.op("dve", lambda e: e.memset(Ab[:], 0.0), W=[Ab_b])

        def C(nm):
            return cv[nm][0]

        def Cb(nm):
            return cv[nm][1]

        def prep_head(N):
            nch = N // 64
            nblk = N // 128
            cx.op("act", lambda e: e.activation(out=f1[0:64, 0:N], in_=xs[0:64, 12, 0:N], func=AF.Exp, scale=2.0), R=[xs_b], W=[f_b[0]])
            cx.op("dve", lambda e: e.tensor_scalar(out=f1[0:64, 0:N], in0=f1[0:64, 0:N], scalar1=1.0, scalar2=None, op0=ALU.add), R=[f_b[0]], W=[f_b[0]])
            cx.op("dve", lambda e: e.reciprocal(out=f1[0:64, 0:N], in_=f1[0:64, 0:N]), R=[f_b[0]], W=[f_b[0]])
            cx.op("dve", lambda e: e.tensor_scalar(out=lora_in[0:64, 0:N], in0=f1[0:64, 0:N], scalar1=-2.0, scalar2=1.0, op0=ALU.mult, op1=ALU.add), R=[f_b[0]], W=[lora_b])
            cx.op("dve", lambda e: e.tensor_copy(out=lora_in[64:128, 0:N], in_=xs[64:128, 12, 0:N]), R=[xs_b], W=[lora_b])
            cx.op("act", lambda e: e.activation(out=f2[:, 0:N], in_=xs[:, 13, 0:N], func=AF.Exp, scale=-1.0), R=[xs_b], W=[f_b[1]])
            cx.op("dve", lambda e: e.tensor_scalar(out=f2[:, 0:N], in0=f2[:, 0:N], scalar1=1.0, scalar2=None, op0=ALU.add), R=[f_b[1]], W=[f_b[1]])
            cx.op("dve", lambda e: e.reciprocal(out=f2[:, 0:N], in_=f2[:, 0:N]), R=[f_b[1]], W=[f_b[1]])
            cx.op("dve", lambda e: e.tensor_copy(out=sgx[:, 0:N], in_=f2[:, 0:N]), R=[f_b[1]], W=[sgx_b])

        def prep_m(N, m, sample=False):
            nch = N // 64
            ms = slice(m * 128, (m + 1) * 128)
            pa, pab = next_bank()
            cx.op("pe", lambda e: e.matmul(pa[:, 0:N], lhsT=wa2[64:128, ms], rhs=lora_in[64:128, 0:N], start=True, stop=True), R=[wa2_b, lora_b], W=[pab])
            cx.op("act", lambda e: e.activation(out=a_t[:, m, 0:N], in_=pa[:, 0:N], func=AF.Exp, scale=-1.0, bias=C("na0")[:, m:m + 1]), R=[pab, Cb("na0")], W=[a_b])
            cx.op("dve", lambda e: e.tensor_scalar(out=a_t[:, m, 0:N], in0=a_t[:, m, 0:N], scalar1=1.0, scalar2=None, op0=ALU.add), R=[a_b], W=[a_b])
            cx.op("dve", lambda e: e.reciprocal(out=a_t[:, m, 0:N], in_=a_t[:, m, 0:N]), R=[a_b], W=[a_b])
            pg, pgb = next_bank()
            cx.op("pe", lambda e: e.matmul(pg[:, 0:N], lhsT=g2t[:, ms], rhs=sgx[:, 0:N], start=True, stop=True), R=[g2_b, sgx_b], W=[pgb])
            cx.op("act", lambda e: e.copy(out=gT[:, m, 0:N], in_=pg[:, 0:N]), R=[pgb], W=[gT_b])
            ms = slice(m * 128, (m + 1) * 128)
            pw, pwb = next_bank()
            cx.op("pe", lambda e: e.matmul(pw[:, 0:N], lhsT=wa2[0:64, ms], rhs=lora_in[0:64, 0:N], start=True, stop=True), R=[wa2_b, lora_b], W=[pwb])
            cx.op("act", lambda e: e.activation(out=f1[:, 0:N], in_=pw[:, 0:N], func=AF.Exp, scale=-1.0, bias=C("nw0")[:, m:m + 1]), R=[pwb, Cb("nw0")], W=[f_b[0]])
            cx.op("dve", lambda e: e.tensor_scalar(out=f1[:, 0:N], in0=f1[:, 0:N], scalar1=1.0, scalar2=None, op0=ALU.add), R=[f_b[0]], W=[f_b[0]])
            cx.op("act", lambda e: e.activation(out=f1[:, 0:N], in_=f1[:, 0:N], func=AF.Ln), R=[f_b[0]], W=[f_b[0]])
            cx.op("dve", lambda e: e.tensor_scalar(out=f1[:, 0:N], in0=f1[:, 0:N], scalar1=-1.0, scalar2=-0.5, op0=ALU.mult, op1=ALU.add), R=[f_b[0]], W=[f_b[0]])
            cx.op("act", lambda e: e.activation(out=f1[:, 0:N], in_=f1[:, 0:N], func=AF.Exp), R=[f_b[0]], W=[f_b[0]])
            cx.op("dve", lambda e: e.tensor_scalar(out=ld[:, m, 0:N], in0=f1[:, 0:N], scalar1=-1.0, scalar2=None, op0=ALU.mult), R=[f_b[0]], W=[ld_b])
            r_, k_ = xs[:, m, 0:N], xs[:, 4 + m, 0:N]
            A_ = a_t[:, m, 0:N]
            cx.op("dve", lambda e: e.tensor_scalar(out=f1[:, 0:N], in0=k_, scalar1=C("k_k")[:, m:m + 1], scalar2=None, op0=ALU.mult), R=[xs_b, Cb("k_k")], W=[f_b[0]])
            cx.op("act", lambda e: e.activation(out=fbf[:, 0:N], in_=f1[:, 0:N], func=AF.Square), R=[f_b[0]], W=[fbf_b])
            pn, pnb = next_bank()
            cx.op("pe", lambda e: e.matmul(pn[:, 0:N], lhsT=bones[:], rhs=fbf[:, 0:N], start=True, stop=True), R=[cst_b, fbf_b], W=[pnb])
            cx.op("dve", lambda e: e.tensor_scalar(out=f2[:, 0:N], in0=pn[:, 0:N], scalar1=1e-24, scalar2=None, op0=ALU.max), R=[pnb], W=[f_b[1]])
            cx.op("act", lambda e: e.activation(out=f2[:, 0:N], in_=f2[:, 0:N], func=AF.Ln), R=[f_b[1]], W=[f_b[1]])
            cx.op("act", lambda e: e.activation(out=f2[:, 0:N], in_=f2[:, 0:N], func=AF.Exp, scale=-0.5), R=[f_b[1]], W=[f_b[1]])
            cx.op("dve", lambda e: e.tensor_tensor(out=f1[:, 0:N], in0=f1[:, 0:N], in1=f2[:, 0:N], op=ALU.mult), R=[f_b[0], f_b[1]], W=[f_b[0]])
            cx.op("dve", lambda e: e.tensor_scalar(out=f2[:, 0:N], in0=A_, scalar1=C("k_a")[:, m:m + 1], scalar2=C("omka")[:, m:m + 1], op0=ALU.mult, op1=ALU.add),
                  R=[a_b, Cb("k_a"), Cb("omka")], W=[f_b[1]])
            cx.op("dve", lambda e: e.tensor_tensor(out=f2[:, 0:N], in0=f2[:, 0:N], in1=k_, op=ALU.mult), R=[f_b[1], xs_b], W=[f_b[1]])
            cx.op("dve", lambda e: e.tensor_tensor(out=f3[:, 0:N], in0=f1[:, 0:N], in1=A_, op=ALU.mult), R=[f_b[0], a_b], W=[f_b[2]])
            if sample:
                cx.op("dve", lambda e: e.tensor_copy(out=raw["kkn"][0][:, m, :], in_=f1[:, 0:NS]), R=[f_b[0]], W=[raw["kkn"][1]])
                cx.op("dve", lambda e: e.tensor_copy(out=raw["kf"][0][:, m, :], in_=f2[:, 0:NS]), R=[f_b[1]], W=[raw["kf"][1]])
                cx.op("dve", lambda e: e.tensor_copy(out=raw["b"][0][:, m, :], in_=f3[:, 0:NS]), R=[f_b[2]], W=[raw["b"][1]])
                cx.op("dve", lambda e: e.tensor_copy(out=raw["r"][0][:, m, :], in_=xs[:, m, 0:NS]), R=[xs_b], W=[raw["r"][1]])
                cx.op("act", lambda e: e.activation(out=raw["w"][0][:, m, :], in_=ld[:, m, 0:NS], func=AF.Exp), R=[ld_b], W=[raw["w"][1]])
            cx.op("dve", lambda e: e.scalar_tensor_tensor(out=fbf[:, 0:N], in0=r_, scalar=C("r_k")[:, m:m + 1], in1=f2[:, 0:N], op0=ALU.mult, op1=ALU.mult),
                  R=[xs_b, Cb("r_k"), f_b[1]], W=[fbf_b])
            pq, pqb = next_bank()
            cx.op("pe", lambda e: e.matmul(pq[:, 0:N], lhsT=bones[:], rhs=fbf[:, 0:N], start=True, stop=True), R=[cst_b, fbf_b], W=[pqb])
            cx.op("act", lambda e: e.copy(out=bs_t[:, m, 0:N], in_=pq[:, 0:N]), R=[pqb], W=[bs_b])
            if sample:
                return
            cx.op("dve", lambda e: e.tensor_tensor_scan(out=f4[:, 0:N], data0=resetm[:, 0:N], data1=ld[:, m, 0:N], initial=0.0, op0=ALU.mult, op1=ALU.add),
                  R=[cst_b, ld_b], W=[f_b[3]])
            c3 = f4[:, 0:N].rearrange("p (c t) -> p c t", t=64)
            cx.op("act", lambda e: e.activation(out=gC[:, m, 0:nch], in_=c3[:, :, 63], func=AF.Exp), R=[f_b[3]], W=[gC_b])
            cx.op("act", lambda e: e.activation(out=f5[:, 0:N], in_=f4[:, 0:N], func=AF.Exp), R=[f_b[3]], W=[f_b[4]])
            cx.op("dve", lambda e: e.tensor_tensor(out=f6[:, 0:N], in0=f4[:, 0:N], in1=ld[:, m, 0:N], op=ALU.subtract), R=[f_b[3], ld_b], W=[f_b[5]])
            cx.op("act", lambda e: e.activation(out=f6[:, 0:N], in_=f6[:, 0:N], func=AF.Exp), R=[f_b[5]], W=[f_b[5]])
            krv = kr[:, m, 0:nch, :]
            kbv = kb_[:, m, 0:nch, :]
            v3 = lambda tl: tl[:, 0:N].rearrange("p (c t) -> p c t", t=64)
            cx.op("dve", lambda e: e.tensor_tensor(out=krv[:, :, 0:64], in0=v3(f1), in1=v3(f6), op=ALU.mult), R=[f_b[0], f_b[5]], W=[kr_b])
            cx.op("dve", lambda e: e.tensor_tensor(out=krv[:, :, 64:128], in0=xs[:, m, 0:N].rearrange("p (c t) -> p c t", t=64), in1=v3(f5), op=ALU.mult),
                  R=[xs_b, f_b[4]], W=[kr_b])
            cx.op("dve", lambda e: e.tensor_tensor(out=v3(f6), in0=c3[:, :, 63:64].to_broadcast([128, nch, 64]), in1=c3, op=ALU.subtract), R=[f_b[3]], W=[f_b[5]])
            cx.op("act", lambda e: e.activation(out=f6[:, 0:N], in_=f6[:, 0:N], func=AF.Exp), R=[f_b[5]], W=[f_b[5]])
            cx.op("dve", lambda e: e.tensor_tensor(out=khat[:, m, 0:N], in0=f2[:, 0:N], in1=f6[:, 0:N], op=ALU.mult), R=[f_b[1], f_b[5]], W=[khat_b])
            cx.op("dve", lambda e: e.scalar_tensor_tensor(out=bhat[:, m, 0:N], in0=f3[:, 0:N], scalar=-1.0, in1=f6[:, 0:N], op0=ALU.mult, op1=ALU.mult),
                  R=[f_b[2], f_b[5]], W=[bhat_b])
            cx.op("act", lambda e: e.activation(out=f5[:, 0:N], in_=f4[:, 0:N], func=AF.Exp, scale=-1.0), R=[f_b[3]], W=[f_b[4]])
            cx.op("dve", lambda e: e.tensor_tensor(out=f2[:, 0:N], in0=f2[:, 0:N], in1=f5[:, 0:N], op=ALU.mult), R=[f_b[1], f_b[4]], W=[f_b[1]])
            cx.op("dve", lambda e: e.tensor_tensor(out=f3[:, 0:N], in0=f3[:, 0:N], in1=f5[:, 0:N], op=ALU.mult), R=[f_b[2], f_b[4]], W=[f_b[2]])
            cx.op("dve", lambda e: e.tensor_copy(out=kbv[:, :, 0:64], in_=v3(f2)), R=[f_b[1]], W=[kb_b])
            cx.op("dve", lambda e: e.tensor_copy(out=kbv[:, :, 64:128], in_=v3(f3)), R=[f_b[2]], W=[kb_b])

        def prep_tail(N, sample=False):
            nblk = N // 128
            if sample:
                return
            for (srcf, srcb, dst, dstb) in ((lambda m, b: xs[:, 8 + m, b * 128:(b + 1) * 128], xs_b, V_tm, Vtm_b),
                                            (lambda m, b: khat[:, m, b * 128:(b + 1) * 128], khat_b, Kh_tm, Kh_b),
                                            (lambda m, b: bhat[:, m, b * 128:(b + 1) * 128], bhat_b, Bh_tm, Bh_b)):
                for b in range(nblk):
                    pb, pbb = next_bank()
                    for m in range(4):
                        cx.op("pe", lambda e, m=m: e.transpose(pb[:, m * 128:(m + 1) * 128], srcf(m, b), ident[:]), R=[srcb, ident_b], W=[pbb], inc=(m == 3))
                    cx.op("act", lambda e: e.copy(out=dst[:, b, :], in_=pb[:, :]), R=[pbb], W=[dstb])


        def rwkv_prep(N, sample=False):
            prep_head(N)
            for m in range(4):
                prep_m(N, m, sample)
            prep_tail(N, sample)

        def mm64(out, lhsT, rhs, rowb, colb, R, W, start=True, stop=True, inc=True):
            Epe = cx.E["pe"]
            if st.get("rowb") is not None and st["rowb"] != rowb and Epe.cnt > 0:
                Epe.eng.wait_ge(Epe.sem, Epe.cnt)
            st["rowb"] = rowb
            cx.op("pe", lambda e: e.matmul(out, lhsT=lhsT, rhs=rhs, start=start, stop=stop, tile_position=(rowb, colb)), R=R, W=W, inc=True)

        def rwkv_block(b, stage=9):
            pA0, pA0b = next_bank(); pA1, pA1b = next_bank()
            pB0, pB0b = next_bank(); pB1, pB1b = next_bank()
            pC, pCb = next_bank()
            for cp in range(2):
                ci = 2 * b + cp
                cb = 64 * cp
                for h in range(8):
                    hb, m = 64 * (h % 2), h // 2
                    pa, pab = (pA0, pA0b) if h < 4 else (pA1, pA1b)
                    pbk, pbkb = (pB0, pB0b) if h < 4 else (pB1, pB1b)
                    hs = slice((h % 4) * 128, (h % 4 + 1) * 128)
                    mm64(pa[cb:cb + 64, hs], kb_[hb:hb + 64, m, ci, 0:64], kr[hb:hb + 64, m, ci, :], hb, cb, [kb_b, kr_b], [pab])
                    mm64(pbk[cb:cb + 64, hs], kb_[hb:hb + 64, m, ci, 64:128], kr[hb:hb + 64, m, ci, :], hb, cb, [kb_b, kr_b], [pbkb])
                    mm64(pC[cb:cb + 64, h * 64:(h + 1) * 64], kr[hb:hb + 64, m, ci, 0:64], kb_[hb:hb + 64, m, ci, 64:128], hb, cb, [kb_b, kr_b], [pCb])
            Amf = Am[:].rearrange("p h c -> p (h c)")
            Bmf = Bm[:].rearrange("p h c -> p (h c)")
            cx.op("dve", lambda e: e.tensor_tensor(out=Amf[:, 0:512], in0=pA0[:, :], in1=maskA4[:], op=ALU.mult), R=[pA0b, cst_b], W=[Am_b])
            cx.op("dve", lambda e: e.tensor_tensor(out=Amf[:, 512:1024], in0=pA1[:, :], in1=maskA4[:], op=ALU.mult), R=[pA1b, cst_b], W=[Am_b])
            cx.op("dve", lambda e: e.scalar_tensor_tensor(out=Bmf[:, 0:512], in0=pB0[:, :], scalar=-1.0, in1=maskA4[:], op0=ALU.mult, op1=ALU.mult), R=[pB0b, cst_b], W=[Bm_b])
            cx.op("dve", lambda e: e.scalar_tensor_tensor(out=Bmf[:, 512:1024], in0=pB1[:, :], scalar=-1.0, in1=maskA4[:], op0=ALU.mult, op1=ALU.mult), R=[pB1b, cst_b], W=[Bm_b])
            if stage < 3:
                return
            zc, zcb, ztc, ztcb = Zb[0], Zb_b[0], ZTb[0], ZTb_b[0]
            fl = lambda tl: tl[:].rearrange("p h c -> p (h c)")
            cx.op("dve", lambda e: e.tensor_copy(out=zc[:], in_=Bm[:, :, 0:64]), R=[Bm_b], W=[zcb])
            cx.op("dve", lambda e: e.scalar_tensor_tensor(out=fl(ztc), in0=pC[:, :], scalar=-1.0, in1=maskC8[:], op0=ALU.mult, op1=ALU.mult), R=[pCb, cst_b], W=[ztcb])
            cx.op("dve", lambda e: e.tensor_tensor(out=fl(Wf), in0=fl(zc), in1=I8[:], op=ALU.add), R=[zcb, cst_b], W=[Wf_b])
            cx.op("act", lambda e: e.copy(out=Wb[:], in_=Wf[:]), R=[Wf_b], W=[Wb_b])
            cur = 0
            for lvl in range(5):
                nz, nzb, nzt, nztb = Zb[1 - cur], Zb_b[1 - cur], ZTb[1 - cur], ZTb_b[1 - cur]
                pz, pzb = next_bank(); pzt, pztb = next_bank()
                for cp in range(2):
                    cb = 64 * cp
                    for h in range(8):
                        hs = slice(h * 64, (h + 1) * 64)
                        if lvl < 4:
                            mm64(pz[cb:cb + 64, hs], ztc[cb:cb + 64, h, :], zc[cb:cb + 64, h, :], cb, cb, [ztcb, zcb], [pzb])
                        mm64(pzt[cb:cb + 64, hs], zc[cb:cb + 64, h, :], ztc[cb:cb + 64, h, :], cb, cb, [ztcb, zcb], [pztb])
                if lvl < 4:
                    cx.op("act", lambda e: e.copy(out=fl(nz), in_=pz[:, :]), R=[pzb], W=[nzb])
                cx.op("dve", lambda e: e.tensor_copy(out=fl(nzt), in_=pzt[:, :]), R=[pztb], W=[nztb])
                pw, pwb = next_bank()
                for cp in range(2):
                    cb = 64 * cp
                    for h in range(8):
                        hs = slice(h * 64, (h + 1) * 64)
                        mm64(pw[cb:cb + 64, hs], nzt[cb:cb + 64, h, :], Wb[cb:cb + 64, h, :], cb, cb, [nztb, Wb_b], [pwb])
                cx.op("dve", lambda e: e.tensor_tensor(out=fl(Wf), in0=fl(Wf), in1=pw[:, :], op=ALU.add), R=[Wf_b, pwb], W=[Wf_b])
                cx.op("act", lambda e: e.copy(out=Wb[:], in_=Wf[:]), R=[Wf_b], W=[Wb_b])
                cur = 1 - cur
                zc, zcb, ztc, ztcb = Zb[cur], Zb_b[cur], ZTb[cur], ZTb_b[cur]
            if stage < 4:
                return
            for cp in range(2):
                ci = 2 * b + cp
                cb = 64 * cp
                pR, pRb = next_bank()
                for h in range(8):
                    hb, m = 64 * (h % 2), h // 2
                    hs = slice(h * 64, (h + 1) * 64)
                    mm64(pR[cb:cb + 64, hs], kr[hb:hb + 64, m, ci, 0:64], Ab[hb:hb + 64, m, :], hb, cb, [kr_b, Ab_b], [pRb], start=True, stop=False, inc=False)
                    mm64(pR[cb:cb + 64, hs], Am[cb:cb + 64, h, 0:64], V_tm[cb:cb + 64, b, hs], cb, cb, [Am_b, Vtm_b], [pRb], start=False, stop=True, inc=(h == 7))
                cx.op("act", lambda e: e.copy(out=Rb[cb:cb + 64].rearrange("p h c -> p (h c)"), in_=pR[cb:cb + 64, :]), R=[pRb], W=[Rb_b])
                pU, pUb = next_bank()
                for h in range(8):
                    hs = slice(h * 64, (h + 1) * 64)
                    mm64(pU[cb:cb + 64, hs], Wb[cb:cb + 64, h, :], Rb[cb:cb + 64, h, :], cb, cb, [Wb_b, Rb_b], [pUb], inc=(h == 7))
                cx.op("act", lambda e: e.copy(out=Ub[cb:cb + 64].rearrange("p h c -> p (h c)"), in_=pU[cb:cb + 64, :]), R=[pUb], W=[Ub_b])
                pY, pYb = next_bank()
                pAn, pAnb = next_bank()
                for h in range(8):
                    hb, m = 64 * (h % 2), h // 2
                    hs = slice(h * 64, (h + 1) * 64)
                    mm64(pY[cb:cb + 64, hs], kr[hb:hb + 64, m, ci, 64:128], Ab[hb:hb + 64, m, :], hb, cb, [kr_b, Ab_b], [pYb], start=True, stop=False, inc=False)
                    mm64(pY[cb:cb + 64, hs], Am[cb:cb + 64, h, 64:128], V_tm[cb:cb + 64, b, hs], cb, cb, [Am_b, Vtm_b], [pYb], start=False, stop=False, inc=False)
                    mm64(pY[cb:cb + 64, hs], Bm[cb:cb + 64, h, 64:128], Ub[cb:cb + 64, h, :], cb, cb, [Bm_b, Ub_b], [pYb], start=False, stop=True, inc=(h == 7))
                cx.op("act", lambda e: e.copy(out=y_tm[cb:cb + 64, b, :], in_=pY[cb:cb + 64, :]), R=[pYb], W=[ytm_b])
                for h in range(8):
                    hb, m = 64 * (h % 2), h // 2
                    hs = slice(h * 64, (h + 1) * 64)
                    ms_ = slice(m * 64, (m + 1) * 64)
                    mm64(pAn[hb:hb + 64, ms_], Kh_tm[cb:cb + 64, b, hs], V_tm[cb:cb + 64, b, hs], cb, hb, [Kh_b, Vtm_b], [pAnb], start=True, stop=False, inc=False)
                    mm64(pAn[hb:hb + 64, ms_], Bh_tm[cb:cb + 64, b, hs], Ub[cb:cb + 64, h, :], cb, hb, [Bh_b, Ub_b], [pAnb], start=False, stop=True, inc=(h == 7))
                for m in range(4):
                    cx.op("dve", lambda e, m=m: e.scalar_tensor_tensor(out=Af[:, m, :], in0=Af[:, m, :], scalar=gC[:, m, ci:ci + 1], in1=pAn[:, m * 64:(m + 1) * 64],
                                                                      op0=ALU.mult, op1=ALU.add), R=[Af_b, gC_b, pAnb], W=[Af_b])
                cx.op("act", lambda e: e.copy(out=Ab[:], in_=Af[:]), R=[Af_b], W=[Ab_b])

        def rwkv_finish(N):
            nblk = N // 128
            ng = nblk * 8
            yv = y_tm[:, 0:nblk, :].rearrange("p b (h v) -> p (b h) v", v=64)
            ysq_v = z[:, 0:4, :].rearrange("p a t -> p (a t)")[:, 0:nblk * 512]
            sv = ysq_v.rearrange("p (g v) -> p g v", v=64)
            cx.op("dve", lambda e: e.reduce_sum(out=gst[:, 0, 0:ng], in_=yv, axis=AX.X), R=[ytm_b], W=[gst_b])
            cx.op("dve", lambda e: e.tensor_tensor(out=ysq_v, in0=y_tm[:, 0:nblk, :].rearrange("p b c -> p (b c)"), in1=y_tm[:, 0:nblk, :].rearrange("p b c -> p (b c)"), op=ALU.mult), R=[ytm_b], W=[ysq_b])
            cx.op("dve", lambda e: e.reduce_sum(out=gst[:, 1, 0:ng], in_=sv, axis=AX.X), R=[ysq_b], W=[gst_b])
            cx.op("dve", lambda e: e.tensor_scalar(out=gst[:, 0, 0:ng], in0=gst[:, 0, 0:ng], scalar1=1.0 / 64, scalar2=None, op0=ALU.mult), R=[gst_b], W=[gst_b])
            cx.op("dve", lambda e: e.tensor_tensor(out=gst[:, 2, 0:ng], in0=gst[:, 0, 0:ng], in1=gst[:, 0, 0:ng], op=ALU.mult), R=[gst_b], W=[gst_b])
            cx.op("dve", lambda e: e.scalar_tensor_tensor(out=gst[:, 1, 0:ng], in0=gst[:, 1, 0:ng], scalar=1.0 / 64, in1=gst[:, 2, 0:ng], op0=ALU.mult, op1=ALU.subtract),
                  R=[gst_b], W=[gst_b])
            cx.op("act", lambda e: e.activation(out=gst[:, 1, 0:ng], in_=gst[:, 1, 0:ng], func=AF.Ln, bias=gneps[:]), R=[gst_b, eps_b], W=[gst_b])
            cx.op("act", lambda e: e.activation(out=gst[:, 1, 0:ng], in_=gst[:, 1, 0:ng], func=AF.Exp, scale=-0.5), R=[gst_b], W=[gst_b])
            cx.op("dve", lambda e: e.tensor_tensor(out=yv, in0=yv, in1=gst[:, 0, 0:ng].unsqueeze(2).to_broadcast([128, ng, 64]), op=ALU.subtract), R=[ytm_b, gst_b], W=[ytm_b])
            cx.op("dve", lambda e: e.tensor_tensor(out=yv, in0=yv, in1=gst[:, 1, 0:ng].unsqueeze(2).to_broadcast([128, ng, 64]), op=ALU.mult), R=[ytm_b, gst_b], W=[ytm_b])
            for m in range(4):
                pb, pbb = next_bank()
                for b in range(nblk):
                    cx.op("pe", lambda e, b=b: e.transpose(pb[:, b * 128:(b + 1) * 128], y_tm[:, b, m * 128:(m + 1) * 128], ident[:]), R=[ytm_b, ident_b], W=[pbb], inc=(b == nblk - 1))
                cx.op("act", lambda e: e.activation(out=f1[:, 0:N], in_=pb[:, 0:N], func=AF.Identity, scale=C("ln_x_w")[:, m:m + 1], bias=C("ln_x_b")[:, m:m + 1]),
                      R=[pbb, Cb("ln_x_w"), Cb("ln_x_b")], W=[f_b[0]])
                cx.op("dve", lambda e: e.tensor_tensor(out=f2[:, 0:N], in0=bs_t[:, m, 0:N], in1=xs[:, 8 + m, 0:N], op=ALU.mult), R=[bs_b, xs_b], W=[f_b[1]])
                cx.op("dve", lambda e: e.tensor_tensor(out=f1[:, 0:N], in0=f1[:, 0:N], in1=f2[:, 0:N], op=ALU.add), R=[f_b[0], f_b[1]], W=[f_b[0]])
                cx.op("dve", lambda e: e.tensor_tensor(out=mixT[:, 4 + m, st["moff"]:st["moff"] + N], in0=f1[:, 0:N], in1=gT[:, m, 0:N], op=ALU.mult), R=[f_b[0], gT_b], W=[mixT_b])

        def wkv_out(dst):
            for m in range(4):
                pb, pbb = next_bank()
                cx.op("pe", lambda e: e.transpose(pb[0:64, 0:128], Af[:, m, :], ident[:]), R=[Af_b, ident_b], W=[pbb])
                cx.op("act", lambda e: e.copy(out=wkvT[:, 2 * m:2 * m + 2, :].rearrange("p h k -> p (h k)"), in_=pb[0:64, 0:128]), R=[pbb], W=[wkvT_b])
            cx.dma("sp", lambda e: e.dma_start(out=dst.rearrange("h v k -> v h k"), in_=wkvT), R=[wkvT_b])

        def out_proj(N):
            for m2 in range(4):
                wp, wpb = load_wpiece(I["w_out"], m2 * 256)
                for hf in range(2):
                    mo = m2 * 2 + hf
                    pb, pbb = next_bank()
                    for c in range(8):
                        cx.op("pe", lambda e, c=c: e.matmul(pb[:, 0:N], lhsT=wp[:, c, hf * 128:(hf + 1) * 128], rhs=mixT[:, c, 0:N], start=(c == 0), stop=(c == 7)),
                              R=[wpb, mixT_b], W=[pbb], inc=(c == 7))
                    cx.op("act", lambda e, mo=mo: e.copy(out=z[:, mo, 0:N], in_=pb[:, 0:N]), R=[pbb], W=[z_b])
            post_norm_add("n_mix_post", N, False)

        cx.dma("sp", lambda e: e.dma_start(out=diag64[:], in_=I["diag64"]), W=[sconst_b])
        cx.dma("sp", lambda e: e.dma_start(out=bones_f[:], in_=I["blockones"]), W=[sconst_b])
        cx.dma("sp", lambda e: e.dma_start(out=ones_f32[:], in_=I["ones"]), W=[sconst_b])
        cx.dma("sp", lambda e: e.dma_start(out=ptb[:], in_=I["pt_own"].broadcast_to([128, NS * NPG])), W=[idx_b])
        cx.op("pool", lambda e: e.iota(iota_c[:], pattern=[[0, 1]], base=0, channel_multiplier=1), W=[idx_b])
        cx.op("pool", lambda e: e.tensor_scalar(out=idx[:], in0=ptb[:], scalar1=128, scalar2=None, op0=ALU.mult), R=[idx_b], W=[idx_b])
        cx.op("pool", lambda e: e.tensor_tensor(out=idx[:], in0=idx[:], in1=iota_c[:].to_broadcast([128, NS * NPG]), op=ALU.add), R=[idx_b], W=[idx_b])

        def gather_page(n, pg):
            i = st.get("pg", 0) % NKR
            st["pg"] = st.get("pg", 0) + 1
            j = n * NPG + pg
            cx.dma("pool", lambda e: e.indirect_dma_start(out=kpg[i][:], out_offset=None, in_=I["cache_k"][:, :],
                                                         in_offset=bass.IndirectOffsetOnAxis(ap=idx[:, j:j + 1], axis=0)), R=[idx_b], W=[kpg_b[i]])
            cx.dma("pool", lambda e: e.indirect_dma_start(out=vpg[i][:], out_offset=None, in_=I["cache_v"][:, :],
                                                         in_offset=bass.IndirectOffsetOnAxis(ap=idx[:, j:j + 1], axis=0)), R=[idx_b], W=[vpg_b[i]])
            return kpg[i], kpg_b[i], vpg[i], vpg_b[i]

        def sample_attention():
            cx.op("act", lambda e: e.activation(out=qs_bf[:], in_=qk_tm[:, 0, 0:512], func=AF.Copy, scale=0.125), R=[qk_b], W=[qs_b])
            cx.op("dve", lambda e: e.tensor_copy(out=knew_bf[:], in_=qk_tm[:, 0, 512:1024]), R=[qk_b], W=[new_b])
            cx.op("dve", lambda e: e.tensor_copy(out=vnew_bf[:], in_=v_tm[:, 0, :]), R=[v_b], W=[new_b])
            po, pob = pbank[6], pbank_b[6]
            pl, plb = pbank[7], pbank_b[7]
            for n in range(NS):
                pq, pqb = next_bank()
                cx.op("dve", lambda e: e.tensor_scalar(out=sel_t[:], in0=ones_bf[:], scalar1=ident[:, n:n + 1], scalar2=None, op0=ALU.mult), R=[ones_b, ident_b], W=[sel_b])
                cx.op("pe", lambda e: e.matmul(pq[:, :], lhsT=sel_t[:], rhs=qs_bf[:], start=True, stop=True), R=[sel_b, qs_b], W=[pqb])
                pages = []
                for pg in range(NPG1):
                    if pg < NPG:
                        kt, ktb, vt, vtb = gather_page(n, pg)
                    else:
                        kt, ktb, vt, vtb = knew_bf, new_b, vnew_bf, new_b
                    pr, prb = prod[pg % 2], prod_b[pg % 2]
                    cx.op("dve", lambda e: e.tensor_tensor(out=pr, in0=kt[:], in1=pq[:, :], op=ALU.mult), R=[ktb, pqb], W=[prb])
                    cx.op("dve", lambda e: e.reduce_sum(out=sT[:, pg, :], in_=pr.rearrange("p (g d) -> p g d", d=64), axis=AX.X), R=[prb], W=[sT_b])
                    pages.append((vt, vtb))
                    if pg % 4 == 3 or pg == NPG1 - 1:
                        lo = (pg // 4) * 4
                        cx.op("act", lambda e: e.activation(out=pT[:, lo:pg + 1, :], in_=sT[:, lo:pg + 1, :], func=AF.Exp), R=[sT_b], W=[pT_b])
                        if pg == NPG1 - 1:
                            cx.op("dve", lambda e: e.tensor_scalar(out=pT[:, NPG, :], in0=pT[:, NPG, :], scalar1=ident[:, n:n + 1], scalar2=None, op0=ALU.mult),
                                  R=[pT_b, ident_b], W=[pT_b])
                        for p2 in range(lo, pg + 1):
                            vt2, vtb2 = pages[p2]
                            for h in range(4):
                                cx.op("pe", lambda e: e.matmul(po[:, n * 8 + 2 * h:n * 8 + 2 * h + 2], lhsT=vt2[:, h * 128:(h + 1) * 128], rhs=pT[:, p2, 2 * h:2 * h + 2],
                                                               start=(n == 0 and p2 == 0 and h == 0), stop=(p2 == NPG1 - 1), skip_group_check=True), R=[vtb2, pT_b], W=[pob], inc=True)
                cx.op("dve", lambda e: e.reduce_sum(out=psum8[:], in_=pT[:].rearrange("p g c -> p c g"), axis=AX.X), R=[pT_b], W=[psum8_b])
                cx.op("pe", lambda e: e.matmul(pl[:, n * 8:(n + 1) * 8], lhsT=ones_f32[:], rhs=psum8[:], start=True, stop=True), R=[sconst_b, psum8_b], W=[plb])
            cx.op("act", lambda e: e.copy(out=oS[:].rearrange("p n c -> p (n c)"), in_=po[:, 0:NS * 8]), R=[pob], W=[oS_b])
            cx.op("dve", lambda e: e.reciprocal(out=lS[:].rearrange("p n c -> p (n c)"), in_=pl[:, 0:NS * 8]), R=[plb], W=[lS_b])
            cx.op("dve", lambda e: e.tensor_tensor(out=oS[:], in0=oS[:], in1=lS[:], op=ALU.mult), R=[oS_b, lS_b], W=[oS_b])
            ov = oS[:].rearrange("p n (h j) -> p h n j", j=2)
            cx.op("dve", lambda e: e.scalar_tensor_tensor(out=yaS[:], in0=ov[:, :, :, 1], scalar=lam_t[:, 0:1], in1=ov[:, :, :, 0], op0=ALU.mult, op1=ALU.add),
                  R=[oS_b, lam_b], W=[yaS_b])
            cx.op("dve", lambda e: e.memset(mixT[:, :, :], 0.0), W=[mixT_b])
            for h in range(4):
                cx.op("dve", lambda e: e.memset(ya[:], 0.0), W=[ya_b])
                cx.op("dve", lambda e: e.tensor_copy(out=ya[:, 0:NS], in_=yaS[:, h, :]), R=[yaS_b], W=[ya_b])
                subln_norm(h, TT)

        def expand(nm, half):
            X, Xbuf = raw[nm]
            n0 = half * HN
            cx.op("dve", lambda e: e.tensor_tensor(out=Xe, in0=X[:, :, n0:n0 + HN].unsqueeze(3).to_broadcast([128, 4, HN, 64]),
                                                   in1=diag64[:].unsqueeze(1).unsqueeze(1).to_broadcast([128, 4, HN, 64]), op=ALU.mult),
                  R=[Xbuf, sconst_b], W=[Xe_b])
            for m in range(4):
                pb, pbb = next_bank()
                cx.op("pe", lambda e: e.matmul(pb[:, 0:HN * 64], lhsT=bones_f[:], rhs=Xe[:, m, :, :].rearrange("p n k -> p (n k)"), start=True, stop=True),
                      R=[sconst_b, Xe_b], W=[pbb])
                cx.op("act", lambda e: e.copy(out=Xb[:, :, m, :], in_=pb[:, 0:HN * 64].rearrange("p (n k) -> p n k", k=64)), R=[pbb], W=[Xb_b])

        def sample_wkv():
            for half in range(NS // HN):
                n0 = half * HN
                cx.dma("sp", lambda e: e.dma_start(out=S_t[:], in_=I["swkv"][n0:n0 + HN].rearrange("n (m hp) v k -> (hp v) n m k", hp=2)), W=[S_b])
                vT = xs[:, 8:12, n0:n0 + HN].rearrange("p m n -> p n m")
                expand("kkn", half)
                cx.op("dve", lambda e: e.tensor_tensor(out=T1, in0=S_t[:], in1=Xb[:], op=ALU.mult), R=[S_b, Xb_b], W=[T1_b])
                cx.op("dve", lambda e: e.reduce_sum(out=red[:], in_=T1, axis=AX.X), R=[T1_b], W=[red_b])
                expand("w", half)
                cx.op("dve", lambda e: e.tensor_tensor(out=S_t[:], in0=S_t[:], in1=Xb[:], op=ALU.mult), R=[S_b, Xb_b], W=[S_b])
                expand("b", half)
                cx.op("dve", lambda e: e.tensor_tensor(out=T1, in0=Xb[:], in1=red[:].unsqueeze(3).to_broadcast([128, HN, 4, 64]), op=ALU.mult), R=[Xb_b, red_b], W=[T1_b])
                cx.op("dve", lambda e: e.tensor_tensor(out=S_t[:], in0=S_t[:], in1=T1, op=ALU.subtract), R=[S_b, T1_b], W=[S_b])
                expand("kf", half)
                cx.op("dve", lambda e: e.tensor_tensor(out=T1, in0=Xb[:], in1=vT.unsqueeze(3).to_broadcast([128, HN, 4, 64]), op=ALU.mult), R=[Xb_b, xs_b], W=[T1_b])
                cx.op("dve", lambda e: e.tensor_tensor(out=S_t[:], in0=S_t[:], in1=T1, op=ALU.add), R=[S_b, T1_b], W=[S_b])
                cx.dma("sp", lambda e: e.dma_start(out=O["wkvs"][n0:n0 + HN].rearrange("n (m hp) v k -> (hp v) n m k", hp=2), in_=S_t[:]), R=[S_b])
                expand("r", half)
                cx.op("dve", lambda e: e.tensor_tensor(out=T1, in0=S_t[:], in1=Xb[:], op=ALU.mult), R=[S_b, Xb_b], W=[T1_b])
                cx.op("dve", lambda e: e.reduce_sum(out=yS[:, :, n0:n0 + HN].rearrange("p m n -> p n m"), in_=T1, axis=AX.X), R=[T1_b], W=[yS_b])
            yf = yS[:].rearrange("p m n -> p (m n)")
            NN = 4 * NS
            pm, pmb = next_bank()
            cx.op("pe", lambda e: e.matmul(pm[:, 0:NN], lhsT=bones_f[:], rhs=yf, start=True, stop=True), R=[sconst_b, yS_b], W=[pmb])
            cx.op("dve", lambda e: e.scalar_tensor_tensor(out=gS[0][:], in0=pm[:, 0:NN], scalar=-1.0 / 64, in1=yf, op0=ALU.mult, op1=ALU.add), R=[pmb, yS_b], W=[gS_b[0]])
            cx.op("dve", lambda e: e.tensor_tensor(out=gS[1][:], in0=gS[0][:], in1=gS[0][:], op=ALU.mult), R=[gS_b[0]], W=[gS_b[1]])
            pv, pvb = next_bank()
            cx.op("pe", lambda e: e.matmul(pv[:, 0:NN], lhsT=bones_f[:], rhs=gS[1][:], start=True, stop=True), R=[sconst_b, gS_b[1]], W=[pvb])
            cx.op("act", lambda e: e.activation(out=gS[2][:], in_=pv[:, 0:NN], func=AF.Ln, scale=1.0 / 64, bias=gneps[:]), R=[pvb, eps_b], W=[gS_b[2]])
            cx.op("act", lambda e: e.activation(out=gS[2][:], in_=gS[2][:], func=AF.Exp, scale=-0.5), R=[gS_b[2]], W=[gS_b[2]])
            cx.op("dve", lambda e: e.tensor_tensor(out=gS[0][:], in0=gS[0][:], in1=gS[2][:], op=ALU.mult), R=[gS_b[0], gS_b[2]], W=[gS_b[0]])
            for m in range(4):
                ynm = gS[0][:, m * NS:(m + 1) * NS]
                cx.op("act", lambda e: e.activation(out=f1[:, 0:NS], in_=ynm, func=AF.Identity, scale=C("ln_x_w")[:, m:m + 1], bias=C("ln_x_b")[:, m:m + 1]),
                      R=[gS_b[0], Cb("ln_x_w"), Cb("ln_x_b")], W=[f_b[0]])
                cx.op("dve", lambda e: e.tensor_tensor(out=f2[:, 0:NS], in0=bs_t[:, m, 0:NS], in1=xs[:, 8 + m, 0:NS], op=ALU.mult), R=[bs_b, xs_b], W=[f_b[1]])
                cx.op("dve", lambda e: e.tensor_tensor(out=f1[:, 0:NS], in0=f1[:, 0:NS], in1=f2[:, 0:NS], op=ALU.add), R=[f_b[0], f_b[1]], W=[f_b[0]])
                cx.op("dve", lambda e: e.tensor_tensor(out=mixT[:, 4 + m, 0:NS], in0=f1[:, 0:NS], in1=gT[:, m, 0:NS], op=ALU.mult), R=[f_b[0], gT_b], W=[mixT_b])

        def sample_tile():
            st["moff"] = 0
            load_x_tile(I["xs_pad"], 1)
            ffn(I["ffn1_gate"], I["ffn1_up"], I["ffn1_down"], "n_ffn1_pre", "n_ffn1_post", TT)
            win_stage(TT)
            sub_front(0, I["cosS"], I["sinS"])
            cx.dma("sp", lambda e: e.dma_start(out=O["ks_pad"], in_=qk_tm[:, 0, 512:1024]), R=[qk_b])
            cx.dma("sp", lambda e: e.dma_start(out=O["vs_pad"], in_=v_tm[:, 0, :]), R=[v_b])
            cx.dma("sp", lambda e: e.dma_start(out=O["shs"], in_=pbT[:, :, 1:TT + 1]), R=[pbT_b])
            cx.dma("sp", lambda e: e.dma_start(out=xs[:, :, 0:TT], in_=I["sshT"]), W=[xs_b])
            cx.op("dve", lambda e: e.tensor_tensor(out=xs[:, :, 0:TT], in0=xs[:, :, 0:TT], in1=pbT[:, :, 1:TT + 1], op=ALU.subtract), R=[pbT_b, xs_b], W=[xs_b])
            for j in range(14):
                cx.op("dve", lambda e, j=j: e.scalar_tensor_tensor(out=xs[:, j, 0:TT], in0=xs[:, j, 0:TT], scalar=mu_t[:, j:j + 1], in1=pbT[:, j, 1:TT + 1],
                                                                  op0=ALU.mult, op1=ALU.add), R=[xs_b, mu_b, pbT_b], W=[xs_b])
            sample_attention()
            rwkv_prep(TT, sample=True)
            sample_wkv()
            out_proj(TT)
            ffn(I["ffn2_gate"], I["ffn2_up"], I["ffn2_down"], "n_ffn2_pre", "n_ffn2_post", TT)
            store_x_tile(O["ys_pad"], 1)

        for tf in range(NTF):
            load_x_tile(I["xp"][tf * TF:(tf + 1) * TF, :], TF // 128)
            ffn(I["ffn1_gate"], I["ffn1_up"], I["ffn1_down"], "n_ffn1_pre", "n_ffn1_post", TF)
            win_stage(TF)
            for s in range(NSUB):
                t = tf * NSUB + s
                st["moff"] = s * TT
                nblk = sub_front(s, I["cosT"][t * TT:(t + 1) * TT, :], I["sinT"][t * TT:(t + 1) * TT, :])
                prompt_kv_out(t, nblk)
                shift_mix(TT)
                if t == NT - 1:
                    cx.op("dve", lambda e: e.tensor_copy(out=shc[:], in_=pbT[:, :, TT]), R=[pbT_b], W=[shc_b])
                    cx.dma("sp", lambda e: e.dma_start(out=O["shp"], in_=shc[:]), R=[shc_b])
                cx.op("dve", lambda e: e.tensor_copy(out=pbT[:, :, 0:1], in_=pbT[:, :, TT:TT + 1]), R=[pbT_b], W=[pbT_b])
                prep_head(TT)
                for h in range(4):
                    attention_head(t, h)
                    prep_m(TT, h)
                prep_tail(TT)
                for b in range(NBLK):
                    rwkv_block(b, stage)
                rwkv_finish(TT)
                if t == NT - 1:
                    wkv_out(O["wkvp"])
            out_proj(TF)
            ffn(I["ffn2_gate"], I["ffn2_up"], I["ffn2_down"], "n_ffn2_pre", "n_ffn2_post", TF)
            store_x_tile(O["yp"][tf * TF:(tf + 1) * TF, :], TF // 128)

        if NPOOL > 0:
            sample_tile()
        cx.wait_all_dma("sp")
    return nc


def host_consts(SEQ, pos0=0, past_len=2048):
    TT = 128
    pos = np.arange(SEQ, dtype=np.float32) + pos0
    inv = (500000.0 ** (-np.arange(8, dtype=np.float32) / 8)).astype(np.float32)
    ang = pos[:, None] * inv[None, :]
    cosT = np.tile(np.cos(ang).astype(np.float32), (1, 16))
    sinT = np.tile(np.sin(ang).astype(np.float32), (1, 16))
    kk = np.arange(128)[:, None, None] + 128 * np.arange(TT // 128)[None, :, None]
    qq = np.arange(TT)[None, None, :]
    cmask = (kk <= qq).astype(np.float32)
    resetm = np.ones((128, TT), np.float32); resetm[:, ::64] = 0.0
    i = np.arange(128)[:, None] % 64
    tcol = np.arange(128)[None, :]
    mA = np.where(tcol < 64, i < (tcol % 64), i <= (tcol % 64)).astype(np.float32)
    maskA4 = np.tile(mA, (1, 4))
    mC = (np.arange(64)[None, :] < i).astype(np.float32)
    maskC8 = np.tile(mC, (1, 8))
    I8 = np.tile((np.arange(64)[None, :] == i).astype(np.float32), (1, 8))
    blockones = (np.arange(128)[:, None] // 64 == np.arange(128)[None, :] // 64).astype(np.float32)
    angS = float(past_len) * inv
    cosS = np.tile(np.cos(angS).astype(np.float32)[None, :], (128, 16))
    sinS = np.tile(np.sin(angS).astype(np.float32)[None, :], (128, 16))
    diag64 = (np.arange(64)[None, :] == i).astype(np.float32)
    return {"ident": np.eye(128, dtype=np.float32), "ones": np.ones((128, 128), np.float32),
            "cosS": cosS, "sinS": sinS, "diag64": diag64,
            "cosT": cosT, "sinT": sinT, "cmask": cmask, "resetm": resetm, "maskA4": maskA4, "maskC8": maskC8,
            "I8": I8, "blockones": blockones}


def fm(vec, nchunk):
    return np.ascontiguousarray(np.asarray(vec, np.float32).reshape(nchunk, 128).T)


_SEQ = 4096


def kernel(x_prompt, x_sample, cache_k, cache_v, state_wkv, state_shift, page_table,
           n_ffn1_pre, n_ffn1_post, ffn1_gate, ffn1_up, ffn1_down,
           n_mix_pre, n_mix_post, w_in, w_out,
           lambda_q1, lambda_k1, lambda_q2, lambda_k2, subln,
           mu_shift, w0, w2, a0, a2, g2, k_k, k_a, r_k, ln_x_w, ln_x_b,
           n_ffn2_pre, n_ffn2_post, ffn2_gate, ffn2_up, ffn2_down):
    f = lambda a: np.ascontiguousarray(np.asarray(a, dtype=np.float32))
    x_prompt = f(x_prompt)
    B, SEQ = x_prompt.shape[0], x_prompt.shape[1]
    page_table = np.asarray(page_table).astype(np.int32)
    n_s, NPG = page_table.shape
    NPOOL = np.asarray(cache_k).shape[1]
    assert n_s == 8 * NS and B == 4
    nc = build_program(SEQ, NPOOL, NPG, debug=False)
    consts = host_consts(SEQ, past_len=NPG * PAGE)
    ck = f(cache_k[0]).reshape(NPOOL * PAGE, 512)
    cv_ = f(cache_v[0]).reshape(NPOOL * PAGE, 512)
    xs_all = f(x_sample)[:, 0, :]
    ss_all = f(state_shift[0])
    sw_all = f(state_wkv[0])
    shared = {
        "ffn1_gate": f(ffn1_gate[0]), "ffn1_up": f(ffn1_up[0]), "ffn1_down": f(ffn1_down[0]),
        "ffn2_gate": f(ffn2_gate[0]), "ffn2_up": f(ffn2_up[0]), "ffn2_down": f(ffn2_down[0]),
        "w_in": f(w_in[0]), "w_out": f(w_out[0]),
        "n_ffn1_pre": fm(n_ffn1_pre[0], 8), "n_ffn1_post": fm(n_ffn1_post[0], 8),
        "n_mix_pre": fm(n_mix_pre[0], 8), "n_mix_post": fm(n_mix_post[0], 8),
        "n_ffn2_pre": fm(n_ffn2_pre[0], 8), "n_ffn2_post": fm(n_ffn2_post[0], 8),
        "mu": fm(mu_shift[0], 14),
        "lamv": np.concatenate([f(lambda_q1[0]), f(lambda_k1[0]), f(lambda_q2[0]), f(lambda_k2[0])]).reshape(1, 256),
        "subln": f(subln[0]).reshape(128, 1),
        "w0": fm(w0[0], 4), "a0": fm(a0[0], 4), "k_k": fm(k_k[0], 4), "k_a": fm(k_a[0], 4),
        "r_k": fm(np.asarray(r_k[0]).reshape(-1), 4), "ln_x_w": fm(ln_x_w[0], 4), "ln_x_b": fm(ln_x_b[0], 4),
        "wa2": np.ascontiguousarray(np.concatenate([f(w2[0]), f(a2[0])], axis=0)), "g2": f(g2[0]),
        "cache_k": ck, "cache_v": cv_,
        **consts,
    }
    in_maps = []
    for c in range(8):
        sl = slice(NS * c, NS * (c + 1))
        xs_pad = np.zeros((128, D), np.float32)
        xs_pad[:NS] = xs_all[sl]
        sshT = np.zeros((128, 14, 128), np.float32)
        sshT[:, :, :NS] = ss_all[sl].reshape(NS, 14, 128).transpose(2, 1, 0)
        in_maps.append(dict(shared, xp=x_prompt[c // 2], xs_pad=xs_pad, sshT=sshT,
                            pt_own=np.ascontiguousarray(page_table[sl].reshape(1, NS * NPG)),
                            swkv=np.ascontiguousarray(sw_all[sl])))
    res = run_bass_kernel_spmd(nc, in_maps, core_ids=list(range(8)))
    R_ = res.results
    r = [R_[2 * s] for s in range(B)]
    yp = np.stack([q["yp"] for q in r]).astype(np.float32)
    kp = np.stack([q["kp"].reshape(SEQ, 4, 128) for q in r])[None].astype(np.float32)
    vp = np.stack([q["vp"].reshape(SEQ, 4, 128) for q in r])[None].astype(np.float32)
    wkvp = np.stack([q["wkvp"] for q in r])[None].astype(np.float32)
    shp = np.stack([q["shp"].T.reshape(-1) for q in r])[None].astype(np.float32)
    ys = np.concatenate([R_[c]["ys_pad"][:NS] for c in range(8)])[:, None, :].astype(np.float32)
    ks = np.concatenate([R_[c]["ks_pad"][:NS] for c in range(8)]).reshape(1, n_s, 1, 4, 128).astype(np.float32)
    vs = np.concatenate([R_[c]["vs_pad"][:NS] for c in range(8)]).reshape(1, n_s, 1, 4, 128).astype(np.float32)
    wkvs = np.concatenate([R_[c]["wkvs"] for c in range(8)])[None].astype(np.float32)
    shs = np.concatenate([R_[c]["shs"][:, :, :NS].transpose(2, 1, 0).reshape(NS, SHIFT) for c in range(8)])[None].astype(np.float32)
    return (yp, ys, kp, vp, wkvp, shp, ks, vs, wkvs, shs)
```
